# Optimizing a Trainium2 kernel written in Bass

```python
import math
import jax
import jax.numpy as jnp
from jax import lax
import numpy as np


D_MODEL = 4096
BATCH = 4
SEQ = 4096
DEPTH = 1

PLE_DIM = 256
D_FF = 11008
RMS_EPS = 1e-6

SSD_HEAD_DIM = 64
SSD_WIDTH = 5 * D_MODEL // 8
SSD_HEADS = SSD_WIDTH // SSD_HEAD_DIM
SSD_GROUPS = 8
SSD_STATE = 128
SSD_CONV = 4
SSD_CHUNK = 128
SSD_CONV_DIM = SSD_WIDTH + 2 * SSD_GROUPS * SSD_STATE

ATT_HEAD_DIM = 128
ATT_WIDTH = 3 * D_MODEL // 8
ATT_HEADS = ATT_WIDTH // ATT_HEAD_DIM
ATT_BRANCHES = ((128, 1), (512, 4), (2048, 16))
ATT_BLOCK = 128

MIX_WIDTH = SSD_WIDTH + ATT_WIDTH
IN_PROJ_WIDTH = SSD_WIDTH + SSD_CONV_DIM + SSD_HEADS + 3 * ATT_WIDTH

kernel_name = "hybrid_ssd_dilated_attn_macaron_ple"


def rms_norm(x, w):
    xf = x.astype(jnp.float32)
    y = xf * lax.rsqrt(jnp.mean(xf * xf, axis=-1, keepdims=True) + RMS_EPS)
    return (y * w.astype(jnp.float32)).astype(x.dtype)


def swiglu(x, w_gate, w_up, w_down):
    return (jax.nn.silu(x @ w_gate) * (x @ w_up)) @ w_down


def alibi_slopes(n):
    def pow2(m):
        start = 2.0 ** (-8.0 / m)
        return [start ** (i + 1) for i in range(m)]
    if math.log2(n).is_integer():
        s = pow2(n)
    else:
        c = 2 ** int(math.floor(math.log2(n)))
        s = pow2(c) + pow2(2 * c)[0::2][: n - c]
    return np.asarray(s, dtype=np.float32)


def causal_depthwise_conv(x, w, b):
    k_w, ch = w.shape
    y = lax.conv_general_dilated(
        x, w[:, None, :].astype(x.dtype), window_strides=(1,), padding=((k_w - 1, 0),),
        dimension_numbers=('NWC', 'WIO', 'NWC'), feature_group_count=ch)
    return y + b.astype(x.dtype)


def ssd_mixer(z, xbc, dt_raw, conv_w, conv_b, dt_bias, a_log, d_skip, norm_w):
    bsz, seq, _ = xbc.shape
    G, HG, P, N, Q = SSD_GROUPS, SSD_HEADS // SSD_GROUPS, SSD_HEAD_DIM, SSD_STATE, SSD_CHUNK
    nc = seq // Q
    xbc = jax.nn.silu(causal_depthwise_conv(xbc, conv_w, conv_b)).astype(jnp.float32)
    xs, bm, cm = jnp.split(xbc, [SSD_WIDTH, SSD_WIDTH + G * N], axis=-1)
    xs = xs.reshape(bsz, nc, Q, G, HG, P)
    bm = bm.reshape(bsz, nc, Q, G, N)
    cm = cm.reshape(bsz, nc, Q, G, N)
    dt = jax.nn.softplus(dt_raw.astype(jnp.float32) + dt_bias.astype(jnp.float32)).reshape(bsz, nc, Q, G, HG)
    a = -jnp.exp(a_log.astype(jnp.float32)).reshape(G, HG)
    a_cs = jnp.cumsum(jnp.moveaxis(dt * a, 2, -1), axis=-1)
    tril = jnp.tril(jnp.ones((Q, Q), dtype=bool))
    seg = a_cs[..., :, None] - a_cs[..., None, :]
    l_mat = jnp.exp(jnp.where(tril, seg, -jnp.inf))
    xdt = xs * dt[..., None]
    cb = jnp.einsum('bclgn,bcsgn->bcgls', cm, bm)
    y_diag = jnp.einsum('bcgls,bcgjls,bcsgjp->bclgjp', cb, l_mat, xdt)
    decay_states = jnp.exp(a_cs[..., -1:] - a_cs)
    states = jnp.einsum('bcsgn,bcgjs,bcsgjp->bcgjpn', bm, decay_states, xdt)
    chunk_decay = jnp.exp(a_cs[..., -1])

    def step(h, inp):
        st, dec = inp
        return h * dec[..., None, None] + st, h

    h0 = jnp.zeros((bsz, G, HG, P, N), jnp.float32)
    _, prev = lax.scan(step, h0, (jnp.moveaxis(states, 1, 0), jnp.moveaxis(chunk_decay, 1, 0)))
    prev = jnp.moveaxis(prev, 0, 1)
    y_off = jnp.einsum('bclgn,bcgjpn,bcgjl->bclgjp', cm, prev, jnp.exp(a_cs))
    y = y_diag + y_off + xs * d_skip.astype(jnp.float32).reshape(G, HG, 1)
    y = y.reshape(bsz, seq, G, HG * P) * jax.nn.silu(z.astype(jnp.float32).reshape(bsz, seq, G, HG * P))
    y = y * lax.rsqrt(jnp.mean(y * y, axis=-1, keepdims=True) + RMS_EPS)
    y = y * norm_w.astype(jnp.float32).reshape(G, HG * P)
    return y.reshape(bsz, seq, SSD_WIDTH).astype(z.dtype)


def dilated_branch(q, k, v, slopes, window, dilation):
    bsz, seq, H, E = q.shape
    L = seq // dilation
    C = ATT_BLOCK
    w_sub = window // dilation
    nb = -(-L // C)
    Lp = nb * C

    def to_sub(t):
        return t.reshape(bsz, L, dilation, H, E).transpose(0, 2, 3, 1, 4)

    qs = jnp.pad(to_sub(q), ((0, 0), (0, 0), (0, 0), (0, Lp - L), (0, 0))).reshape(bsz, dilation, H, nb, C, E)

    def key_blocks(t):
        t = jnp.pad(to_sub(t), ((0, 0), (0, 0), (0, 0), (C, Lp - L), (0, 0))).reshape(bsz, dilation, H, nb + 1, C, E)
        return jnp.concatenate([t[:, :, :, :-1], t[:, :, :, 1:]], axis=4)

    kb = key_blocks(k)
    vb = key_blocks(v)
    s = jnp.einsum('bdhnqe,bdhnke->bdhnqk', qs, kb)
    qi = jnp.arange(C)[:, None]
    kj = jnp.arange(2 * C)[None, :]
    dist = C + qi - kj
    key_pos = (jnp.arange(nb)[:, None, None] - 1) * C + kj[None]
    valid = (dist >= 0) & (dist <= w_sub) & (key_pos >= 0)
    bias = -slopes[:, None, None] * (dist * dilation).astype(jnp.float32)
    s = jnp.where(valid[None, None, None], s + bias[None, None, :, None], -jnp.inf)
    m = jnp.max(s, axis=-1, keepdims=True)
    pexp = jnp.exp(s - m)
    l = jnp.sum(pexp, axis=-1, keepdims=True)
    o = jnp.einsum('bdhnqk,bdhnke->bdhnqe', pexp, vb) / l
    lse = (m + jnp.log(l))[..., 0]
    o = o.reshape(bsz, dilation, H, Lp, E)[:, :, :, :L].transpose(0, 3, 1, 2, 4).reshape(bsz, seq, H, E)
    lse = lse.reshape(bsz, dilation, H, Lp)[:, :, :, :L].transpose(0, 3, 1, 2).reshape(bsz, seq, H)
    return o, lse


def dilated_attention(q, k, v):
    slopes = jnp.asarray(alibi_slopes(ATT_HEADS))
    outs = []
    lses = []
    for window, dilation in ATT_BRANCHES:
        o, lse = dilated_branch(q, k, v, slopes, window, dilation)
        outs.append(o)
        lses.append(lse)
    wts = jax.nn.softmax(jnp.stack(lses), axis=0)
    return jnp.einsum('gbsh,gbshe->bshe', wts, jnp.stack(outs))


def hybrid_mixer(u, w_in, conv_w, conv_b, dt_bias, a_log, d_skip, ssd_norm_w, w_out):
    bsz, seq, _ = u.shape
    proj = u @ w_in
    i1 = SSD_WIDTH
    i2 = i1 + SSD_CONV_DIM
    i3 = i2 + SSD_HEADS
    i4 = i3 + ATT_WIDTH
    i5 = i4 + ATT_WIDTH
    z, xbc, dt_raw, q, k, v = jnp.split(proj, [i1, i2, i3, i4, i5], axis=-1)
    ssd = ssd_mixer(z, xbc, dt_raw, conv_w, conv_b, dt_bias, a_log, d_skip, ssd_norm_w)
    q = q.astype(jnp.float32).reshape(bsz, seq, ATT_HEADS, ATT_HEAD_DIM) * (ATT_HEAD_DIM ** -0.5)
    k = k.astype(jnp.float32).reshape(bsz, seq, ATT_HEADS, ATT_HEAD_DIM)
    v = v.astype(jnp.float32).reshape(bsz, seq, ATT_HEADS, ATT_HEAD_DIM)
    att = dilated_attention(q, k, v).reshape(bsz, seq, ATT_WIDTH).astype(u.dtype)
    return jnp.concatenate([ssd, att], axis=-1) @ w_out


def setup_inputs(seed: int = 0) -> dict:
    key = jax.random.key(seed)
    ks = jax.random.split(key, 28)

    def nrm(k, shape, scale):
        return jax.random.normal(k, shape, jnp.float32) * scale

    def gain(k, n):
        return 1.0 + 0.02 * jax.random.normal(k, (DEPTH, n), jnp.float32)

    dt = jnp.exp(jax.random.uniform(ks[13], (DEPTH, SSD_HEADS), jnp.float32) * (math.log(0.1) - math.log(0.001)) + math.log(0.001))
    dt = jnp.maximum(dt, 1e-4)
    dt_bias = dt + jnp.log(-jnp.expm1(-dt))
    a_log = jnp.log(jax.random.uniform(ks[14], (DEPTH, SSD_HEADS), jnp.float32, 1.0, 16.0))
    return {
        'x': jax.random.normal(ks[0], (BATCH, SEQ, D_MODEL), jnp.float32),
        'p': jax.random.normal(ks[1], (DEPTH, BATCH, SEQ, PLE_DIM), jnp.float32),
        'ffn1_pre_w': gain(ks[2], D_MODEL),
        'ffn1_post_w': gain(ks[3], D_MODEL),
        'ffn1_w_gate': nrm(ks[4], (DEPTH, D_MODEL, D_FF), D_MODEL ** -0.5),
        'ffn1_w_up': nrm(ks[5], (DEPTH, D_MODEL, D_FF), D_MODEL ** -0.5),
        'ffn1_w_down': nrm(ks[6], (DEPTH, D_FF, D_MODEL), D_FF ** -0.5),
        'mix_pre_w': gain(ks[7], D_MODEL),
        'mix_post_w': gain(ks[8], D_MODEL),
        'w_in': nrm(ks[9], (DEPTH, D_MODEL, IN_PROJ_WIDTH), D_MODEL ** -0.5),
        'conv_w': nrm(ks[10], (DEPTH, SSD_CONV, SSD_CONV_DIM), SSD_CONV ** -0.5),
        'conv_b': nrm(ks[11], (DEPTH, SSD_CONV_DIM), 0.01),
        'dt_bias': dt_bias,
        'a_log': a_log,
        'd_skip': 1.0 + 0.1 * jax.random.normal(ks[15], (DEPTH, SSD_HEADS), jnp.float32),
        'ssd_norm_w': gain(ks[16], SSD_WIDTH),
        'w_out': nrm(ks[17], (DEPTH, MIX_WIDTH, D_MODEL), MIX_WIDTH ** -0.5),
        'ffn2_pre_w': gain(ks[18], D_MODEL),
        'ffn2_post_w': gain(ks[19], D_MODEL),
        'ffn2_w_gate': nrm(ks[20], (DEPTH, D_MODEL, D_FF), D_MODEL ** -0.5),
        'ffn2_w_up': nrm(ks[21], (DEPTH, D_MODEL, D_FF), D_MODEL ** -0.5),
        'ffn2_w_down': nrm(ks[22], (DEPTH, D_FF, D_MODEL), D_FF ** -0.5),
        'ple_pre_w': gain(ks[23], D_MODEL),
        'ple_post_w': gain(ks[24], D_MODEL),
        'w_ple_gate': nrm(ks[25], (DEPTH, D_MODEL, D_MODEL), D_MODEL ** -0.5),
        'w_ple_proj': nrm(ks[26], (DEPTH, PLE_DIM, D_MODEL), PLE_DIM ** -0.5),
    }


def reference(x, p, ffn1_pre_w, ffn1_post_w, ffn1_w_gate, ffn1_w_up, ffn1_w_down,
              mix_pre_w, mix_post_w, w_in, conv_w, conv_b, dt_bias, a_log, d_skip, ssd_norm_w, w_out,
              ffn2_pre_w, ffn2_post_w, ffn2_w_gate, ffn2_w_up, ffn2_w_down,
              ple_pre_w, ple_post_w, w_ple_gate, w_ple_proj):
    h = x
    for i in range(DEPTH):
        f = swiglu(rms_norm(h, ffn1_pre_w[i]), ffn1_w_gate[i], ffn1_w_up[i], ffn1_w_down[i])
        h = h + 0.5 * rms_norm(f, ffn1_post_w[i])
        mix = hybrid_mixer(rms_norm(h, mix_pre_w[i]), w_in[i], conv_w[i], conv_b[i], dt_bias[i],
                           a_log[i], d_skip[i], ssd_norm_w[i], w_out[i])
        h = h + rms_norm(mix, mix_post_w[i])
        f = swiglu(rms_norm(h, ffn2_pre_w[i]), ffn2_w_gate[i], ffn2_w_up[i], ffn2_w_down[i])
        h = h + 0.5 * rms_norm(f, ffn2_post_w[i])
        gate = jax.nn.sigmoid(rms_norm(h, ple_pre_w[i]) @ w_ple_gate[i])
        h = h + rms_norm(gate * (p[i].astype(h.dtype) @ w_ple_proj[i]), ple_post_w[i])
    return h
```

```python
import math
from contextlib import ExitStack
import numpy as np
import concourse.bass as bass
import concourse.mybir as mybir
from concourse.bass_utils import run_bass_kernel_spmd

F32 = mybir.dt.float32
BF16 = mybir.dt.bfloat16
AF = mybir.ActivationFunctionType
ALU = mybir.AluOpType

D = 4096
KC = 32
EPS = 1e-6
SSD_W = 2560
NH = 40
NG = 8
ATT_W = 1536
AH = 12
OFF_Z, OFF_X, OFF_DT, OFF_Q, OFF_K, OFF_V = 0, 2560, 7168, 7208, 8744, 10280
EPOCH = 20000
BIG = 1.0e6
import os
DEBUG = int(os.environ.get("KDEBUG", "0"))
QS = 128 ** -0.5


def alibi_slopes(n):
    def pow2(m):
        start = 2.0 ** (-8.0 / m)
        return [start ** (i + 1) for i in range(m)]
    if math.log2(n).is_integer():
        s = pow2(n)
    else:
        c = 2 ** int(math.floor(math.log2(n)))
        s = pow2(c) + pow2(2 * c)[0::2][: n - c]
    return [float(np.float32(v)) for v in s]


class Ev:
    __slots__ = ("sem", "val", "key", "order")

    def __init__(self, sem, val, key, order):
        self.sem, self.val, self.key, self.order = sem, val, key, order


class Buf:
    def __init__(self, name, t=None):
        self.name = name
        self.t = t
        self.writes = {}
        self.reads = {}


class Sched:
    def __init__(self, nc, es):
        self.nc, self.es = nc, es
        self.eng = {"pe": nc.tensor, "act": nc.scalar, "dve": nc.vector, "pool": nc.gpsimd, "sp": nc.sync}
        self.cnt = {k: 0 for k in self.eng}
        self.psems = {k: [] for k in self.eng}
        self.seen = {k: {} for k in self.eng}
        self.dsem = {}
        self.dcnt = {}
        self.nsem = 0

    def _sem(self, name):
        self.nsem += 1
        return self.es.enter_context(self.nc.semaphore(name))

    def _wait(self, e, ev):
        if self.seen[e].get(ev.key, -1) >= ev.order:
            return
        self.eng[e].wait_ge(ev.sem, ev.val)
        self.seen[e][ev.key] = ev.order

    def _deps(self, e, reads, writes):
        for b in reads:
            for ev in b.writes.values():
                if not (e == "pe" and ev.key == "pe"):
                    self._wait(e, ev)
        for b in writes:
            for ev in list(b.writes.values()) + list(b.reads.values()):
                if not (e == "pe" and ev.key == "pe"):
                    self._wait(e, ev)

    def _record(self, ev, reads, writes):
        for b in reads:
            b.reads[ev.key] = ev
        for b in writes:
            b.writes = {ev.key: ev}
            b.reads = {}

    def op(self, e, fn, r=(), w=()):
        self._deps(e, r, w)
        ins = fn(self.eng[e])
        n = self.cnt[e]
        ep = n // EPOCH
        while len(self.psems[e]) <= ep:
            self.psems[e].append(self._sem(f"p_{e}_{len(self.psems[e])}"))
        self.cnt[e] = n + 1
        ins.then_inc(self.psems[e][ep], 1)
        ev = Ev(self.psems[e][ep], n % EPOCH + 1, e, n)
        self._record(ev, r, w)
        return ev

    def dma(self, q, out_ap, in_ap, src, dst, **kw):
        self._deps(q, [src], [dst])
        key = "d:" + src.name + ">" + dst.name
        if key not in self.dsem:
            self.dsem[key] = self._sem("d%d" % len(self.dsem))
            self.dcnt[key] = 0
        self.dcnt[key] += 16
        self.eng[q].dma_start(out=out_ap, in_=in_ap, **kw).then_inc(self.dsem[key], 16)
        ev = Ev(self.dsem[key], self.dcnt[key], key, self.dcnt[key])
        self._record(ev, [src], [dst])
        return ev

    def cc(self, in_ap, out_ap, src, dst, groups):
        self._deps("pool", [src], [dst])
        if "cc" not in self.dsem:
            self.dsem["cc"] = self._sem("ccs")
            self.dcnt["cc"] = 0
        self.dcnt["cc"] += 1
        self.eng["pool"].collective_compute("AllGather", ALU.bypass, replica_groups=groups,
                                            ins=[in_ap], outs=[out_ap]).then_inc(self.dsem["cc"], 1)
        ev = Ev(self.dsem["cc"], self.dcnt["cc"], "cc", self.dcnt["cc"])
        self._record(ev, [src], [dst])
        return ev

    def finish(self, e, bufs):
        for b in bufs:
            for ev in list(b.writes.values()):
                self._wait(e, ev)


def host_consts():
    c = {}
    c["ident"] = np.eye(128, dtype=np.float32)
    kk = np.arange(128)[:, None]
    ll = np.arange(128)[None, :]
    c["triu"] = (kk <= ll).astype(np.float32)
    c["negm"] = np.where(ll < kk, -30000.0, 0.0).astype(np.float32)
    prev = np.where(kk >= ll, 128.0 + ll - kk, BIG)
    cur = np.where(kk <= ll, (ll - kk).astype(np.float64), BIG)
    c["distm"] = np.concatenate([prev, cur], axis=1).astype(np.float32)
    return c


def build(S, DFF):
    NT = S // 512
    NCH = S // 128
    SL = S // 2
    NTL = SL // 512
    NCL = SL // 128
    FC = DFF // 128
    nc = bass.Bass("TRN2", target_bir_lowering=False)

    def din(name, shape):
        return nc.dram_tensor(name, list(shape), F32, kind="ExternalInput").ap()

    x_in = din("x", [SL, D])
    p_in = din("p", [SL, 256])
    sel_in = din("sel", [128, 2])
    Wg = [din("ffn1_w_gate", [D, DFF]), din("ffn2_w_gate", [D, DFF])]
    Wu = [din("ffn1_w_up", [D, DFF]), din("ffn2_w_up", [D, DFF])]
    Wd = [din("ffn1_w_down", [DFF, D]), din("ffn2_w_down", [DFF, D])]
    W_in = din("w_in", [D, 11816])
    W_out = din("w_out", [D, D])
    W_pg = din("w_ple_gate", [D, D])
    W_pp = din("w_ple_proj", [256, D])
    gcols_in = din("gcols", [128, 4 * KC])
    postw_in = din("postw", [4, 128, D])
    convw_in = din("convw", [128, 36 * 4])
    convb_in = din("convb", [128, 36])
    hv_in = din("headv", [128, 3 * NH])
    dskx_in = din("dskx", [128, SSD_W])
    snw_in = din("snw", [128, SSD_W])
    ident_in = din("ident", [128, 128])
    triu_in = din("triu", [128, 128])
    negm_in = din("negm", [128, 128])
    distm_in = din("distm", [128, 256])
    out = nc.dram_tensor("out", [SL, D], F32, kind="ExternalOutput").ap()
    def dscr(name, shape, dt):
        return nc.dram_tensor(name, list(shape), dt).ap()

    Hloc_t = [nc.dram_tensor("Hloc%d" % k, [128, D], F32) for k in range(NCL)]
    Hg_t = [nc.dram_tensor("Hg%d" % k, [256, D], F32) for k in range(NCL)]
    Hloc = [t.ap() for t in Hloc_t]
    Hg = [t.ap() for t in Hg_t]
    Fd = dscr("Fd", [512, D], F32)
    ZS = dscr("ZS", [S, SSD_W], BF16)
    XS = dscr("XS", [S, SSD_W], BF16)
    BM = dscr("BM", [S, 1024], BF16)
    BT = dscr("BT", [1024, S], BF16)
    CT = dscr("CT", [1024, S], BF16)
    QT = dscr("QT", [ATT_W, S], BF16)
    KT = dscr("KT", [ATT_W, S], BF16)
    Vd = dscr("Vd", [S, ATT_W], BF16)
    MIXT = dscr("MIXT", [D, S], BF16)

    es = ExitStack()
    with es:
        sch = Sched(nc, es)
        op, dma = sch.op, sch.dma

        def sb(name, shape, dt):
            return Buf(name, es.enter_context(nc.sbuf_tensor("s_" + name, list(shape), dt)))

        def ps(name, shape, dt):
            return Buf(name, es.enter_context(nc.psum_tensor(name, list(shape), dt)))

        bx, bp, bW = Buf("x"), Buf("p"), Buf("W")
        bcst = Buf("cst")
        bH, bHg, bFd, bZS, bXS, bBM, bBT, bCT = Buf("H"), Buf("Hg"), Buf("Fd"), Buf("ZS"), Buf("XS"), Buf("BM"), Buf("BT"), Buf("CT")
        bQT, bKT, bV, bMIXT, bout = Buf("QT"), Buf("KT"), Buf("V"), Buf("MIXT"), Buf("out")

        xnT = sb("xnT", [128, KC, 512], BF16)
        big = sb("big", [128, max(FC * 512, 43008)], BF16)
        slabs = [sb("slab%d" % i, [128, KC, 256], BF16) for i in range(2)]
        ra = sb("ra", [128, D], F32)
        xs = sb("xs", [128, D], BF16)
        fst = [sb("fst%d" % i, [128, 4, 256], F32) for i in range(2)]
        sil = [sb("sil%d" % i, [128, 512], F32) for i in range(2)]
        stg = [sb("stg%d" % i, [128, 512], BF16) for i in range(2)]
        gcols = sb("gcols", [128, 4 * KC], F32)
        convw = sb("convw", [128, 36 * 4], F32)
        convb = sb("convb", [128, 36], F32)
        halo = sb("halo", [128, 36, 4], F32)
        headv = sb("headv", [128, 3 * NH], F32)
        ident = sb("ident", [128, 128], BF16)
        identf = sb("identf", [128, 128], F32)
        triu = sb("triu", [128, 128], F32)
        onesf = sb("onesf", [128, 128], F32)
        onesb = sb("onesb", [128, 128], BF16)
        distm = sb("distm", [128, 256], F32)
        dt_sb = sb("dt_sb", [128, NCH, NH], F32)
        ss = sb("ss", [128, 1], F32)
        rstd = sb("rstd", [128, 1], F32)
        cst = sb("cstage", [128, 516], F32)
        acc = sb("cacc", [128, 512], F32)
        sm = sb("small", [128, 8 * NH], F32)
        pf = [ps("pf%d" % i, [128, 512], F32) for i in range(6)]
        pb = [ps("pb%d" % i, [128, 1024], BF16) for i in range(2)]
        st = {"pf": 0, "pb": 0, "slab": 0, "fst": 0, "sil": 0, "stg": 0}

        def nxt(kind, lst):
            i = st[kind]
            st[kind] = i + 1
            return lst[i % len(lst)]

        def ld(buf, ap_in, tmp=None):
            dma("sp", buf.t[:], ap_in, bcst, buf)

        ld(gcols, gcols_in[:, :])
        ld(convw, convw_in[:, :])
        ld(convb, convb_in[:, :])
        ld(headv, hv_in[:, :])
        ld(identf, ident_in[:, :])
        ld(triu, triu_in[:, :])
        ld(distm, distm_in[:, :])
        sel = sb("sel", [128, 2], F32)
        ld(sel, sel_in[:, :])
        op("dve", lambda e: e.tensor_copy(out=ident.t[:], in_=identf.t[:]), r=[identf], w=[ident])
        op("pool", lambda e: e.memset(onesf.t[:], 1.0), w=[onesf])
        op("pool", lambda e: e.memset(onesb.t[:], 1.0), w=[onesb])
        op("pool", lambda e: e.memset(halo.t[:], 0.0), w=[halo])

        def load_slab(parts):
            s = nxt("slab", slabs)
            for (W, k0, nk, c0, ncols, d0) in parts:
                wv = W.rearrange("(kc p) n -> p kc n", p=128)
                dma("pool", s.t[:, 0:nk, d0:d0 + ncols], wv[:, k0:k0 + nk, c0:c0 + ncols], bW, s)
            return s

        def rmsnorm_stats(src):
            n = D
            op("pool", lambda e: e.memset(ss.t[:], 0.0), w=[ss])
            op("act", lambda e: e.activation(out=xs.t[:], in_=src.t[:], func=AF.Square, accum_out=ss.t[:, 0:1]),
               r=[src], w=[xs, ss])
            op("dve", lambda e: e.tensor_scalar(out=rstd.t[:], in0=ss.t[:], scalar1=1.0 / n, scalar2=EPS,
                                               op0=ALU.mult, op1=ALU.add), r=[ss], w=[rstd])
            op("act", lambda e: e.activation(out=rstd.t[:], in_=rstd.t[:], func=AF.Sqrt), r=[rstd], w=[rstd])
            op("dve", lambda e: e.reciprocal(out=rstd.t[:], in_=rstd.t[:]), r=[rstd], w=[rstd])

        def rows_norm_T(getsrc, bsrc, gi):
            for c in range(4):
                dma("sp", ra.t[:], getsrc(c), bsrc, ra)
                rmsnorm_stats(ra)
                op("act", lambda e: e.activation(out=xs.t[:], in_=ra.t[:], func=AF.Copy, scale=rstd.t[:, 0:1]),
                   r=[ra, rstd], w=[xs])
                for kg in range(4):
                    bank = nxt("pb", pb)
                    for kk in range(8):
                        k = kg * 8 + kk
                        op("pe", lambda e: e.transpose(out=bank.t[:, kk * 128:(kk + 1) * 128],
                                                       in_=xs.t[:, k * 128:(k + 1) * 128], identity=ident.t[:]),
                           r=[xs, ident], w=[bank])
                    for kk in range(8):
                        k = kg * 8 + kk
                        g = gcols.t[:, gi * KC + k:gi * KC + k + 1]
                        o = xnT.t[:, k, c * 128:(c + 1) * 128]
                        i_ = bank.t[:, kk * 128:(kk + 1) * 128]
                        if kk % 2:
                            op("act", lambda e: e.activation(out=o, in_=i_, func=AF.Copy, scale=g),
                               r=[bank, gcols], w=[xnT])
                        else:
                            op("dve", lambda e: e.tensor_scalar(out=o, in0=i_, scalar1=g, scalar2=None, op0=ALU.mult),
                               r=[bank, gcols], w=[xnT])

        def lin_fm(W, c0, nchunks, evac):
            m = 0
            while m < nchunks:
                nm = min(2, nchunks - m)
                s = load_slab([(W, 0, KC, c0 + m * 128, nm * 128, 0)])
                for mm in range(nm):
                    bank = nxt("pf", pf)
                    for k in range(KC):
                        op("pe", lambda e: e.matmul(bank.t[:], lhsT=s.t[:, k, mm * 128:(mm + 1) * 128],
                                                    rhs=xnT.t[:, k, :], start=(k == 0), stop=(k == KC - 1)),
                           r=[s, xnT], w=[bank])
                    evac(m + mm, bank)
                m += nm

        def lin_tm(W, nk_tot, c0, ncols_tot, lhs, evac, blk=256):
            j = 0
            while j * blk < ncols_tot:
                cc = c0 + j * blk
                ncols = min(blk, ncols_tot - j * blk)
                banks = [nxt("pf", pf) for _ in range(4)]
                for k0 in range(0, nk_tot, KC):
                    nk = min(KC, nk_tot - k0)
                    s = load_slab([(W, k0, nk, cc, ncols, 0)])
                    for tc in range(4):
                        for kk in range(nk):
                            k = k0 + kk
                            op("pe", lambda e: e.matmul(banks[tc].t[:, 0:ncols],
                                                        lhsT=lhs.t[:, k, tc * 128:(tc + 1) * 128],
                                                        rhs=s.t[:, kk, 0:ncols],
                                                        start=(k == 0), stop=(k == nk_tot - 1)),
                               r=[s, lhs], w=[banks[tc]])
                evac(j, cc - c0, ncols, banks)
                j += 1

        def evac_to_Fd(func=None):
            def ev(j, coff, ncols, banks):
                f = nxt("fst", fst)
                for tc in range(4):
                    if tc % 2:
                        op("act", lambda e: e.activation(out=f.t[:, tc, 0:ncols], in_=banks[tc].t[:, 0:ncols],
                                                         func=AF.Copy), r=[banks[tc]], w=[f])
                    else:
                        op("dve", lambda e: e.tensor_copy(out=f.t[:, tc, 0:ncols], in_=banks[tc].t[:, 0:ncols]),
                           r=[banks[tc]], w=[f])
                dma("sp", Fd.rearrange("(t p) c -> p t c", p=128)[:, :, coff:coff + ncols], f.t[:, :, 0:ncols], f, bFd)
            return ev

        def post_stage(getsrc, bsrc, getdst, bdst, wi, coef):
            wbv = big.t[:, 2048:2048 + 2 * D].bitcast(F32)
            rbv = big.t[:, 2048 + 2 * D:2048 + 4 * D].bitcast(F32)
            dma("sp", wbv, postw_in[wi, :, :], bcst, big)
            for c in range(4):
                dma("sp", ra.t[:], Fd[c * 128:(c + 1) * 128, :], bFd, ra)
                dma("sp", rbv, getsrc(c), bsrc, big)
                rmsnorm_stats(ra)
                op("dve", lambda e: e.scalar_tensor_tensor(out=ra.t[:], in0=ra.t[:], scalar=rstd.t[:, 0:1], in1=wbv,
                                                           op0=ALU.mult, op1=ALU.mult), r=[ra, rstd, big], w=[ra])
                op("dve", lambda e: e.scalar_tensor_tensor(out=rbv, in0=ra.t[:], scalar=float(coef), in1=rbv,
                                                           op0=ALU.mult, op1=ALU.add), r=[ra, big], w=[big])
                dma("sp", getdst(c), rbv, big, bdst)

        def ffn_stage(li, getsrc, bsrc, getdst, bdst, gi, wi):
            rows_norm_T(getsrc, bsrc, gi)
            gT = big
            for fc in range(FC):
                s = load_slab([(Wg[li], 0, KC, fc * 128, 128, 0), (Wu[li], 0, KC, fc * 128, 128, 128)])
                pg = nxt("pf", pf)
                pu = nxt("pf", pf)
                for k in range(KC):
                    op("pe", lambda e: e.matmul(pg.t[:], lhsT=s.t[:, k, 0:128], rhs=xnT.t[:, k, :],
                                                start=(k == 0), stop=(k == KC - 1)), r=[s, xnT], w=[pg])
                for k in range(KC):
                    op("pe", lambda e: e.matmul(pu.t[:], lhsT=s.t[:, k, 128:256], rhs=xnT.t[:, k, :],
                                                start=(k == 0), stop=(k == KC - 1)), r=[s, xnT], w=[pu])
                sl = nxt("sil", sil)
                op("act", lambda e: e.activation(out=sl.t[:], in_=pg.t[:], func=AF.Silu), r=[pg], w=[sl])
                op("dve", lambda e: e.tensor_tensor(out=gT.t[:, fc * 512:(fc + 1) * 512], in0=sl.t[:], in1=pu.t[:],
                                                    op=ALU.mult), r=[sl, pu], w=[gT])
            gT3 = gT.t[:, 0:FC * 512].rearrange("p (k n) -> p k n", n=512)
            lin_tm_view(Wd[li], FC, 0, D, gT, gT3, evac_to_Fd())
            post_stage(getsrc, bsrc, getdst, bdst, wi, 0.5)

        def lin_tm_view(W, nk_tot, c0, ncols_tot, track, view, evac, blk=256):
            j = 0
            while j * blk < ncols_tot:
                cc = c0 + j * blk
                ncols = min(blk, ncols_tot - j * blk)
                banks = [nxt("pf", pf) for _ in range(4)]
                for k0 in range(0, nk_tot, KC):
                    nk = min(KC, nk_tot - k0)
                    s = load_slab([(W, k0, nk, cc, ncols, 0)])
                    for tc in range(4):
                        for kk in range(nk):
                            k = k0 + kk
                            op("pe", lambda e: e.matmul(banks[tc].t[:, 0:ncols],
                                                        lhsT=view[:, k, tc * 128:(tc + 1) * 128],
                                                        rhs=s.t[:, kk, 0:ncols],
                                                        start=(k == 0), stop=(k == nk_tot - 1)),
                               r=[s, track], w=[banks[tc]])
                evac(j, cc - c0, ncols, banks)
                j += 1

        def inproj_stage(ti):
            r0 = ti * 512
            rows_norm_T(lambda c: Hg[(ti * 4 + c) % NCL][((ti * 4 + c) // NCL) * 128:((ti * 4 + c) // NCL + 1) * 128, :], bHg, 1)

            def ev_z(j, coff, ncols, banks):
                for tc in range(4):
                    sg = nxt("stg", stg)
                    op("act", lambda e: e.activation(out=sg.t[:, 0:ncols], in_=banks[tc].t[:, 0:ncols], func=AF.Silu),
                       r=[banks[tc]], w=[sg])
                    dma("sp", ZS[r0 + tc * 128:r0 + (tc + 1) * 128, coff:coff + ncols], sg.t[:, 0:ncols], sg, bZS)
            lin_tm(W_in, KC, OFF_Z, SSD_W, xnT, ev_z)

            def ev_v(j, coff, ncols, banks):
                for tc in range(4):
                    sg = nxt("stg", stg)
                    op("dve", lambda e: e.tensor_copy(out=sg.t[:, 0:ncols], in_=banks[tc].t[:, 0:ncols]),
                       r=[banks[tc]], w=[sg])
                    dma("sp", Vd[r0 + tc * 128:r0 + (tc + 1) * 128, coff:coff + ncols], sg.t[:, 0:ncols], sg, bV)
            lin_tm(W_in, KC, OFF_V, ATT_W, xnT, ev_v)

            def ev_dt(j, coff, ncols, banks):
                for tc in range(4):
                    ch = ti * 4 + tc
                    o = dt_sb.t[:, ch, :]
                    op("dve", lambda e: e.tensor_tensor(out=o, in0=banks[tc].t[:, 0:NH], in1=headv.t[:, 0:NH], op=ALU.add),
                       r=[banks[tc], headv], w=[dt_sb])
                    op("act", lambda e: e.activation(out=o, in_=o, func=AF.Exp), r=[dt_sb], w=[dt_sb])
                    op("act", lambda e: e.activation(out=o, in_=o, func=AF.Ln, bias=1.0, scale=1.0), r=[dt_sb], w=[dt_sb])
            lin_tm(W_in, KC, OFF_DT, NH, xnT, ev_dt)

            def ev_xbc(m, bank):
                op("dve", lambda e: e.tensor_copy(out=cst.t[:, 0:3], in_=halo.t[:, m, 0:3]), r=[halo], w=[cst])
                op("act", lambda e: e.activation(out=cst.t[:, 3:515], in_=bank.t[:], func=AF.Copy), r=[bank], w=[cst])
                op("dve", lambda e: e.tensor_scalar(out=acc.t[:], in0=cst.t[:, 0:512], scalar1=convw.t[:, m * 4:m * 4 + 1],
                                                   scalar2=convb.t[:, m:m + 1], op0=ALU.mult, op1=ALU.add),
                   r=[cst, convw, convb], w=[acc])
                for kq in range(1, 4):
                    op("dve", lambda e: e.scalar_tensor_tensor(out=acc.t[:], in0=cst.t[:, kq:kq + 512],
                                                               scalar=convw.t[:, m * 4 + kq:m * 4 + kq + 1], in1=acc.t[:],
                                                               op0=ALU.mult, op1=ALU.add), r=[cst, convw, acc], w=[acc])
                op("dve", lambda e: e.tensor_copy(out=halo.t[:, m, 0:3], in_=cst.t[:, 512:515]), r=[cst], w=[halo])
                sg = nxt("stg", stg)
                op("act", lambda e: e.activation(out=sg.t[:], in_=acc.t[:], func=AF.Silu), r=[acc], w=[sg])
                if m >= 28:
                    dma("sp", CT[(m - 28) * 128:(m - 27) * 128, r0:r0 + 512], sg.t[:], sg, bCT)
                    return
                if m >= 20:
                    dma("sp", BT[(m - 20) * 128:(m - 19) * 128, r0:r0 + 512], sg.t[:], sg, bBT)
                bank2 = nxt("pb", pb)
                for tc in range(4):
                    op("pe", lambda e: e.transpose(out=bank2.t[:, tc * 128:(tc + 1) * 128],
                                                   in_=sg.t[:, tc * 128:(tc + 1) * 128], identity=ident.t[:]),
                       r=[sg, ident], w=[bank2])
                sg2 = nxt("stg", stg)
                op("dve", lambda e: e.tensor_copy(out=sg2.t[:], in_=bank2.t[:, 0:512]), r=[bank2], w=[sg2])
                if m >= 20:
                    dst, bd, cc = BM, bBM, (m - 20) * 128
                else:
                    dst, bd, cc = XS, bXS, m * 128
                dma("sp", dst.rearrange("(t p) c -> p t c", p=128)[:, ti * 4:(ti + 1) * 4, cc:cc + 128],
                    sg2.t[:].rearrange("p (t c) -> p t c", c=128), sg2, bd)
            lin_fm(W_in, OFF_X, 36, ev_xbc)

            def mk_ev(dst, bd):
                def ev(m, bank):
                    sg = nxt("stg", stg)
                    if m % 2:
                        op("act", lambda e: e.activation(out=sg.t[:], in_=bank.t[:], func=AF.Copy), r=[bank], w=[sg])
                    else:
                        op("dve", lambda e: e.tensor_copy(out=sg.t[:], in_=bank.t[:]), r=[bank], w=[sg])
                    dma("sp", dst[m * 128:(m + 1) * 128, r0:r0 + 512], sg.t[:], sg, bd)
                return ev
            lin_fm(W_in, OFF_Q, AH, mk_ev(QT, bQT))
            lin_fm(W_in, OFF_K, AH, mk_ev(KT, bKT))

        def attention():
            slopes = alibi_slopes(AH)
            bg = big.t
            QTh = bg[:, 0:S]
            KTh = bg[:, S:2 * S]
            Vh = bg[:, 2 * S:3 * S]
            attT = bg[:, 3 * S:4 * S]
            numT = bg[:, 4 * S:6 * S].bitcast(F32)
            denT = bg[:, 6 * S:8 * S].bitcast(F32)
            for h in range(AH):
                dma("sp", QTh, QT[h * 128:(h + 1) * 128, :], bQT, big)
                dma("sp", KTh, KT[h * 128:(h + 1) * 128, :], bKT, big)
                for bi, d in enumerate((1, 4, 16)):
                    nb = S // (128 * d)
                    cp = -slopes[h] * d / QS
                    with nc.allow_non_contiguous_dma(reason="dilated V gather"):
                        for r in range(d):
                            dma("sp", Vh.rearrange("p (r b f) -> p r b f", r=d, b=nb)[:, r, :, :],
                                Vd.rearrange("(b i r) f -> i r b f", i=128, r=d)[:, r, :, h * 128:(h + 1) * 128], bV, big)
                    V4 = Vh.rearrange("p (r b f) -> p r b f", r=d, b=nb)
                    for r in range(d):
                        for b in range(nb):
                            def tok(bb):
                                s0 = bb * 128 * d + r
                                return slice(s0, s0 + 127 * d + 1, d)
                            qs = tok(b)
                            sbank = nxt("pf", pf)
                            if b > 0:
                                op("pe", lambda e: e.matmul(sbank.t[:, 0:128], lhsT=KTh[:, tok(b - 1)], rhs=QTh[:, qs],
                                                            start=True, stop=True), r=[big], w=[sbank])
                            op("pe", lambda e: e.matmul(sbank.t[:, 128:256], lhsT=KTh[:, qs], rhs=QTh[:, qs],
                                                        start=True, stop=True), r=[big], w=[sbank])
                            lo = 0 if b > 0 else 128
                            sl = nxt("sil", sil)
                            op("dve", lambda e: e.scalar_tensor_tensor(out=sl.t[:, lo:256], in0=distm.t[:, lo:256],
                                                                       scalar=float(cp), in1=sbank.t[:, lo:256],
                                                                       op0=ALU.mult, op1=ALU.add),
                               r=[distm, sbank], w=[sl])
                            pT = nxt("stg", stg)
                            op("act", lambda e: e.activation(out=pT.t[:, lo:256], in_=sl.t[:, lo:256], func=AF.Exp,
                                                             scale=float(QS)), r=[sl], w=[pT])
                            nbank = nxt("pf", pf)
                            if b > 0:
                                op("pe", lambda e: e.matmul(nbank.t[:, 0:128], lhsT=V4[:, r, b - 1, :], rhs=pT.t[:, 0:128],
                                                            start=True, stop=False), r=[big, pT], w=[nbank])
                            op("pe", lambda e: e.matmul(nbank.t[:, 0:128], lhsT=V4[:, r, b, :], rhs=pT.t[:, 128:256],
                                                        start=(b == 0), stop=True), r=[big, pT], w=[nbank])
                            if b > 0:
                                op("pe", lambda e: e.matmul(nbank.t[:, 128:256], lhsT=onesb.t[:], rhs=pT.t[:, 0:128],
                                                            start=True, stop=False), r=[onesb, pT], w=[nbank])
                            op("pe", lambda e: e.matmul(nbank.t[:, 128:256], lhsT=onesb.t[:], rhs=pT.t[:, 128:256],
                                                        start=(b == 0), stop=True), r=[onesb, pT], w=[nbank])
                            if bi == 0:
                                op("dve", lambda e: e.tensor_copy(out=numT[:, qs], in_=nbank.t[:, 0:128]), r=[nbank], w=[big])
                                op("act", lambda e: e.activation(out=denT[:, qs], in_=nbank.t[:, 128:256], func=AF.Copy),
                                   r=[nbank], w=[big])
                            else:
                                op("dve", lambda e: e.tensor_tensor(out=numT[:, qs], in0=numT[:, qs], in1=nbank.t[:, 0:128],
                                                                    op=ALU.add), r=[nbank, big], w=[big])
                                op("dve", lambda e: e.tensor_tensor(out=denT[:, qs], in0=denT[:, qs], in1=nbank.t[:, 128:256],
                                                                    op=ALU.add), r=[nbank, big], w=[big])
                op("dve", lambda e: e.reciprocal(out=denT, in_=denT), r=[big], w=[big])
                op("dve", lambda e: e.tensor_tensor(out=attT, in0=numT, in1=denT, op=ALU.mult), r=[big], w=[big])
                dma("sp", MIXT[SSD_W + h * 128:SSD_W + (h + 1) * 128, :], attT, big, bMIXT)

        def ssd():
            bg = big.t
            o = 0

            def carve(n, dt=BF16):
                nonlocal o
                ne = n if dt == BF16 else 2 * n
                v = bg[:, o:o + ne]
                o += ne
                return v if dt == BF16 else v.bitcast(F32)
            xs_c = carve(SSD_W)
            zs_c = carve(SSD_W)
            bm_c = carve(1024)
            bt_c = carve(1024)
            ct_c = carve(1024)
            xdt_b = carve(SSD_W)
            xdtd_b = carve(SSD_W)
            Hbf = carve(SSD_W)
            Hst = carve(SSD_W, F32)
            xdt_f = carve(SSD_W, F32)
            y_f = carve(SSD_W, F32)
            t_f = xdt_f
            Rj = carve(128, F32)
            LT = carve(128, F32)
            MT = carve(128)
            negm = carve(128, F32)
            snw = carve(SSD_W, F32)
            dskx = carve(SSD_W, F32)
            dma("sp", negm, negm_in[:, :], bcst, big)
            dma("sp", snw, snw_in[:, :], bcst, big)
            dma("sp", dskx, dskx_in[:, :], bcst, big)
            smt = sm.t
            dtA = smt[:, 0:NH]
            acs = smt[:, NH:2 * NH]
            nacs = smt[:, 2 * NH:3 * NH]
            ea = smt[:, 3 * NH:4 * NH]
            cd = smt[:, 4 * NH:5 * NH]
            dsc = smt[:, 5 * NH:6 * NH]
            a_b = smt[:, 6 * NH:7 * NH]
            gss = smt[:, 7 * NH:7 * NH + 8]
            op("act", lambda e: e.activation(out=a_b, in_=headv.t[:, NH:2 * NH], func=AF.Exp), r=[headv], w=[sm])
            op("dve", lambda e: e.tensor_scalar(out=a_b, in0=a_b, scalar1=-1.0, scalar2=None, op0=ALU.mult), r=[sm], w=[sm])
            op("pool", lambda e: e.memset(Hst, 0.0), w=[big])
            op("pool", lambda e: e.memset(Hbf, 0.0), w=[big])
            for c in range(NCH):
                rs = slice(c * 128, (c + 1) * 128)
                dma("sp", xs_c, XS[rs, :], bXS, big)
                dma("sp", zs_c, ZS[rs, :], bZS, big)
                dma("sp", bm_c, BM[rs, :], bBM, big)
                dma("sp", bt_c.rearrange("p (g s) -> p g s", g=NG), BT.rearrange("(g n) s -> n g s", n=128)[:, :, rs], bBT, big)
                dma("sp", ct_c.rearrange("p (g s) -> p g s", g=NG), CT.rearrange("(g n) s -> n g s", n=128)[:, :, rs], bCT, big)
                dtc = dt_sb.t[:, c, :]
                op("dve", lambda e: e.tensor_tensor(out=dtA, in0=dtc, in1=a_b, op=ALU.mult), r=[dt_sb, sm], w=[sm])
                pa = pf[4]
                op("pe", lambda e: e.matmul(pa.t[:, 0:NH], lhsT=triu.t[:], rhs=dtA, start=True, stop=True), r=[triu, sm], w=[pa])
                op("pe", lambda e: e.matmul(pa.t[:, 64:64 + NH], lhsT=onesf.t[:], rhs=dtA, start=True, stop=True),
                   r=[onesf, sm], w=[pa])
                op("dve", lambda e: e.tensor_copy(out=acs, in_=pa.t[:, 0:NH]), r=[pa], w=[sm])
                op("dve", lambda e: e.tensor_scalar(out=nacs, in0=pa.t[:, 0:NH], scalar1=-1.0, scalar2=None, op0=ALU.mult),
                   r=[pa], w=[sm])
                op("act", lambda e: e.activation(out=ea, in_=pa.t[:, 0:NH], func=AF.Exp), r=[pa], w=[sm])
                op("act", lambda e: e.activation(out=cd, in_=pa.t[:, 64:64 + NH], func=AF.Exp), r=[pa], w=[sm])
                op("dve", lambda e: e.tensor_tensor(out=dsc, in0=pa.t[:, 64:64 + NH], in1=acs, op=ALU.subtract), r=[pa, sm], w=[sm])
                op("act", lambda e: e.activation(out=dsc, in_=dsc, func=AF.Exp), r=[sm], w=[sm])
                for j in range(NH):
                    hs = slice(j * 64, (j + 1) * 64)
                    op("dve", lambda e: e.tensor_scalar(out=xdt_f[:, hs], in0=xs_c[:, hs], scalar1=dtc[:, j:j + 1], scalar2=None,
                                                       op0=ALU.mult), r=[big, dt_sb], w=[big])
                    op("pool", lambda e: e.tensor_copy(out=xdt_b[:, hs], in_=xdt_f[:, hs]), r=[big], w=[big])
                    op("dve", lambda e: e.tensor_scalar(out=xdtd_b[:, hs], in0=xdt_f[:, hs], scalar1=dsc[:, j:j + 1], scalar2=None,
                                                       op0=ALU.mult), r=[big, sm], w=[big])
                for g in range(NG):
                    gs = slice(g * 320, (g + 1) * 320)
                    n_s = slice(g * 128, (g + 1) * 128)
                    pyo = pf[0]
                    op("pe", lambda e: e.matmul(pyo.t[:, 0:320], lhsT=ct_c[:, n_s], rhs=Hbf[:, gs], start=True, stop=True),
                       r=[big], w=[pyo])
                    pst = pf[1]
                    op("pe", lambda e: e.matmul(pst.t[:, 0:320], lhsT=bm_c[:, n_s], rhs=xdtd_b[:, gs], start=True, stop=True),
                       r=[big], w=[pst])
                    pcb = pf[2]
                    op("pe", lambda e: e.matmul(pcb.t[:, 0:128], lhsT=bt_c[:, n_s], rhs=ct_c[:, n_s], start=True, stop=True),
                       r=[big], w=[pcb])
                    pyd = pf[3]
                    for jj in range(5):
                        j = g * 5 + jj
                        hs = slice(j * 64, (j + 1) * 64)
                        op("dve", lambda e: e.tensor_scalar(out=Rj, in0=triu.t[:], scalar1=dtA[:, j:j + 1], scalar2=None,
                                                           op0=ALU.mult), r=[triu, sm], w=[big])
                        pbc = pf[4 + (j % 2)]
                        op("pe", lambda e: e.matmul(pbc.t[:, 0:128], lhsT=onesf.t[:], rhs=Rj, start=True, stop=False),
                           r=[onesf, big], w=[pbc])
                        op("pe", lambda e: e.matmul(pbc.t[:, 0:128], lhsT=identf.t[:], rhs=negm, start=False, stop=True),
                           r=[identf, big], w=[pbc])
                        op("act", lambda e: e.activation(out=LT, in_=pbc.t[:, 0:128], func=AF.Exp, bias=nacs[:, j:j + 1], scale=1.0),
                           r=[pbc, sm], w=[big])
                        op("dve", lambda e: e.tensor_tensor(out=MT, in0=LT, in1=pcb.t[:, 0:128], op=ALU.mult), r=[big, pcb], w=[big])
                        op("pe", lambda e: e.matmul(pyd.t[:, jj * 64:(jj + 1) * 64], lhsT=MT, rhs=xdt_b[:, hs], start=True, stop=True),
                           r=[big], w=[pyd])
                    op("pool", lambda e: e.tensor_tensor(out=t_f[:, gs], in0=xs_c[:, gs], in1=dskx[:, gs], op=ALU.mult), r=[big], w=[big])
                    op("dve", lambda e: e.tensor_tensor(out=y_f[:, gs], in0=t_f[:, gs], in1=pyd.t[:, 0:320], op=ALU.add), r=[big, pyd], w=[big])
                    for jj in range(5):
                        j = g * 5 + jj
                        hs = slice(j * 64, (j + 1) * 64)
                        op("dve", lambda e: e.scalar_tensor_tensor(out=y_f[:, hs], in0=pyo.t[:, jj * 64:(jj + 1) * 64],
                                                                   scalar=ea[:, j:j + 1], in1=y_f[:, hs],
                                                                   op0=ALU.mult, op1=ALU.add), r=[pyo, sm, big], w=[big])
                        op("dve", lambda e: e.scalar_tensor_tensor(out=Hst[:, hs], in0=Hst[:, hs], scalar=cd[:, j:j + 1],
                                                                   in1=pst.t[:, jj * 64:(jj + 1) * 64],
                                                                   op0=ALU.mult, op1=ALU.add), r=[pst, sm, big], w=[big])
                    op("act", lambda e: e.activation(out=Hbf[:, gs], in_=Hst[:, gs], func=AF.Copy), r=[big], w=[big])
                    op("dve", lambda e: e.tensor_tensor(out=y_f[:, gs], in0=y_f[:, gs], in1=zs_c[:, gs], op=ALU.mult), r=[big], w=[big])
                    op("pool", lambda e: e.memset(gss[:, g:g + 1], 0.0), w=[sm])
                    op("act", lambda e: e.activation(out=t_f[:, gs], in_=y_f[:, gs], func=AF.Square, accum_out=gss[:, g:g + 1]),
                       r=[big], w=[big, sm])
                    op("dve", lambda e: e.tensor_scalar(out=gss[:, g:g + 1], in0=gss[:, g:g + 1], scalar1=1.0 / 320, scalar2=EPS,
                                                       op0=ALU.mult, op1=ALU.add), r=[sm], w=[sm])
                    op("act", lambda e: e.activation(out=gss[:, g:g + 1], in_=gss[:, g:g + 1], func=AF.Sqrt), r=[sm], w=[sm])
                    op("dve", lambda e: e.reciprocal(out=gss[:, g:g + 1], in_=gss[:, g:g + 1]), r=[sm], w=[sm])
                    op("dve", lambda e: e.scalar_tensor_tensor(out=xdt_b[:, gs], in0=y_f[:, gs], scalar=gss[:, g:g + 1],
                                                               in1=snw[:, gs], op0=ALU.mult, op1=ALU.mult), r=[big, sm], w=[big])
                for q4 in range(5):
                    bank = nxt("pb", pb)
                    for kk in range(4):
                        k = q4 * 4 + kk
                        op("pe", lambda e: e.transpose(out=bank.t[:, kk * 128:(kk + 1) * 128], in_=xdt_b[:, k * 128:(k + 1) * 128],
                                                       identity=ident.t[:]), r=[big, ident], w=[bank])
                    sg = nxt("stg", stg)
                    op("act", lambda e: e.activation(out=sg.t[:], in_=bank.t[:, 0:512], func=AF.Copy), r=[bank], w=[sg])
                    dma("sp", MIXT.rearrange("(k p) s -> p k s", p=128)[:, q4 * 4:(q4 + 1) * 4, rs],
                        sg.t[:].rearrange("p (k s) -> p k s", s=128), sg, bMIXT)

        def tail_stage(ti):
            r0 = ti * 512
            hl = lambda c: Hloc[ti * 4 + c][:, :]
            tmpT = big.t[:, 0:KC * 512].rearrange("p (k n) -> p k n", n=512)
            mv = MIXT.rearrange("(k p) s -> p k s", p=128)
            dma("sp", xnT.t[:], mv[:, :, r0:r0 + 512], bMIXT, xnT)
            dma("sp", tmpT, mv[:, :, SL + r0:SL + r0 + 512], bMIXT, big)
            op("dve", lambda e: e.tensor_scalar(out=xnT.t[:], in0=xnT.t[:], scalar1=sel.t[:, 0:1], scalar2=None, op0=ALU.mult),
               r=[xnT, sel], w=[xnT])
            op("dve", lambda e: e.scalar_tensor_tensor(out=xnT.t[:], in0=tmpT, scalar=sel.t[:, 1:2], in1=xnT.t[:],
                                                       op0=ALU.mult, op1=ALU.add), r=[big, sel, xnT], w=[xnT])
            lin_tm(W_out, KC, 0, D, xnT, evac_to_Fd())
            post_stage(hl, bH, hl, bH, 1, 1.0)
            ffn_stage(1, hl, bH, hl, bH, 2, 2)
            rows_norm_T(hl, bH, 3)
            pT = big.t[:, 0:1024].rearrange("p (k n) -> p k n", n=512)
            for c in range(4):
                dma("sp", ra.t[:, 0:256], p_in[r0 + c * 128:r0 + (c + 1) * 128, :], bp, ra)
                op("dve", lambda e: e.tensor_copy(out=xs.t[:, 0:256], in_=ra.t[:, 0:256]), r=[ra], w=[xs])
                bank = nxt("pb", pb)
                for kk in range(2):
                    op("pe", lambda e: e.transpose(out=bank.t[:, kk * 128:(kk + 1) * 128], in_=xs.t[:, kk * 128:(kk + 1) * 128],
                                                   identity=ident.t[:]), r=[xs, ident], w=[bank])
                for kk in range(2):
                    op("dve", lambda e: e.tensor_copy(out=pT[:, kk, c * 128:(c + 1) * 128], in_=bank.t[:, kk * 128:(kk + 1) * 128]),
                       r=[bank], w=[big])
            ppst = ra.t[:, 0:1024].rearrange("p (t c) -> p t c", c=256)

            for j in range(D // 256):
                cc = j * 256
                banks = [nxt("pf", pf) for _ in range(4)]
                s = load_slab([(W_pp, 0, 2, cc, 256, 0)])
                for tc in range(4):
                    for k in range(2):
                        op("pe", lambda e: e.matmul(banks[tc].t[:, 0:256], lhsT=pT[:, k, tc * 128:(tc + 1) * 128],
                                                    rhs=s.t[:, k, 0:256], start=(k == 0), stop=(k == 1)),
                           r=[s, big], w=[banks[tc]])
                for tc in range(4):
                    op("act", lambda e: e.activation(out=ppst[:, tc, :], in_=banks[tc].t[:, 0:256], func=AF.Copy),
                       r=[banks[tc]], w=[ra])
                banks = [nxt("pf", pf) for _ in range(4)]
                s = load_slab([(W_pg, 0, KC, cc, 256, 0)])
                for tc in range(4):
                    for k in range(KC):
                        op("pe", lambda e: e.matmul(banks[tc].t[:, 0:256], lhsT=xnT.t[:, k, tc * 128:(tc + 1) * 128],
                                                    rhs=s.t[:, k, 0:256], start=(k == 0), stop=(k == KC - 1)),
                           r=[s, xnT], w=[banks[tc]])
                f = nxt("fst", fst)
                for tc in range(4):
                    op("act", lambda e: e.activation(out=f.t[:, tc, :], in_=banks[tc].t[:, 0:256], func=AF.Sigmoid),
                       r=[banks[tc]], w=[f])
                    op("dve", lambda e: e.tensor_tensor(out=f.t[:, tc, :], in0=f.t[:, tc, :], in1=ppst[:, tc, :], op=ALU.mult),
                       r=[f, ra], w=[f])
                dma("sp", Fd.rearrange("(t p) c -> p t c", p=128)[:, :, cc:cc + 256], f.t[:], f, bFd)
            post_stage(hl, bH, lambda c: out[r0 + c * 128:r0 + (c + 1) * 128, :], bout, 3, 1.0)

        groups = [[0, 1], [2, 3], [4, 5], [6, 7]]
        for ti in range(NTL):
            ffn_stage(0, lambda c: x_in[ti * 512 + c * 128:ti * 512 + (c + 1) * 128, :], bx,
                      lambda c: Hloc[ti * 4 + c][:, :], bH, 0, 0)
        for k in range(NCL):
            sch.cc(Hloc_t[k].ap().opt(), Hg_t[k].ap().opt(), bH, bHg, groups)
        for ti in range(NT):
            inproj_stage(ti)
        attention()
        ssd()
        for ti in range(NTL):
            tail_stage(ti)
        sch.finish("sp", [bout])
        sch.finish("act", [bout])
    return nc


def prep_inputs(inp, b, par):
    f = np.float32

    def col(v):
        v = np.asarray(v, f).reshape(-1)
        return np.ascontiguousarray(v.reshape(-1, 128).T)

    def bc(v):
        v = np.asarray(v, f).reshape(1, -1)
        return np.ascontiguousarray(np.broadcast_to(v, (128, v.shape[1])))
    m = {}
    SL = inp["x"].shape[1] // 2
    m["x"] = np.ascontiguousarray(inp["x"][b, par * SL:(par + 1) * SL])
    m["p"] = np.ascontiguousarray(inp["p"][0, b, par * SL:(par + 1) * SL])
    m["sel"] = np.ascontiguousarray(np.broadcast_to(np.eye(2, dtype=f)[par][None, :], (128, 2)))
    for k in ("ffn1_w_gate", "ffn1_w_up", "ffn1_w_down", "ffn2_w_gate", "ffn2_w_up", "ffn2_w_down",
              "w_in", "w_out", "w_ple_gate", "w_ple_proj"):
        m[k] = np.asarray(inp[k][0], f)
    m["gcols"] = np.ascontiguousarray(np.concatenate(
        [col(inp[k][0]) for k in ("ffn1_pre_w", "mix_pre_w", "ffn2_pre_w", "ple_pre_w")], axis=1))
    m["postw"] = np.ascontiguousarray(np.stack(
        [bc(inp[k][0]) for k in ("ffn1_post_w", "mix_post_w", "ffn2_post_w", "ple_post_w")], axis=0))
    cw = np.asarray(inp["conv_w"][0], f)
    m["convw"] = np.ascontiguousarray(cw.T.reshape(36, 128, 4).transpose(1, 0, 2).reshape(128, 144))
    m["convb"] = col(inp["conv_b"][0])
    m["headv"] = np.ascontiguousarray(np.concatenate(
        [bc(inp["dt_bias"][0]), bc(inp["a_log"][0]), bc(inp["d_skip"][0])], axis=1))
    m["dskx"] = bc(np.repeat(np.asarray(inp["d_skip"][0], f), 64))
    m["snw"] = bc(inp["ssd_norm_w"][0])
    m.update(host_consts())
    return m


_NC_CACHE = {}


def kernel(**inputs):
    B, S, _ = inputs["x"].shape
    DFF = inputs["ffn1_w_gate"].shape[-1]
    key = (S, DFF)
    if key not in _NC_CACHE:
        _NC_CACHE[key] = build(S, DFF)
    nc = _NC_CACHE[key]
    n = 8
    in_maps = [prep_inputs(inputs, (c // 2) % B, c % 2) for c in range(n)]
    res = run_bass_kernel_spmd(nc, in_maps, core_ids=list(range(n)))
    outs = [np.concatenate([res.results[2 * b]["out"], res.results[2 * b + 1]["out"]], axis=0) for b in range(B)]
    return np.stack(outs, axis=0).astype(np.float32)
```

```python
import math
from contextlib import ExitStack
import numpy as np
import concourse.bass as bass
import concourse.mybir as mybir
from concourse.bass_utils import run_bass_kernel_spmd

F32 = mybir.dt.float32
BF16 = mybir.dt.bfloat16
AF = mybir.ActivationFunctionType
ALU = mybir.AluOpType

D = 4096
KC = 32
EPS = 1e-6
SSD_W = 2560
NH = 40
NG = 8
ATT_W = 1536
AH = 12
OFF_Z, OFF_X, OFF_DT, OFF_Q, OFF_K, OFF_V = 0, 2560, 7168, 7208, 8744, 10280
EPOCH = 20000
BIG = 1.0e6
import os
DEBUG = int(os.environ.get("KDEBUG", "0"))
QS = 128 ** -0.5


def alibi_slopes(n):
    def pow2(m):
        start = 2.0 ** (-8.0 / m)
        return [start ** (i + 1) for i in range(m)]
    if math.log2(n).is_integer():
        s = pow2(n)
    else:
        c = 2 ** int(math.floor(math.log2(n)))
        s = pow2(c) + pow2(2 * c)[0::2][: n - c]
    return [float(np.float32(v)) for v in s]


class Ev:
    __slots__ = ("sem", "val", "key", "order")

    def __init__(self, sem, val, key, order):
        self.sem, self.val, self.key, self.order = sem, val, key, order


class Buf:
    def __init__(self, name, t=None):
        self.name = name
        self.t = t
        self.writes = {}
        self.reads = {}


class Sched:
    def __init__(self, nc, es):
        self.nc, self.es = nc, es
        self.eng = {"pe": nc.tensor, "act": nc.scalar, "dve": nc.vector, "pool": nc.gpsimd, "sp": nc.sync}
        self.cnt = {k: 0 for k in self.eng}
        self.psems = {k: [] for k in self.eng}
        self.seen = {k: {} for k in self.eng}
        self.dsem = {}
        self.dcnt = {}
        self.nsem = 0

    def _sem(self, name):
        self.nsem += 1
        return self.es.enter_context(self.nc.semaphore(name))

    def _wait(self, e, ev):
        if self.seen[e].get(ev.key, -1) >= ev.order:
            return
        self.eng[e].wait_ge(ev.sem, ev.val)
        self.seen[e][ev.key] = ev.order

    def _deps(self, e, reads, writes):
        for b in reads:
            for ev in b.writes.values():
                if not (e == "pe" and ev.key == "pe"):
                    self._wait(e, ev)
        for b in writes:
            for ev in list(b.writes.values()) + list(b.reads.values()):
                if not (e == "pe" and ev.key == "pe"):
                    self._wait(e, ev)

    def _record(self, ev, reads, writes):
        for b in reads:
            b.reads[ev.key] = ev
        for b in writes:
            b.writes = {ev.key: ev}
            b.reads = {}

    def op(self, e, fn, r=(), w=()):
        self._deps(e, r, w)
        ins = fn(self.eng[e])
        n = self.cnt[e]
        ep = n // EPOCH
        while len(self.psems[e]) <= ep:
            self.psems[e].append(self._sem(f"p_{e}_{len(self.psems[e])}"))
        self.cnt[e] = n + 1
        ins.then_inc(self.psems[e][ep], 1)
        ev = Ev(self.psems[e][ep], n % EPOCH + 1, e, n)
        self._record(ev, r, w)
        return ev

    def dma(self, q, out_ap, in_ap, src, dst, **kw):
        self._deps(q, [src], [dst])
        key = "d:" + src.name + ">" + dst.name
        if key not in self.dsem:
            self.dsem[key] = self._sem("d%d" % len(self.dsem))
            self.dcnt[key] = 0
        self.dcnt[key] += 16
        self.eng[q].dma_start(out=out_ap, in_=in_ap, **kw).then_inc(self.dsem[key], 16)
        ev = Ev(self.dsem[key], self.dcnt[key], key, self.dcnt[key])
        self._record(ev, [src], [dst])
        return ev

    def cc(self, in_ap, out_ap, src, dst, groups):
        self._deps("pool", [src], [dst])
        if "cc" not in self.dsem:
            self.dsem["cc"] = self._sem("ccs")
            self.dcnt["cc"] = 0
        self.dcnt["cc"] += 1
        self.eng["pool"].collective_compute("AllGather", ALU.bypass, replica_groups=groups,
                                            ins=[in_ap], outs=[out_ap]).then_inc(self.dsem["cc"], 1)
        ev = Ev(self.dsem["cc"], self.dcnt["cc"], "cc", self.dcnt["cc"])
        self._record(ev, [src], [dst])
        return ev

    def finish(self, e, bufs):
        for b in bufs:
            for ev in list(b.writes.values()):
                self._wait(e, ev)


def host_consts():
    c = {}
    c["ident"] = np.eye(128, dtype=np.float32)
    kk = np.arange(128)[:, None]
    ll = np.arange(128)[None, :]
    c["triu"] = (kk <= ll).astype(np.float32)
    c["negm"] = np.where(ll < kk, -30000.0, 0.0).astype(np.float32)
    prev = np.where(kk >= ll, 128.0 + ll - kk, BIG)
    cur = np.where(kk <= ll, (ll - kk).astype(np.float64), BIG)
    c["distm"] = np.concatenate([prev, cur], axis=1).astype(np.float32)
    return c


def build(S, DFF):
    NT = S // 512
    NCH = S // 128
    SL = S // 2
    NTL = SL // 512
    NCL = SL // 128
    FC = DFF // 128
    nc = bass.Bass("TRN2", target_bir_lowering=False)

    def din(name, shape):
        return nc.dram_tensor(name, list(shape), F32, kind="ExternalInput").ap()

    x_in = din("x", [SL, D])
    p_in = din("p", [SL, 256])
    sel_in = din("sel", [128, 2])
    Wg = [din("ffn1_w_gate", [D, DFF]), din("ffn2_w_gate", [D, DFF])]
    Wu = [din("ffn1_w_up", [D, DFF]), din("ffn2_w_up", [D, DFF])]
    Wd = [din("ffn1_w_down", [DFF, D]), din("ffn2_w_down", [DFF, D])]
    W_in = din("w_in", [D, 11816])
    W_out = din("w_out", [D, D])
    W_pg = din("w_ple_gate", [D, D])
    W_pp = din("w_ple_proj", [256, D])
    gcols_in = din("gcols", [128, 4 * KC])
    postw_in = din("postw", [4, 128, D])
    convw_in = din("convw", [128, 36 * 4])
    convb_in = din("convb", [128, 36])
    hv_in = din("headv", [128, 3 * NH])
    dskx_in = din("dskx", [128, SSD_W])
    snw_in = din("snw", [128, SSD_W])
    ident_in = din("ident", [128, 128])
    triu_in = din("triu", [128, 128])
    negm_in = din("negm", [128, 128])
    distm_in = din("distm", [128, 256])
    out = nc.dram_tensor("out", [SL, D], F32, kind="ExternalOutput").ap()
    def dscr(name, shape, dt):
        return nc.dram_tensor(name, list(shape), dt).ap()

    Hloc_t = [nc.dram_tensor("Hloc%d" % k, [128, D], F32) for k in range(NCL)]
    Hg_t = [nc.dram_tensor("Hg%d" % k, [256, D], F32) for k in range(NCL)]
    Hloc = [t.ap() for t in Hloc_t]
    Hg = [t.ap() for t in Hg_t]
    Fd = dscr("Fd", [512, D], F32)
    ZS = dscr("ZS", [S, SSD_W], BF16)
    XS = dscr("XS", [S, SSD_W], BF16)
    BM = dscr("BM", [S, 1024], BF16)
    BT = dscr("BT", [1024, S], BF16)
    CT = dscr("CT", [1024, S], BF16)
    QT = dscr("QT", [ATT_W, S], BF16)
    KT = dscr("KT", [ATT_W, S], BF16)
    Vd = dscr("Vd", [S, ATT_W], BF16)
    MIXT = dscr("MIXT", [D, S], BF16)

    es = ExitStack()
    with es:
        sch = Sched(nc, es)
        op, dma = sch.op, sch.dma

        def sb(name, shape, dt):
            return Buf(name, es.enter_context(nc.sbuf_tensor("s_" + name, list(shape), dt)))

        def ps(name, shape, dt):
            return Buf(name, es.enter_context(nc.psum_tensor(name, list(shape), dt)))

        bx, bp, bW = Buf("x"), Buf("p"), Buf("W")
        bcst = Buf("cst")
        bH, bHg, bFd, bZS, bXS, bBM, bBT, bCT = Buf("H"), Buf("Hg"), Buf("Fd"), Buf("ZS"), Buf("XS"), Buf("BM"), Buf("BT"), Buf("CT")
        bQT, bKT, bV, bMIXT, bout = Buf("QT"), Buf("KT"), Buf("V"), Buf("MIXT"), Buf("out")

        xnT = sb("xnT", [128, KC, 512], BF16)
        big = sb("big", [128, max(FC * 512, 43008)], BF16)
        slabs = [sb("slab%d" % i, [128, KC, 256], BF16) for i in range(2)]
        ra = sb("ra", [128, D], F32)
        xs = sb("xs", [128, D], BF16)
        fst = [sb("fst%d" % i, [128, 4, 256], F32) for i in range(2)]
        sil = [sb("sil%d" % i, [128, 512], F32) for i in range(2)]
        stg = [sb("stg%d" % i, [128, 512], BF16) for i in range(2)]
        gcols = sb("gcols", [128, 4 * KC], F32)
        convw = sb("convw", [128, 36 * 4], F32)
        convb = sb("convb", [128, 36], F32)
        halo = sb("halo", [128, 36, 4], F32)
        headv = sb("headv", [128, 3 * NH], F32)
        ident = sb("ident", [128, 128], BF16)
        identf = sb("identf", [128, 128], F32)
        triu = sb("triu", [128, 128], F32)
        onesf = sb("onesf", [128, 128], F32)
        onesb = sb("onesb", [128, 128], BF16)
        distm = sb("distm", [128, 256], F32)
        dt_sb = sb("dt_sb", [128, NCH, NH], F32)
        ss = sb("ss", [128, 1], F32)
        rstd = sb("rstd", [128, 1], F32)
        cst = sb("cstage", [128, 516], F32)
        acc = sb("cacc", [128, 512], F32)
        sm = sb("small", [128, 8 * NH], F32)
        gss = sb("gss", [128, 8], F32)
        pf = [ps("pf%d" % i, [128, 512], F32) for i in range(6)]
        pb = [ps("pb%d" % i, [128, 1024], BF16) for i in range(2)]
        st = {"pf": 0, "pb": 0, "slab": 0, "fst": 0, "sil": 0, "stg": 0}

        def nxt(kind, lst):
            i = st[kind]
            st[kind] = i + 1
            return lst[i % len(lst)]

        def ld(buf, ap_in, tmp=None):
            dma("sp", buf.t[:], ap_in, bcst, buf)

        ld(gcols, gcols_in[:, :])
        ld(convw, convw_in[:, :])
        ld(convb, convb_in[:, :])
        ld(headv, hv_in[:, :])
        ld(identf, ident_in[:, :])
        ld(triu, triu_in[:, :])
        ld(distm, distm_in[:, :])
        sel = sb("sel", [128, 2], F32)
        ld(sel, sel_in[:, :])
        op("dve", lambda e: e.tensor_copy(out=ident.t[:], in_=identf.t[:]), r=[identf], w=[ident])
        op("pool", lambda e: e.memset(onesf.t[:], 1.0), w=[onesf])
        op("pool", lambda e: e.memset(onesb.t[:], 1.0), w=[onesb])
        op("pool", lambda e: e.memset(halo.t[:], 0.0), w=[halo])

        def load_slab(parts):
            s = nxt("slab", slabs)
            for (W, k0, nk, c0, ncols, d0) in parts:
                wv = W.rearrange("(kc p) n -> p kc n", p=128)
                dma("pool", s.t[:, 0:nk, d0:d0 + ncols], wv[:, k0:k0 + nk, c0:c0 + ncols], bW, s)
            return s

        def rmsnorm_stats(src):
            n = D
            op("pool", lambda e: e.memset(ss.t[:], 0.0), w=[ss])
            op("act", lambda e: e.activation(out=xs.t[:], in_=src.t[:], func=AF.Square, accum_out=ss.t[:, 0:1]),
               r=[src], w=[xs, ss])
            op("dve", lambda e: e.tensor_scalar(out=rstd.t[:], in0=ss.t[:], scalar1=1.0 / n, scalar2=EPS,
                                               op0=ALU.mult, op1=ALU.add), r=[ss], w=[rstd])
            op("act", lambda e: e.activation(out=rstd.t[:], in_=rstd.t[:], func=AF.Sqrt), r=[rstd], w=[rstd])
            op("dve", lambda e: e.reciprocal(out=rstd.t[:], in_=rstd.t[:]), r=[rstd], w=[rstd])

        def rows_norm_T(getsrc, bsrc, gi):
            for c in range(4):
                dma("sp", ra.t[:], getsrc(c), bsrc, ra)
                rmsnorm_stats(ra)
                op("act", lambda e: e.activation(out=xs.t[:], in_=ra.t[:], func=AF.Copy, scale=rstd.t[:, 0:1]),
                   r=[ra, rstd], w=[xs])
                for kg in range(4):
                    bank = nxt("pb", pb)
                    for kk in range(8):
                        k = kg * 8 + kk
                        op("pe", lambda e: e.transpose(out=bank.t[:, kk * 128:(kk + 1) * 128],
                                                       in_=xs.t[:, k * 128:(k + 1) * 128], identity=ident.t[:]),
                           r=[xs, ident], w=[bank])
                    for kk in range(8):
                        k = kg * 8 + kk
                        g = gcols.t[:, gi * KC + k:gi * KC + k + 1]
                        o = xnT.t[:, k, c * 128:(c + 1) * 128]
                        i_ = bank.t[:, kk * 128:(kk + 1) * 128]
                        if kk % 2:
                            op("act", lambda e: e.activation(out=o, in_=i_, func=AF.Copy, scale=g),
                               r=[bank, gcols], w=[xnT])
                        else:
                            op("dve", lambda e: e.tensor_scalar(out=o, in0=i_, scalar1=g, scalar2=None, op0=ALU.mult),
                               r=[bank, gcols], w=[xnT])

        def lin_fm(W, c0, nchunks, evac):
            m = 0
            while m < nchunks:
                nm = min(2, nchunks - m)
                s = load_slab([(W, 0, KC, c0 + m * 128, nm * 128, 0)])
                for mm in range(nm):
                    bank = nxt("pf", pf)
                    for k in range(KC):
                        op("pe", lambda e: e.matmul(bank.t[:], lhsT=s.t[:, k, mm * 128:(mm + 1) * 128],
                                                    rhs=xnT.t[:, k, :], start=(k == 0), stop=(k == KC - 1)),
                           r=[s, xnT], w=[bank])
                    evac(m + mm, bank)
                m += nm

        def lin_tm(W, nk_tot, c0, ncols_tot, lhs, evac, blk=256):
            j = 0
            while j * blk < ncols_tot:
                cc = c0 + j * blk
                ncols = min(blk, ncols_tot - j * blk)
                banks = [nxt("pf", pf) for _ in range(4)]
                for k0 in range(0, nk_tot, KC):
                    nk = min(KC, nk_tot - k0)
                    s = load_slab([(W, k0, nk, cc, ncols, 0)])
                    for tc in range(4):
                        for kk in range(nk):
                            k = k0 + kk
                            op("pe", lambda e: e.matmul(banks[tc].t[:, 0:ncols],
                                                        lhsT=lhs.t[:, k, tc * 128:(tc + 1) * 128],
                                                        rhs=s.t[:, kk, 0:ncols],
                                                        start=(k == 0), stop=(k == nk_tot - 1)),
                               r=[s, lhs], w=[banks[tc]])
                evac(j, cc - c0, ncols, banks)
                j += 1

        def evac_to_Fd(func=None):
            def ev(j, coff, ncols, banks):
                f = nxt("fst", fst)
                for tc in range(4):
                    if tc % 2:
                        op("act", lambda e: e.activation(out=f.t[:, tc, 0:ncols], in_=banks[tc].t[:, 0:ncols],
                                                         func=AF.Copy), r=[banks[tc]], w=[f])
                    else:
                        op("dve", lambda e: e.tensor_copy(out=f.t[:, tc, 0:ncols], in_=banks[tc].t[:, 0:ncols]),
                           r=[banks[tc]], w=[f])
                dma("sp", Fd.rearrange("(t p) c -> p t c", p=128)[:, :, coff:coff + ncols], f.t[:, :, 0:ncols], f, bFd)
            return ev

        def post_stage(getsrc, bsrc, getdst, bdst, wi, coef):
            wbv = big.t[:, 2048:2048 + 2 * D].bitcast(F32)
            rbv = big.t[:, 2048 + 2 * D:2048 + 4 * D].bitcast(F32)
            dma("sp", wbv, postw_in[wi, :, :], bcst, big)
            for c in range(4):
                dma("sp", ra.t[:], Fd[c * 128:(c + 1) * 128, :], bFd, ra)
                dma("sp", rbv, getsrc(c), bsrc, big)
                rmsnorm_stats(ra)
                op("dve", lambda e: e.scalar_tensor_tensor(out=ra.t[:], in0=ra.t[:], scalar=rstd.t[:, 0:1], in1=wbv,
                                                           op0=ALU.mult, op1=ALU.mult), r=[ra, rstd, big], w=[ra])
                op("dve", lambda e: e.scalar_tensor_tensor(out=rbv, in0=ra.t[:], scalar=float(coef), in1=rbv,
                                                           op0=ALU.mult, op1=ALU.add), r=[ra, big], w=[big])
                dma("sp", getdst(c), rbv, big, bdst)

        def ffn_stage(li, getsrc, bsrc, getdst, bdst, gi, wi):
            rows_norm_T(getsrc, bsrc, gi)
            gT = big
            for fc in range(FC):
                s = load_slab([(Wg[li], 0, KC, fc * 128, 128, 0), (Wu[li], 0, KC, fc * 128, 128, 128)])
                pg = nxt("pf", pf)
                pu = nxt("pf", pf)
                for k in range(KC):
                    op("pe", lambda e: e.matmul(pg.t[:], lhsT=s.t[:, k, 0:128], rhs=xnT.t[:, k, :],
                                                start=(k == 0), stop=(k == KC - 1)), r=[s, xnT], w=[pg])
                for k in range(KC):
                    op("pe", lambda e: e.matmul(pu.t[:], lhsT=s.t[:, k, 128:256], rhs=xnT.t[:, k, :],
                                                start=(k == 0), stop=(k == KC - 1)), r=[s, xnT], w=[pu])
                sl = nxt("sil", sil)
                op("act", lambda e: e.activation(out=sl.t[:], in_=pg.t[:], func=AF.Silu), r=[pg], w=[sl])
                op("dve", lambda e: e.tensor_tensor(out=gT.t[:, fc * 512:(fc + 1) * 512], in0=sl.t[:], in1=pu.t[:],
                                                    op=ALU.mult), r=[sl, pu], w=[gT])
            gT3 = gT.t[:, 0:FC * 512].rearrange("p (k n) -> p k n", n=512)
            lin_tm_view(Wd[li], FC, 0, D, gT, gT3, evac_to_Fd())
            post_stage(getsrc, bsrc, getdst, bdst, wi, 0.5)

        def lin_tm_view(W, nk_tot, c0, ncols_tot, track, view, evac, blk=256):
            j = 0
            while j * blk < ncols_tot:
                cc = c0 + j * blk
                ncols = min(blk, ncols_tot - j * blk)
                banks = [nxt("pf", pf) for _ in range(4)]
                for k0 in range(0, nk_tot, KC):
                    nk = min(KC, nk_tot - k0)
                    s = load_slab([(W, k0, nk, cc, ncols, 0)])
                    for tc in range(4):
                        for kk in range(nk):
                            k = k0 + kk
                            op("pe", lambda e: e.matmul(banks[tc].t[:, 0:ncols],
                                                        lhsT=view[:, k, tc * 128:(tc + 1) * 128],
                                                        rhs=s.t[:, kk, 0:ncols],
                                                        start=(k == 0), stop=(k == nk_tot - 1)),
                               r=[s, track], w=[banks[tc]])
                evac(j, cc - c0, ncols, banks)
                j += 1

        def inproj_stage(ti):
            r0 = ti * 512
            rows_norm_T(lambda c: Hg[(ti * 4 + c) % NCL][((ti * 4 + c) // NCL) * 128:((ti * 4 + c) // NCL + 1) * 128, :], bHg, 1)

            def ev_z(j, coff, ncols, banks):
                for tc in range(4):
                    sg = nxt("stg", stg)
                    op("act", lambda e: e.activation(out=sg.t[:, 0:ncols], in_=banks[tc].t[:, 0:ncols], func=AF.Silu),
                       r=[banks[tc]], w=[sg])
                    dma("sp", ZS[r0 + tc * 128:r0 + (tc + 1) * 128, coff:coff + ncols], sg.t[:, 0:ncols], sg, bZS)
            lin_tm(W_in, KC, OFF_Z, SSD_W, xnT, ev_z)

            def ev_v(j, coff, ncols, banks):
                for tc in range(4):
                    sg = nxt("stg", stg)
                    op("dve", lambda e: e.tensor_copy(out=sg.t[:, 0:ncols], in_=banks[tc].t[:, 0:ncols]),
                       r=[banks[tc]], w=[sg])
                    dma("sp", Vd[r0 + tc * 128:r0 + (tc + 1) * 128, coff:coff + ncols], sg.t[:, 0:ncols], sg, bV)
            lin_tm(W_in, KC, OFF_V, ATT_W, xnT, ev_v)

            def ev_dt(j, coff, ncols, banks):
                for tc in range(4):
                    ch = ti * 4 + tc
                    o = dt_sb.t[:, ch, :]
                    op("dve", lambda e: e.tensor_tensor(out=o, in0=banks[tc].t[:, 0:NH], in1=headv.t[:, 0:NH], op=ALU.add),
                       r=[banks[tc], headv], w=[dt_sb])
                    op("act", lambda e: e.activation(out=o, in_=o, func=AF.Exp), r=[dt_sb], w=[dt_sb])
                    op("act", lambda e: e.activation(out=o, in_=o, func=AF.Ln, bias=1.0, scale=1.0), r=[dt_sb], w=[dt_sb])
            lin_tm(W_in, KC, OFF_DT, NH, xnT, ev_dt)

            def ev_xbc(m, bank):
                op("dve", lambda e: e.tensor_copy(out=cst.t[:, 0:3], in_=halo.t[:, m, 0:3]), r=[halo], w=[cst])
                op("act", lambda e: e.activation(out=cst.t[:, 3:515], in_=bank.t[:], func=AF.Copy), r=[bank], w=[cst])
                op("dve", lambda e: e.tensor_scalar(out=acc.t[:], in0=cst.t[:, 0:512], scalar1=convw.t[:, m * 4:m * 4 + 1],
                                                   scalar2=convb.t[:, m:m + 1], op0=ALU.mult, op1=ALU.add),
                   r=[cst, convw, convb], w=[acc])
                for kq in range(1, 4):
                    op("dve", lambda e: e.scalar_tensor_tensor(out=acc.t[:], in0=cst.t[:, kq:kq + 512],
                                                               scalar=convw.t[:, m * 4 + kq:m * 4 + kq + 1], in1=acc.t[:],
                                                               op0=ALU.mult, op1=ALU.add), r=[cst, convw, acc], w=[acc])
                op("dve", lambda e: e.tensor_copy(out=halo.t[:, m, 0:3], in_=cst.t[:, 512:515]), r=[cst], w=[halo])
                sg = nxt("stg", stg)
                op("act", lambda e: e.activation(out=sg.t[:], in_=acc.t[:], func=AF.Silu), r=[acc], w=[sg])
                if m >= 28:
                    dma("sp", CT[(m - 28) * 128:(m - 27) * 128, r0:r0 + 512], sg.t[:], sg, bCT)
                    return
                if m >= 20:
                    dma("sp", BT[(m - 20) * 128:(m - 19) * 128, r0:r0 + 512], sg.t[:], sg, bBT)
                bank2 = nxt("pb", pb)
                for tc in range(4):
                    op("pe", lambda e: e.transpose(out=bank2.t[:, tc * 128:(tc + 1) * 128],
                                                   in_=sg.t[:, tc * 128:(tc + 1) * 128], identity=ident.t[:]),
                       r=[sg, ident], w=[bank2])
                sg2 = nxt("stg", stg)
                op("dve", lambda e: e.tensor_copy(out=sg2.t[:], in_=bank2.t[:, 0:512]), r=[bank2], w=[sg2])
                if m >= 20:
                    dst, bd, cc = BM, bBM, (m - 20) * 128
                else:
                    dst, bd, cc = XS, bXS, m * 128
                dma("sp", dst.rearrange("(t p) c -> p t c", p=128)[:, ti * 4:(ti + 1) * 4, cc:cc + 128],
                    sg2.t[:].rearrange("p (t c) -> p t c", c=128), sg2, bd)
            lin_fm(W_in, OFF_X, 36, ev_xbc)

            def mk_ev(dst, bd):
                def ev(m, bank):
                    sg = nxt("stg", stg)
                    if m % 2:
                        op("act", lambda e: e.activation(out=sg.t[:], in_=bank.t[:], func=AF.Copy), r=[bank], w=[sg])
                    else:
                        op("dve", lambda e: e.tensor_copy(out=sg.t[:], in_=bank.t[:]), r=[bank], w=[sg])
                    dma("sp", dst[m * 128:(m + 1) * 128, r0:r0 + 512], sg.t[:], sg, bd)
                return ev
            lin_fm(W_in, OFF_Q, AH, mk_ev(QT, bQT))
            lin_fm(W_in, OFF_K, AH, mk_ev(KT, bKT))

        def fork(names):
            subs = {}
            for n in names:
                bb = Buf(n)
                bb.writes = dict(big.writes)
                bb.reads = dict(big.reads)
                subs[n] = bb
            return subs

        def join(subs):
            m = {}
            for bb in [big] + list(subs.values()):
                for ev in list(bb.writes.values()) + list(bb.reads.values()):
                    if ev.key not in m or m[ev.key].order < ev.order:
                        m[ev.key] = ev
            big.writes = m
            big.reads = {}

        def attention():
            slopes = alibi_slopes(AH)
            bg = big.t
            QTh = bg[:, 0:S]
            KTh = bg[:, S:2 * S]
            Vhs = [bg[:, 2 * S:3 * S], bg[:, 8 * S:9 * S]]
            attT = bg[:, 3 * S:4 * S]
            numT = bg[:, 4 * S:6 * S].bitcast(F32)
            denT = bg[:, 6 * S:8 * S].bitcast(F32)
            sb_ = fork(["a_qk", "a_v0", "a_v1", "a_num", "a_att"])
            bqk, bnum, batt = sb_["a_qk"], sb_["a_num"], sb_["a_att"]
            bvs = [sb_["a_v0"], sb_["a_v1"]]
            vi = 0
            for h in range(AH):
                dma("sp", QTh, QT[h * 128:(h + 1) * 128, :], bQT, bqk)
                dma("sp", KTh, KT[h * 128:(h + 1) * 128, :], bKT, bqk)
                for bi, d in enumerate((1, 4, 16)):
                    nb = S // (128 * d)
                    cp = -slopes[h] * d / QS
                    Vh, bv = Vhs[vi % 2], bvs[vi % 2]
                    vi += 1
                    with nc.allow_non_contiguous_dma(reason="dilated V gather"):
                        for r in range(d):
                            dma("sp", Vh.rearrange("p (r b f) -> p r b f", r=d, b=nb)[:, r, :, :],
                                Vd.rearrange("(b i r) f -> i r b f", i=128, r=d)[:, r, :, h * 128:(h + 1) * 128], bV, bv)
                    V4 = Vh.rearrange("p (r b f) -> p r b f", r=d, b=nb)
                    for r in range(d):
                        for b in range(nb):
                            def tok(bb):
                                s0 = bb * 128 * d + r
                                return slice(s0, s0 + 127 * d + 1, d)
                            qs = tok(b)
                            sbank = nxt("pf", pf)
                            if b > 0:
                                op("pe", lambda e: e.matmul(sbank.t[:, 0:128], lhsT=KTh[:, tok(b - 1)], rhs=QTh[:, qs],
                                                            start=True, stop=True), r=[bqk], w=[sbank])
                            op("pe", lambda e: e.matmul(sbank.t[:, 128:256], lhsT=KTh[:, qs], rhs=QTh[:, qs],
                                                        start=True, stop=True), r=[bqk], w=[sbank])
                            lo = 0 if b > 0 else 128
                            sl = nxt("sil", sil)
                            op("dve", lambda e: e.scalar_tensor_tensor(out=sl.t[:, lo:256], in0=distm.t[:, lo:256],
                                                                       scalar=float(cp), in1=sbank.t[:, lo:256],
                                                                       op0=ALU.mult, op1=ALU.add),
                               r=[distm, sbank], w=[sl])
                            pT = nxt("stg", stg)
                            op("act", lambda e: e.activation(out=pT.t[:, lo:256], in_=sl.t[:, lo:256], func=AF.Exp,
                                                             scale=float(QS)), r=[sl], w=[pT])
                            nbank = nxt("pf", pf)
                            if b > 0:
                                op("pe", lambda e: e.matmul(nbank.t[:, 0:128], lhsT=V4[:, r, b - 1, :], rhs=pT.t[:, 0:128],
                                                            start=True, stop=False), r=[bv, pT], w=[nbank])
                            op("pe", lambda e: e.matmul(nbank.t[:, 0:128], lhsT=V4[:, r, b, :], rhs=pT.t[:, 128:256],
                                                        start=(b == 0), stop=True), r=[bv, pT], w=[nbank])
                            if b > 0:
                                op("pe", lambda e: e.matmul(nbank.t[:, 128:256], lhsT=onesb.t[:], rhs=pT.t[:, 0:128],
                                                            start=True, stop=False), r=[onesb, pT], w=[nbank])
                            op("pe", lambda e: e.matmul(nbank.t[:, 128:256], lhsT=onesb.t[:], rhs=pT.t[:, 128:256],
                                                        start=(b == 0), stop=True), r=[onesb, pT], w=[nbank])
                            if bi == 0:
                                op("dve", lambda e: e.tensor_copy(out=numT[:, qs], in_=nbank.t[:, 0:128]), r=[nbank], w=[bnum])
                                op("act", lambda e: e.activation(out=denT[:, qs], in_=nbank.t[:, 128:256], func=AF.Copy),
                                   r=[nbank], w=[bnum])
                            else:
                                op("dve", lambda e: e.tensor_tensor(out=numT[:, qs], in0=numT[:, qs], in1=nbank.t[:, 0:128],
                                                                    op=ALU.add), r=[nbank, bnum], w=[bnum])
                                op("dve", lambda e: e.tensor_tensor(out=denT[:, qs], in0=denT[:, qs], in1=nbank.t[:, 128:256],
                                                                    op=ALU.add), r=[nbank, bnum], w=[bnum])
                op("dve", lambda e: e.reciprocal(out=denT, in_=denT), r=[bnum], w=[bnum])
                op("dve", lambda e: e.tensor_tensor(out=attT, in0=numT, in1=denT, op=ALU.mult), r=[bnum], w=[batt])
                dma("sp", MIXT[SSD_W + h * 128:SSD_W + (h + 1) * 128, :], attT, batt, bMIXT)
            join(sb_)

        def ssd():
            bg = big.t
            o = 0

            def carve(n, dt=BF16):
                nonlocal o
                ne = n if dt == BF16 else 2 * n
                v = bg[:, o:o + ne]
                o += ne
                return v if dt == BF16 else v.bitcast(F32)
            xs_c = carve(SSD_W)
            zs_c = carve(SSD_W)
            bm_c = carve(1024)
            bt_c = carve(1024)
            ct_c = carve(1024)
            xdt_b = carve(SSD_W)
            xdtd_b = carve(SSD_W)
            Hbf = carve(SSD_W)
            Hst = carve(SSD_W, F32)
            y_f = carve(SSD_W, F32)
            t_f = carve(SSD_W, F32)
            negm = carve(128, F32)
            snw = carve(SSD_W, F32)
            dskx = carve(SSD_W, F32)
            NR = 2
            Rjs = [carve(128, F32) for _ in range(NR)]
            LTs = [carve(128, F32) for _ in range(NR)]
            MTs = [carve(128) for _ in range(NR)]
            names = ["s_xs", "s_zs", "s_bm", "s_bt", "s_ct", "s_xdtb", "s_cst"]
            names += ["s_xdtd%d" % g for g in range(NG)] + ["s_H%d" % g for g in range(NG)] + ["s_Hbf%d" % g for g in range(NG)]
            names += ["s_y%d" % g for g in range(NG)] + ["s_t%d" % g for g in range(NG)]
            names += ["s_R%d" % i for i in range(NR)] + ["s_L%d" % i for i in range(NR)] + ["s_M%d" % i for i in range(NR)]
            sb_ = fork(names)
            b_xs, b_zs, b_bm, b_bt, b_ct, b_xdtb, b_cst = [sb_[n] for n in names[:7]]
            b_xdtd = [sb_["s_xdtd%d" % g] for g in range(NG)]
            b_H = [sb_["s_H%d" % g] for g in range(NG)]
            b_Hbf = [sb_["s_Hbf%d" % g] for g in range(NG)]
            b_y = [sb_["s_y%d" % g] for g in range(NG)]
            b_t = [sb_["s_t%d" % g] for g in range(NG)]
            b_R = [sb_["s_R%d" % i] for i in range(NR)]
            b_L = [sb_["s_L%d" % i] for i in range(NR)]
            b_M = [sb_["s_M%d" % i] for i in range(NR)]
            dma("sp", negm, negm_in[:, :], bcst, b_cst)
            dma("sp", snw, snw_in[:, :], bcst, b_cst)
            dma("sp", dskx, dskx_in[:, :], bcst, b_cst)
            smt = sm.t
            dtA = smt[:, 0:NH]
            acs = smt[:, NH:2 * NH]
            nacs = smt[:, 2 * NH:3 * NH]
            ea = smt[:, 3 * NH:4 * NH]
            cd = smt[:, 4 * NH:5 * NH]
            dsc = smt[:, 5 * NH:6 * NH]
            a_b = smt[:, 6 * NH:7 * NH]
            dtd = smt[:, 7 * NH:8 * NH]
            op("act", lambda e: e.activation(out=a_b, in_=headv.t[:, NH:2 * NH], func=AF.Exp), r=[headv], w=[sm])
            op("dve", lambda e: e.tensor_scalar(out=a_b, in0=a_b, scalar1=-1.0, scalar2=None, op0=ALU.mult), r=[sm], w=[sm])
            for g in range(NG):
                gs = slice(g * 320, (g + 1) * 320)
                op("pool", lambda e: e.memset(Hst[:, gs], 0.0), w=[b_H[g]])
                op("pool", lambda e: e.memset(Hbf[:, gs], 0.0), w=[b_Hbf[g]])
            hc = 0
            for c in range(NCH):
                rs = slice(c * 128, (c + 1) * 128)
                dma("sp", xs_c, XS[rs, :], bXS, b_xs)
                dma("sp", zs_c, ZS[rs, :], bZS, b_zs)
                dma("sp", bm_c, BM[rs, :], bBM, b_bm)
                dma("sp", bt_c.rearrange("p (g s) -> p g s", g=NG), BT.rearrange("(g n) s -> n g s", n=128)[:, :, rs], bBT, b_bt)
                dma("sp", ct_c.rearrange("p (g s) -> p g s", g=NG), CT.rearrange("(g n) s -> n g s", n=128)[:, :, rs], bCT, b_ct)
                dtc = dt_sb.t[:, c, :]
                op("dve", lambda e: e.tensor_tensor(out=dtA, in0=dtc, in1=a_b, op=ALU.mult), r=[dt_sb, sm], w=[sm])
                pa = pf[4]
                op("pe", lambda e: e.matmul(pa.t[:, 0:NH], lhsT=triu.t[:], rhs=dtA, start=True, stop=True), r=[triu, sm], w=[pa])
                op("pe", lambda e: e.matmul(pa.t[:, 64:64 + NH], lhsT=onesf.t[:], rhs=dtA, start=True, stop=True),
                   r=[onesf, sm], w=[pa])
                op("dve", lambda e: e.tensor_copy(out=acs, in_=pa.t[:, 0:NH]), r=[pa], w=[sm])
                op("dve", lambda e: e.tensor_scalar(out=nacs, in0=pa.t[:, 0:NH], scalar1=-1.0, scalar2=None, op0=ALU.mult),
                   r=[pa], w=[sm])
                op("act", lambda e: e.activation(out=ea, in_=pa.t[:, 0:NH], func=AF.Exp), r=[pa], w=[sm])
                op("act", lambda e: e.activation(out=cd, in_=pa.t[:, 64:64 + NH], func=AF.Exp), r=[pa], w=[sm])
                op("dve", lambda e: e.tensor_tensor(out=dsc, in0=pa.t[:, 64:64 + NH], in1=acs, op=ALU.subtract), r=[pa, sm], w=[sm])
                op("act", lambda e: e.activation(out=dsc, in_=dsc, func=AF.Exp), r=[sm], w=[sm])
                op("dve", lambda e: e.tensor_tensor(out=dtd, in0=dsc, in1=dtc, op=ALU.mult), r=[sm, dt_sb], w=[sm])
                for j in range(NH):
                    hs = slice(j * 64, (j + 1) * 64)
                    g = j // 5
                    op("dve", lambda e: e.tensor_scalar(out=xdt_b[:, hs], in0=xs_c[:, hs], scalar1=dtc[:, j:j + 1], scalar2=None,
                                                       op0=ALU.mult), r=[b_xs, dt_sb], w=[b_xdtb])
                    op("dve", lambda e: e.tensor_scalar(out=xdtd_b[:, hs], in0=xs_c[:, hs], scalar1=dtd[:, j:j + 1], scalar2=None,
                                                       op0=ALU.mult), r=[b_xs, sm], w=[b_xdtd[g]])
                for g in range(NG):
                    gs = slice(g * 320, (g + 1) * 320)
                    n_s = slice(g * 128, (g + 1) * 128)
                    pyo = pf[0]
                    op("pe", lambda e: e.matmul(pyo.t[:, 0:320], lhsT=ct_c[:, n_s], rhs=Hbf[:, gs], start=True, stop=True),
                       r=[b_ct, b_Hbf[g]], w=[pyo])
                    pst = pf[1]
                    op("pe", lambda e: e.matmul(pst.t[:, 0:320], lhsT=bm_c[:, n_s], rhs=xdtd_b[:, gs], start=True, stop=True),
                       r=[b_bm, b_xdtd[g]], w=[pst])
                    pcb = pf[2]
                    op("pe", lambda e: e.matmul(pcb.t[:, 0:128], lhsT=bt_c[:, n_s], rhs=ct_c[:, n_s], start=True, stop=True),
                       r=[b_bt, b_ct], w=[pcb])
                    pyd = pf[3]
                    for jj in range(5):
                        j = g * 5 + jj
                        hs = slice(j * 64, (j + 1) * 64)
                        ri = hc % NR
                        hc += 1
                        Rj, LT, MT = Rjs[ri], LTs[ri], MTs[ri]
                        op("dve", lambda e: e.tensor_scalar(out=Rj, in0=triu.t[:], scalar1=dtA[:, j:j + 1], scalar2=None,
                                                           op0=ALU.mult), r=[triu, sm], w=[b_R[ri]])
                        pbc = pf[4 + (j % 2)]
                        op("pe", lambda e: e.matmul(pbc.t[:, 0:128], lhsT=onesf.t[:], rhs=Rj, start=True, stop=False),
                           r=[onesf, b_R[ri]], w=[pbc])
                        op("pe", lambda e: e.matmul(pbc.t[:, 0:128], lhsT=identf.t[:], rhs=negm, start=False, stop=True),
                           r=[identf, b_cst], w=[pbc])
                        op("act", lambda e: e.activation(out=LT, in_=pbc.t[:, 0:128], func=AF.Exp, bias=nacs[:, j:j + 1], scale=1.0),
                           r=[pbc, sm], w=[b_L[ri]])
                        op("dve", lambda e: e.tensor_tensor(out=MT, in0=LT, in1=pcb.t[:, 0:128], op=ALU.mult), r=[b_L[ri], pcb], w=[b_M[ri]])
                        op("pe", lambda e: e.matmul(pyd.t[:, jj * 64:(jj + 1) * 64], lhsT=MT, rhs=xdt_b[:, hs], start=True, stop=True),
                           r=[b_M[ri], b_xdtb], w=[pyd])
                    op("pool", lambda e: e.tensor_tensor(out=t_f[:, gs], in0=xs_c[:, gs], in1=dskx[:, gs], op=ALU.mult),
                       r=[b_xs, b_cst], w=[b_t[g]])
                    op("dve", lambda e: e.tensor_tensor(out=y_f[:, gs], in0=t_f[:, gs], in1=pyd.t[:, 0:320], op=ALU.add),
                       r=[b_t[g], pyd], w=[b_y[g]])
                    for jj in range(5):
                        j = g * 5 + jj
                        hs = slice(j * 64, (j + 1) * 64)
                        op("dve", lambda e: e.scalar_tensor_tensor(out=y_f[:, hs], in0=pyo.t[:, jj * 64:(jj + 1) * 64],
                                                                   scalar=ea[:, j:j + 1], in1=y_f[:, hs],
                                                                   op0=ALU.mult, op1=ALU.add), r=[pyo, sm, b_y[g]], w=[b_y[g]])
                        op("dve", lambda e: e.scalar_tensor_tensor(out=Hst[:, hs], in0=Hst[:, hs], scalar=cd[:, j:j + 1],
                                                                   in1=pst.t[:, jj * 64:(jj + 1) * 64],
                                                                   op0=ALU.mult, op1=ALU.add), r=[pst, sm, b_H[g]], w=[b_H[g]])
                    op("act", lambda e: e.activation(out=Hbf[:, gs], in_=Hst[:, gs], func=AF.Copy), r=[b_H[g]], w=[b_Hbf[g]])
                    op("dve", lambda e: e.tensor_tensor(out=y_f[:, gs], in0=y_f[:, gs], in1=zs_c[:, gs], op=ALU.mult),
                       r=[b_y[g], b_zs], w=[b_y[g]])
                    op("pool", lambda e: e.memset(gss.t[:, g:g + 1], 0.0), w=[gss])
                    op("act", lambda e: e.activation(out=t_f[:, gs], in_=y_f[:, gs], func=AF.Square, accum_out=gss.t[:, g:g + 1]),
                       r=[b_y[g]], w=[b_t[g], gss])
                    op("dve", lambda e: e.tensor_scalar(out=gss.t[:, g:g + 1], in0=gss.t[:, g:g + 1], scalar1=1.0 / 320, scalar2=EPS,
                                                       op0=ALU.mult, op1=ALU.add), r=[gss], w=[gss])
                    op("act", lambda e: e.activation(out=gss.t[:, g:g + 1], in_=gss.t[:, g:g + 1], func=AF.Sqrt), r=[gss], w=[gss])
                    op("dve", lambda e: e.reciprocal(out=gss.t[:, g:g + 1], in_=gss.t[:, g:g + 1]), r=[gss], w=[gss])
                    op("dve", lambda e: e.scalar_tensor_tensor(out=xdtd_b[:, gs], in0=y_f[:, gs], scalar=gss.t[:, g:g + 1],
                                                               in1=snw[:, gs], op0=ALU.mult, op1=ALU.mult),
                       r=[b_y[g], gss, b_cst], w=[b_xdtd[g]])
                for q4 in range(5):
                    bank = nxt("pb", pb)
                    for kk in range(4):
                        k = q4 * 4 + kk
                        gk = (k * 128) // 320
                        gk2 = (k * 128 + 127) // 320
                        op("pe", lambda e: e.transpose(out=bank.t[:, kk * 128:(kk + 1) * 128], in_=xdtd_b[:, k * 128:(k + 1) * 128],
                                                       identity=ident.t[:]), r=[b_xdtd[gk], b_xdtd[gk2], ident], w=[bank])
                    sg = nxt("stg", stg)
                    op("act", lambda e: e.activation(out=sg.t[:], in_=bank.t[:, 0:512], func=AF.Copy), r=[bank], w=[sg])
                    dma("sp", MIXT.rearrange("(k p) s -> p k s", p=128)[:, q4 * 4:(q4 + 1) * 4, rs],
                        sg.t[:].rearrange("p (k s) -> p k s", s=128), sg, bMIXT)
            join(sb_)

        def tail_stage(ti):
            r0 = ti * 512
            hl = lambda c: Hloc[ti * 4 + c][:, :]
            tmpT = big.t[:, 0:KC * 512].rearrange("p (k n) -> p k n", n=512)
            mv = MIXT.rearrange("(k p) s -> p k s", p=128)
            dma("sp", xnT.t[:], mv[:, :, r0:r0 + 512], bMIXT, xnT)
            dma("sp", tmpT, mv[:, :, SL + r0:SL + r0 + 512], bMIXT, big)
            op("dve", lambda e: e.tensor_scalar(out=xnT.t[:], in0=xnT.t[:], scalar1=sel.t[:, 0:1], scalar2=None, op0=ALU.mult),
               r=[xnT, sel], w=[xnT])
            op("dve", lambda e: e.scalar_tensor_tensor(out=xnT.t[:], in0=tmpT, scalar=sel.t[:, 1:2], in1=xnT.t[:],
                                                       op0=ALU.mult, op1=ALU.add), r=[big, sel, xnT], w=[xnT])
            lin_tm(W_out, KC, 0, D, xnT, evac_to_Fd())
            post_stage(hl, bH, hl, bH, 1, 1.0)
            ffn_stage(1, hl, bH, hl, bH, 2, 2)
            rows_norm_T(hl, bH, 3)
            pT = big.t[:, 0:1024].rearrange("p (k n) -> p k n", n=512)
            for c in range(4):
                dma("sp", ra.t[:, 0:256], p_in[r0 + c * 128:r0 + (c + 1) * 128, :], bp, ra)
                op("dve", lambda e: e.tensor_copy(out=xs.t[:, 0:256], in_=ra.t[:, 0:256]), r=[ra], w=[xs])
                bank = nxt("pb", pb)
                for kk in range(2):
                    op("pe", lambda e: e.transpose(out=bank.t[:, kk * 128:(kk + 1) * 128], in_=xs.t[:, kk * 128:(kk + 1) * 128],
                                                   identity=ident.t[:]), r=[xs, ident], w=[bank])
                for kk in range(2):
                    op("dve", lambda e: e.tensor_copy(out=pT[:, kk, c * 128:(c + 1) * 128], in_=bank.t[:, kk * 128:(kk + 1) * 128]),
                       r=[bank], w=[big])
            ppst = ra.t[:, 0:1024].rearrange("p (t c) -> p t c", c=256)

            for j in range(D // 256):
                cc = j * 256
                banks = [nxt("pf", pf) for _ in range(4)]
                s = load_slab([(W_pp, 0, 2, cc, 256, 0)])
                for tc in range(4):
                    for k in range(2):
                        op("pe", lambda e: e.matmul(banks[tc].t[:, 0:256], lhsT=pT[:, k, tc * 128:(tc + 1) * 128],
                                                    rhs=s.t[:, k, 0:256], start=(k == 0), stop=(k == 1)),
                           r=[s, big], w=[banks[tc]])
                for tc in range(4):
                    op("act", lambda e: e.activation(out=ppst[:, tc, :], in_=banks[tc].t[:, 0:256], func=AF.Copy),
                       r=[banks[tc]], w=[ra])
                banks = [nxt("pf", pf) for _ in range(4)]
                s = load_slab([(W_pg, 0, KC, cc, 256, 0)])
                for tc in range(4):
                    for k in range(KC):
                        op("pe", lambda e: e.matmul(banks[tc].t[:, 0:256], lhsT=xnT.t[:, k, tc * 128:(tc + 1) * 128],
                                                    rhs=s.t[:, k, 0:256], start=(k == 0), stop=(k == KC - 1)),
                           r=[s, xnT], w=[banks[tc]])
                f = nxt("fst", fst)
                for tc in range(4):
                    op("act", lambda e: e.activation(out=f.t[:, tc, :], in_=banks[tc].t[:, 0:256], func=AF.Sigmoid),
                       r=[banks[tc]], w=[f])
                    op("dve", lambda e: e.tensor_tensor(out=f.t[:, tc, :], in0=f.t[:, tc, :], in1=ppst[:, tc, :], op=ALU.mult),
                       r=[f, ra], w=[f])
                dma("sp", Fd.rearrange("(t p) c -> p t c", p=128)[:, :, cc:cc + 256], f.t[:], f, bFd)
            post_stage(hl, bH, lambda c: out[r0 + c * 128:r0 + (c + 1) * 128, :], bout, 3, 1.0)

        groups = [[0, 1], [2, 3], [4, 5], [6, 7]]
        with nc.named_scope("A_ffn1"):
            for ti in range(NTL):
                ffn_stage(0, lambda c: x_in[ti * 512 + c * 128:ti * 512 + (c + 1) * 128, :], bx,
                          lambda c: Hloc[ti * 4 + c][:, :], bH, 0, 0)
        with nc.named_scope("X_cc"):
            for k in range(NCL):
                sch.cc(Hloc_t[k].ap().opt(), Hg_t[k].ap().opt(), bH, bHg, groups)
        with nc.named_scope("A_inproj"):
            for ti in range(NT):
                inproj_stage(ti)
        with nc.named_scope("B_att"):
            attention()
        with nc.named_scope("B_ssd"):
            ssd()
        with nc.named_scope("C_tail"):
            for ti in range(NTL):
                tail_stage(ti)
        sch.finish("sp", [bout])
        sch.finish("act", [bout])
    return nc


def prep_inputs(inp, b, par):
    f = np.float32

    def col(v):
        v = np.asarray(v, f).reshape(-1)
        return np.ascontiguousarray(v.reshape(-1, 128).T)

    def bc(v):
        v = np.asarray(v, f).reshape(1, -1)
        return np.ascontiguousarray(np.broadcast_to(v, (128, v.shape[1])))
    m = {}
    SL = inp["x"].shape[1] // 2
    m["x"] = np.ascontiguousarray(inp["x"][b, par * SL:(par + 1) * SL])
    m["p"] = np.ascontiguousarray(inp["p"][0, b, par * SL:(par + 1) * SL])
    m["sel"] = np.ascontiguousarray(np.broadcast_to(np.eye(2, dtype=f)[par][None, :], (128, 2)))
    for k in ("ffn1_w_gate", "ffn1_w_up", "ffn1_w_down", "ffn2_w_gate", "ffn2_w_up", "ffn2_w_down",
              "w_in", "w_out", "w_ple_gate", "w_ple_proj"):
        m[k] = np.asarray(inp[k][0], f)
    m["gcols"] = np.ascontiguousarray(np.concatenate(
        [col(inp[k][0]) for k in ("ffn1_pre_w", "mix_pre_w", "ffn2_pre_w", "ple_pre_w")], axis=1))
    m["postw"] = np.ascontiguousarray(np.stack(
        [bc(inp[k][0]) for k in ("ffn1_post_w", "mix_post_w", "ffn2_post_w", "ple_post_w")], axis=0))
    cw = np.asarray(inp["conv_w"][0], f)
    m["convw"] = np.ascontiguousarray(cw.T.reshape(36, 128, 4).transpose(1, 0, 2).reshape(128, 144))
    m["convb"] = col(inp["conv_b"][0])
    m["headv"] = np.ascontiguousarray(np.concatenate(
        [bc(inp["dt_bias"][0]), bc(inp["a_log"][0]), bc(inp["d_skip"][0])], axis=1))
    m["dskx"] = bc(np.repeat(np.asarray(inp["d_skip"][0], f), 64))
    m["snw"] = bc(inp["ssd_norm_w"][0])
    m.update(host_consts())
    return m


_NC_CACHE = {}


def kernel(**inputs):
    B, S, _ = inputs["x"].shape
    DFF = inputs["ffn1_w_gate"].shape[-1]
    key = (S, DFF)
    if key not in _NC_CACHE:
        _NC_CACHE[key] = build(S, DFF)
    nc = _NC_CACHE[key]
    n = 8
    in_maps = [prep_inputs(inputs, (c // 2) % B, c % 2) for c in range(n)]
    res = run_bass_kernel_spmd(nc, in_maps, core_ids=list(range(n)))
    outs = [np.concatenate([res.results[2 * b]["out"], res.results[2 * b + 1]["out"]], axis=0) for b in range(B)]
    return np.stack(outs, axis=0).astype(np.float32)
```

```python
import math
from contextlib import ExitStack
import numpy as np
import concourse.bass as bass
import concourse.mybir as mybir
from concourse.bass_utils import run_bass_kernel_spmd

F32 = mybir.dt.float32
BF16 = mybir.dt.bfloat16
AF = mybir.ActivationFunctionType
ALU = mybir.AluOpType

D = 4096
KC = 32
KS = 16
EPS = 1e-6
SSD_W = 2560
NH = 40
NG = 8
ATT_W = 1536
AH = 12
OFF_Z, OFF_X, OFF_DT, OFF_Q, OFF_K, OFF_V = 0, 2560, 7168, 7208, 8744, 10280
EPOCH = 20000
BIG = 1.0e6
import os
DEBUG = int(os.environ.get("KDEBUG", "0"))
QS = 128 ** -0.5


def alibi_slopes(n):
    def pow2(m):
        start = 2.0 ** (-8.0 / m)
        return [start ** (i + 1) for i in range(m)]
    if math.log2(n).is_integer():
        s = pow2(n)
    else:
        c = 2 ** int(math.floor(math.log2(n)))
        s = pow2(c) + pow2(2 * c)[0::2][: n - c]
    return [float(np.float32(v)) for v in s]


class Ev:
    __slots__ = ("sem", "val", "key", "order")

    def __init__(self, sem, val, key, order):
        self.sem, self.val, self.key, self.order = sem, val, key, order


class Buf:
    def __init__(self, name, t=None):
        self.name = name
        self.t = t
        self.writes = {}
        self.reads = {}


class Sched:
    def __init__(self, nc, es):
        self.nc, self.es = nc, es
        self.eng = {"pe": nc.tensor, "act": nc.scalar, "dve": nc.vector, "pool": nc.gpsimd, "sp": nc.sync}
        self.cnt = {k: 0 for k in self.eng}
        self.psems = {k: [] for k in self.eng}
        self.seen = {k: {} for k in self.eng}
        self.dsem = {}
        self.dcnt = {}
        self.nsem = 0

    def _sem(self, name):
        self.nsem += 1
        return self.es.enter_context(self.nc.semaphore(name))

    def _wait(self, e, ev):
        if self.seen[e].get(ev.key, -1) >= ev.order:
            return
        self.eng[e].wait_ge(ev.sem, ev.val)
        self.seen[e][ev.key] = ev.order

    def _deps(self, e, reads, writes):
        for b in reads:
            for ev in b.writes.values():
                if not (e == "pe" and ev.key == "pe"):
                    self._wait(e, ev)
        for b in writes:
            for ev in list(b.writes.values()) + list(b.reads.values()):
                if not (e == "pe" and ev.key == "pe"):
                    self._wait(e, ev)

    def _record(self, ev, reads, writes):
        for b in reads:
            b.reads[ev.key] = ev
        for b in writes:
            b.writes = {ev.key: ev}
            b.reads = {}

    def op(self, e, fn, r=(), w=()):
        self._deps(e, r, w)
        ins = fn(self.eng[e])
        n = self.cnt[e]
        ep = n // EPOCH
        while len(self.psems[e]) <= ep:
            self.psems[e].append(self._sem(f"p_{e}_{len(self.psems[e])}"))
        self.cnt[e] = n + 1
        ins.then_inc(self.psems[e][ep], 1)
        ev = Ev(self.psems[e][ep], n % EPOCH + 1, e, n)
        self._record(ev, r, w)
        return ev

    def dma(self, q, out_ap, in_ap, src, dst, **kw):
        self._deps(q, [src], [dst])
        key = "d:" + src.name + ">" + dst.name
        if key not in self.dsem:
            self.dsem[key] = self._sem("d%d" % len(self.dsem))
            self.dcnt[key] = 0
        self.dcnt[key] += 16
        self.eng[q].dma_start(out=out_ap, in_=in_ap, **kw).then_inc(self.dsem[key], 16)
        ev = Ev(self.dsem[key], self.dcnt[key], key, self.dcnt[key])
        self._record(ev, [src], [dst])
        return ev

    def cc(self, in_ap, out_ap, src, dst, groups):
        self._deps("pool", [src], [dst])
        if "cc" not in self.dsem:
            self.dsem["cc"] = self._sem("ccs")
            self.dcnt["cc"] = 0
        self.dcnt["cc"] += 1
        self.eng["pool"].collective_compute("AllGather", ALU.bypass, replica_groups=groups,
                                            ins=[in_ap], outs=[out_ap]).then_inc(self.dsem["cc"], 1)
        ev = Ev(self.dsem["cc"], self.dcnt["cc"], "cc", self.dcnt["cc"])
        self._record(ev, [src], [dst])
        return ev

    def finish(self, e, bufs):
        for b in bufs:
            for ev in list(b.writes.values()):
                self._wait(e, ev)


def host_consts():
    c = {}
    c["ident"] = np.eye(128, dtype=np.float32)
    kk = np.arange(128)[:, None]
    ll = np.arange(128)[None, :]
    c["triu"] = (kk <= ll).astype(np.float32)
    c["negm"] = np.where(ll < kk, -30000.0, 0.0).astype(np.float32)
    prev = np.where(kk >= ll, 128.0 + ll - kk, BIG)
    cur = np.where(kk <= ll, (ll - kk).astype(np.float64), BIG)
    c["distm"] = np.concatenate([prev, cur], axis=1).astype(np.float32)
    return c


def build(S, DFF):
    NT = S // 512
    NCH = S // 128
    SL = S // 2
    NTL = SL // 512
    NCL = SL // 128
    FC = DFF // 128
    nc = bass.Bass("TRN2", target_bir_lowering=False)

    def din(name, shape):
        return nc.dram_tensor(name, list(shape), F32, kind="ExternalInput").ap()

    x_in = din("x", [SL, D])
    p_in = din("p", [SL, 256])
    sel_in = din("sel", [128, 2])
    Wg = [din("ffn1_w_gate", [D, DFF]), din("ffn2_w_gate", [D, DFF])]
    Wu = [din("ffn1_w_up", [D, DFF]), din("ffn2_w_up", [D, DFF])]
    Wd = [din("ffn1_w_down", [DFF, D]), din("ffn2_w_down", [DFF, D])]
    W_in = din("w_in", [D, 11816])
    W_out = din("w_out", [D, D])
    W_pg = din("w_ple_gate", [D, D])
    W_pp = din("w_ple_proj", [256, D])
    gcols_in = din("gcols", [128, 4 * KC])
    postw_in = din("postw", [4, 128, D])
    convw_in = din("convw", [128, 36 * 4])
    convb_in = din("convb", [128, 36])
    hv_in = din("headv", [128, 3 * NH])
    dskx_in = din("dskx", [128, SSD_W])
    snw_in = din("snw", [128, SSD_W])
    ident_in = din("ident", [128, 128])
    triu_in = din("triu", [128, 128])
    negm_in = din("negm", [128, 128])
    distm_in = din("distm", [128, 256])
    out = nc.dram_tensor("out", [SL, D], F32, kind="ExternalOutput").ap()
    def dscr(name, shape, dt):
        return nc.dram_tensor(name, list(shape), dt).ap()

    Hloc_t = [nc.dram_tensor("Hloc%d" % k, [128, D], F32) for k in range(NCL)]
    Hg_t = [nc.dram_tensor("Hg%d" % k, [256, D], F32) for k in range(NCL)]
    Hloc = [t.ap() for t in Hloc_t]
    Hg = [t.ap() for t in Hg_t]
    Fd = dscr("Fd", [512, D], F32)
    ZS = dscr("ZS", [S, SSD_W], BF16)
    XS = dscr("XS", [S, SSD_W], BF16)
    BM = dscr("BM", [S, 1024], BF16)
    BT = dscr("BT", [1024, S], BF16)
    CT = dscr("CT", [1024, S], BF16)
    QT = dscr("QT", [ATT_W, S], BF16)
    KT = dscr("KT", [ATT_W, S], BF16)
    Vd = dscr("Vd", [S, ATT_W], BF16)
    MIXT = dscr("MIXT", [D, S], BF16)

    es = ExitStack()
    with es:
        sch = Sched(nc, es)
        op, dma = sch.op, sch.dma

        def sb(name, shape, dt):
            return Buf(name, es.enter_context(nc.sbuf_tensor("s_" + name, list(shape), dt)))

        def ps(name, shape, dt):
            return Buf(name, es.enter_context(nc.psum_tensor(name, list(shape), dt)))

        bx, bp, bW = Buf("x"), Buf("p"), Buf("W")
        bcst = Buf("cst")
        bH, bHg, bFd, bZS, bXS, bBM, bBT, bCT = Buf("H"), Buf("Hg"), Buf("Fd"), Buf("ZS"), Buf("XS"), Buf("BM"), Buf("BT"), Buf("CT")
        bQT, bKT, bV, bMIXT, bout = Buf("QT"), Buf("KT"), Buf("V"), Buf("MIXT"), Buf("out")

        xnT = sb("xnT", [128, KC, 512], BF16)
        big = sb("big", [128, max(FC * 512, 43008)], BF16)
        slabs = [sb("slab%d" % i, [128, KS, 512], BF16) for i in range(2)]
        ra = sb("ra", [128, D], F32)
        xs = sb("xs", [128, D], BF16)
        fst = [sb("fst%d" % i, [128, 4, 512], F32) for i in range(1)]
        sil = [sb("sil%d" % i, [128, 512], BF16) for i in range(4)]
        stg = [sb("stg%d" % i, [128, 512], BF16) for i in range(2)]
        gcols = sb("gcols", [128, 4 * KC], F32)
        convw = sb("convw", [128, 36 * 4], F32)
        convb = sb("convb", [128, 36], F32)
        halo = sb("halo", [128, 36, 4], F32)
        headv = sb("headv", [128, 3 * NH], F32)
        ident = sb("ident", [128, 128], BF16)
        identf = sb("identf", [128, 128], F32)
        triu = sb("triu", [128, 128], F32)
        onesf = sb("onesf", [128, 128], F32)
        onesb = sb("onesb", [128, 128], BF16)
        distm = sb("distm", [128, 256], F32)
        dt_sb = sb("dt_sb", [128, NCH, NH], F32)
        ss = sb("ss", [128, 1], F32)
        rstd = sb("rstd", [128, 1], F32)
        cst = sb("cstage", [128, 516], F32)
        acc = sb("cacc", [128, 512], F32)
        sm = sb("small", [128, 8 * NH], F32)
        gss = sb("gss", [128, 8], F32)
        pf = [ps("pf%d" % i, [128, 512], F32) for i in range(6)]
        pb = [ps("pb%d" % i, [128, 1024], BF16) for i in range(2)]
        st = {"pf": 0, "pb": 0, "slab": 0, "fst": 0, "sil": 0, "stg": 0}

        def nxt(kind, lst):
            i = st[kind]
            st[kind] = i + 1
            return lst[i % len(lst)]

        def ld(buf, ap_in, tmp=None):
            dma("sp", buf.t[:], ap_in, bcst, buf)

        ld(gcols, gcols_in[:, :])
        ld(convw, convw_in[:, :])
        ld(convb, convb_in[:, :])
        ld(headv, hv_in[:, :])
        ld(identf, ident_in[:, :])
        ld(triu, triu_in[:, :])
        ld(distm, distm_in[:, :])
        sel = sb("sel", [128, 2], F32)
        ld(sel, sel_in[:, :])
        op("dve", lambda e: e.tensor_copy(out=ident.t[:], in_=identf.t[:]), r=[identf], w=[ident])
        op("pool", lambda e: e.memset(onesf.t[:], 1.0), w=[onesf])
        op("pool", lambda e: e.memset(onesb.t[:], 1.0), w=[onesb])
        op("pool", lambda e: e.memset(halo.t[:], 0.0), w=[halo])

        def load_slab(parts):
            s = nxt("slab", slabs)
            for (W, k0, nk, c0, ncols, d0) in parts:
                wv = W.rearrange("(kc p) n -> p kc n", p=128)
                dma("pool", s.t[:, 0:nk, d0:d0 + ncols], wv[:, k0:k0 + nk, c0:c0 + ncols], bW, s)
            return s

        def rmsnorm_stats(src):
            n = D
            op("pool", lambda e: e.memset(ss.t[:], 0.0), w=[ss])
            op("act", lambda e: e.activation(out=xs.t[:], in_=src.t[:], func=AF.Square, accum_out=ss.t[:, 0:1]),
               r=[src], w=[xs, ss])
            op("dve", lambda e: e.tensor_scalar(out=rstd.t[:], in0=ss.t[:], scalar1=1.0 / n, scalar2=EPS,
                                               op0=ALU.mult, op1=ALU.add), r=[ss], w=[rstd])
            op("act", lambda e: e.activation(out=rstd.t[:], in_=rstd.t[:], func=AF.Sqrt), r=[rstd], w=[rstd])
            op("dve", lambda e: e.reciprocal(out=rstd.t[:], in_=rstd.t[:]), r=[rstd], w=[rstd])

        def rows_norm_T(getsrc, bsrc, gi):
            for c in range(4):
                dma("sp", ra.t[:], getsrc(c), bsrc, ra)
                rmsnorm_stats(ra)
                op("act", lambda e: e.activation(out=xs.t[:], in_=ra.t[:], func=AF.Copy, scale=rstd.t[:, 0:1]),
                   r=[ra, rstd], w=[xs])
                for kg in range(4):
                    bank = nxt("pb", pb)
                    for kk in range(8):
                        k = kg * 8 + kk
                        op("pe", lambda e: e.transpose(out=bank.t[:, kk * 128:(kk + 1) * 128],
                                                       in_=xs.t[:, k * 128:(k + 1) * 128], identity=ident.t[:]),
                           r=[xs, ident], w=[bank])
                    for kk in range(8):
                        k = kg * 8 + kk
                        g = gcols.t[:, gi * KC + k:gi * KC + k + 1]
                        o = xnT.t[:, k, c * 128:(c + 1) * 128]
                        i_ = bank.t[:, kk * 128:(kk + 1) * 128]
                        if kk % 2:
                            op("act", lambda e: e.activation(out=o, in_=i_, func=AF.Copy, scale=g),
                               r=[bank, gcols], w=[xnT])
                        else:
                            op("dve", lambda e: e.tensor_scalar(out=o, in0=i_, scalar1=g, scalar2=None, op0=ALU.mult),
                               r=[bank, gcols], w=[xnT])

        def lin_fm(W, c0, nchunks, evac):
            m = 0
            while m < nchunks:
                nm = min(4, nchunks - m)
                banks = [nxt("pf", pf) for _ in range(nm)]
                for k0 in range(0, KC, KS):
                    s_ = load_slab([(W, k0, KS, c0 + m * 128, nm * 128, 0)])
                    for mm in range(nm):
                        for kk in range(KS):
                            k = k0 + kk
                            op("pe", lambda e: e.matmul(banks[mm].t[:], lhsT=s_.t[:, kk, mm * 128:(mm + 1) * 128],
                                                        rhs=xnT.t[:, k, :], start=(k == 0), stop=(k == KC - 1)),
                               r=[s_, xnT], w=[banks[mm]])
                for mm in range(nm):
                    evac(m + mm, banks[mm])
                m += nm

        def lin_tm(W, nk_tot, c0, ncols_tot, lhs, evac, blk=512):
            j = 0
            while j * blk < ncols_tot:
                cc = c0 + j * blk
                ncols = min(blk, ncols_tot - j * blk)
                banks = [nxt("pf", pf) for _ in range(4)]
                for k0 in range(0, nk_tot, KS):
                    nk = min(KS, nk_tot - k0)
                    s = load_slab([(W, k0, nk, cc, ncols, 0)])
                    for tc in range(4):
                        for kk in range(nk):
                            k = k0 + kk
                            op("pe", lambda e: e.matmul(banks[tc].t[:, 0:ncols],
                                                        lhsT=lhs.t[:, k, tc * 128:(tc + 1) * 128],
                                                        rhs=s.t[:, kk, 0:ncols],
                                                        start=(k == 0), stop=(k == nk_tot - 1)),
                               r=[s, lhs], w=[banks[tc]])
                evac(j, cc - c0, ncols, banks)
                j += 1

        def evac_to_Fd(func=None):
            def ev(j, coff, ncols, banks):
                f = nxt("fst", fst)
                for tc in range(4):
                    if tc % 2:
                        op("act", lambda e: e.activation(out=f.t[:, tc, 0:ncols], in_=banks[tc].t[:, 0:ncols],
                                                         func=AF.Copy), r=[banks[tc]], w=[f])
                    else:
                        op("dve", lambda e: e.tensor_copy(out=f.t[:, tc, 0:ncols], in_=banks[tc].t[:, 0:ncols]),
                           r=[banks[tc]], w=[f])
                dma("sp", Fd.rearrange("(t p) c -> p t c", p=128)[:, :, coff:coff + ncols], f.t[:, :, 0:ncols], f, bFd)
            return ev

        def post_stage(getsrc, bsrc, getdst, bdst, wi, coef):
            wbv = big.t[:, 2048:2048 + 2 * D].bitcast(F32)
            rbv = big.t[:, 2048 + 2 * D:2048 + 4 * D].bitcast(F32)
            dma("sp", wbv, postw_in[wi, :, :], bcst, big)
            for c in range(4):
                dma("sp", ra.t[:], Fd[c * 128:(c + 1) * 128, :], bFd, ra)
                dma("sp", rbv, getsrc(c), bsrc, big)
                rmsnorm_stats(ra)
                op("dve", lambda e: e.scalar_tensor_tensor(out=ra.t[:], in0=ra.t[:], scalar=rstd.t[:, 0:1], in1=wbv,
                                                           op0=ALU.mult, op1=ALU.mult), r=[ra, rstd, big], w=[ra])
                op("dve", lambda e: e.scalar_tensor_tensor(out=rbv, in0=ra.t[:], scalar=float(coef), in1=rbv,
                                                           op0=ALU.mult, op1=ALU.add), r=[ra, big], w=[big])
                dma("sp", getdst(c), rbv, big, bdst)

        def ffn_stage(li, getsrc, bsrc, getdst, bdst, gi, wi):
            rows_norm_T(getsrc, bsrc, gi)
            gT = big
            fc = 0
            while fc < FC:
                nm = min(4, FC - fc)
                sls = []
                for (Wm, is_up) in ((Wg[li], False), (Wu[li], True)):
                    banks = [nxt("pf", pf) for _ in range(nm)]
                    for k0 in range(0, KC, KS):
                        s_ = load_slab([(Wm, k0, KS, fc * 128, nm * 128, 0)])
                        for mm in range(nm):
                            for kk in range(KS):
                                k = k0 + kk
                                op("pe", lambda e: e.matmul(banks[mm].t[:], lhsT=s_.t[:, kk, mm * 128:(mm + 1) * 128],
                                                            rhs=xnT.t[:, k, :], start=(k == 0), stop=(k == KC - 1)),
                                   r=[s_, xnT], w=[banks[mm]])
                    for mm in range(nm):
                        if not is_up:
                            sl = nxt("sil", sil)
                            sls.append(sl)
                            op("act", lambda e: e.activation(out=sl.t[:], in_=banks[mm].t[:], func=AF.Silu), r=[banks[mm]], w=[sl])
                        else:
                            sl = sls[mm]
                            op("dve", lambda e: e.tensor_tensor(out=gT.t[:, (fc + mm) * 512:(fc + mm + 1) * 512], in0=sl.t[:],
                                                                in1=banks[mm].t[:], op=ALU.mult), r=[sl, banks[mm]], w=[gT])
                fc += nm
            gT3 = gT.t[:, 0:FC * 512].rearrange("p (k n) -> p k n", n=512)
            lin_tm_view(Wd[li], FC, 0, D, gT, gT3, evac_to_Fd())
            post_stage(getsrc, bsrc, getdst, bdst, wi, 0.5)

        def lin_tm_view(W, nk_tot, c0, ncols_tot, track, view, evac, blk=512):
            j = 0
            while j * blk < ncols_tot:
                cc = c0 + j * blk
                ncols = min(blk, ncols_tot - j * blk)
                banks = [nxt("pf", pf) for _ in range(4)]
                for k0 in range(0, nk_tot, KS):
                    nk = min(KS, nk_tot - k0)
                    s = load_slab([(W, k0, nk, cc, ncols, 0)])
                    for tc in range(4):
                        for kk in range(nk):
                            k = k0 + kk
                            op("pe", lambda e: e.matmul(banks[tc].t[:, 0:ncols],
                                                        lhsT=view[:, k, tc * 128:(tc + 1) * 128],
                                                        rhs=s.t[:, kk, 0:ncols],
                                                        start=(k == 0), stop=(k == nk_tot - 1)),
                               r=[s, track], w=[banks[tc]])
                evac(j, cc - c0, ncols, banks)
                j += 1

        def inproj_stage(ti):
            r0 = ti * 512
            rows_norm_T(lambda c: Hg[(ti * 4 + c) % NCL][((ti * 4 + c) // NCL) * 128:((ti * 4 + c) // NCL + 1) * 128, :], bHg, 1)

            def ev_z(j, coff, ncols, banks):
                for tc in range(4):
                    sg = nxt("stg", stg)
                    op("act", lambda e: e.activation(out=sg.t[:, 0:ncols], in_=banks[tc].t[:, 0:ncols], func=AF.Silu),
                       r=[banks[tc]], w=[sg])
                    dma("sp", ZS[r0 + tc * 128:r0 + (tc + 1) * 128, coff:coff + ncols], sg.t[:, 0:ncols], sg, bZS)
            lin_tm(W_in, KC, OFF_Z, SSD_W, xnT, ev_z)

            def ev_v(j, coff, ncols, banks):
                for tc in range(4):
                    sg = nxt("stg", stg)
                    op("dve", lambda e: e.tensor_copy(out=sg.t[:, 0:ncols], in_=banks[tc].t[:, 0:ncols]),
                       r=[banks[tc]], w=[sg])
                    dma("sp", Vd[r0 + tc * 128:r0 + (tc + 1) * 128, coff:coff + ncols], sg.t[:, 0:ncols], sg, bV)
            lin_tm(W_in, KC, OFF_V, ATT_W, xnT, ev_v)

            def ev_dt(j, coff, ncols, banks):
                for tc in range(4):
                    ch = ti * 4 + tc
                    o = dt_sb.t[:, ch, :]
                    op("dve", lambda e: e.tensor_tensor(out=o, in0=banks[tc].t[:, 0:NH], in1=headv.t[:, 0:NH], op=ALU.add),
                       r=[banks[tc], headv], w=[dt_sb])
                    op("act", lambda e: e.activation(out=o, in_=o, func=AF.Exp), r=[dt_sb], w=[dt_sb])
                    op("act", lambda e: e.activation(out=o, in_=o, func=AF.Ln, bias=1.0, scale=1.0), r=[dt_sb], w=[dt_sb])
            lin_tm(W_in, KC, OFF_DT, NH, xnT, ev_dt)

            def ev_xbc(m, bank):
                op("dve", lambda e: e.tensor_copy(out=cst.t[:, 0:3], in_=halo.t[:, m, 0:3]), r=[halo], w=[cst])
                op("act", lambda e: e.activation(out=cst.t[:, 3:515], in_=bank.t[:], func=AF.Copy), r=[bank], w=[cst])
                op("dve", lambda e: e.tensor_scalar(out=acc.t[:], in0=cst.t[:, 0:512], scalar1=convw.t[:, m * 4:m * 4 + 1],
                                                   scalar2=convb.t[:, m:m + 1], op0=ALU.mult, op1=ALU.add),
                   r=[cst, convw, convb], w=[acc])
                for kq in range(1, 4):
                    op("dve", lambda e: e.scalar_tensor_tensor(out=acc.t[:], in0=cst.t[:, kq:kq + 512],
                                                               scalar=convw.t[:, m * 4 + kq:m * 4 + kq + 1], in1=acc.t[:],
                                                               op0=ALU.mult, op1=ALU.add), r=[cst, convw, acc], w=[acc])
                op("dve", lambda e: e.tensor_copy(out=halo.t[:, m, 0:3], in_=cst.t[:, 512:515]), r=[cst], w=[halo])
                sg = nxt("stg", stg)
                op("act", lambda e: e.activation(out=sg.t[:], in_=acc.t[:], func=AF.Silu), r=[acc], w=[sg])
                if m >= 28:
                    dma("sp", CT[(m - 28) * 128:(m - 27) * 128, r0:r0 + 512], sg.t[:], sg, bCT)
                    return
                if m >= 20:
                    dma("sp", BT[(m - 20) * 128:(m - 19) * 128, r0:r0 + 512], sg.t[:], sg, bBT)
                bank2 = nxt("pb", pb)
                for tc in range(4):
                    op("pe", lambda e: e.transpose(out=bank2.t[:, tc * 128:(tc + 1) * 128],
                                                   in_=sg.t[:, tc * 128:(tc + 1) * 128], identity=ident.t[:]),
                       r=[sg, ident], w=[bank2])
                sg2 = nxt("stg", stg)
                op("dve", lambda e: e.tensor_copy(out=sg2.t[:], in_=bank2.t[:, 0:512]), r=[bank2], w=[sg2])
                if m >= 20:
                    dst, bd, cc = BM, bBM, (m - 20) * 128
                else:
                    dst, bd, cc = XS, bXS, m * 128
                dma("sp", dst.rearrange("(t p) c -> p t c", p=128)[:, ti * 4:(ti + 1) * 4, cc:cc + 128],
                    sg2.t[:].rearrange("p (t c) -> p t c", c=128), sg2, bd)
            lin_fm(W_in, OFF_X, 36, ev_xbc)

            def mk_ev(dst, bd):
                def ev(m, bank):
                    sg = nxt("stg", stg)
                    if m % 2:
                        op("act", lambda e: e.activation(out=sg.t[:], in_=bank.t[:], func=AF.Copy), r=[bank], w=[sg])
                    else:
                        op("dve", lambda e: e.tensor_copy(out=sg.t[:], in_=bank.t[:]), r=[bank], w=[sg])
                    dma("sp", dst[m * 128:(m + 1) * 128, r0:r0 + 512], sg.t[:], sg, bd)
                return ev
            lin_fm(W_in, OFF_Q, AH, mk_ev(QT, bQT))
            lin_fm(W_in, OFF_K, AH, mk_ev(KT, bKT))

        def fork(names):
            subs = {}
            for n in names:
                bb = Buf(n)
                bb.writes = dict(big.writes)
                bb.reads = dict(big.reads)
                subs[n] = bb
            return subs

        def join(subs):
            m = {}
            for bb in [big] + list(subs.values()):
                for ev in list(bb.writes.values()) + list(bb.reads.values()):
                    if ev.key not in m or m[ev.key].order < ev.order:
                        m[ev.key] = ev
            big.writes = m
            big.reads = {}

        def attention():
            slopes = alibi_slopes(AH)
            bg = big.t
            QTh = bg[:, 0:S]
            KTh = bg[:, S:2 * S]
            Vhs = [bg[:, 2 * S:3 * S], bg[:, 8 * S:9 * S]]
            attT = bg[:, 3 * S:4 * S]
            numT = bg[:, 4 * S:6 * S].bitcast(F32)
            denT = bg[:, 6 * S:8 * S].bitcast(F32)
            sb_ = fork(["a_qk", "a_v0", "a_v1", "a_num", "a_att"])
            bqk, bnum, batt = sb_["a_qk"], sb_["a_num"], sb_["a_att"]
            bvs = [sb_["a_v0"], sb_["a_v1"]]
            vi = 0
            for h in range(AH):
                dma("sp", QTh, QT[h * 128:(h + 1) * 128, :], bQT, bqk)
                dma("sp", KTh, KT[h * 128:(h + 1) * 128, :], bKT, bqk)
                for bi, d in enumerate((1, 4, 16)):
                    nb = S // (128 * d)
                    cp = -slopes[h] * d / QS
                    Vh, bv = Vhs[vi % 2], bvs[vi % 2]
                    vi += 1
                    with nc.allow_non_contiguous_dma(reason="dilated V gather"):
                        for r in range(d):
                            dma("sp", Vh.rearrange("p (r b f) -> p r b f", r=d, b=nb)[:, r, :, :],
                                Vd.rearrange("(b i r) f -> i r b f", i=128, r=d)[:, r, :, h * 128:(h + 1) * 128], bV, bv)
                    V4 = Vh.rearrange("p (r b f) -> p r b f", r=d, b=nb)
                    for r in range(d):
                        for b in range(nb):
                            def tok(bb):
                                s0 = bb * 128 * d + r
                                return slice(s0, s0 + 127 * d + 1, d)
                            qs = tok(b)
                            sbank = nxt("pf", pf)
                            if b > 0:
                                op("pe", lambda e: e.matmul(sbank.t[:, 0:128], lhsT=KTh[:, tok(b - 1)], rhs=QTh[:, qs],
                                                            start=True, stop=True), r=[bqk], w=[sbank])
                            op("pe", lambda e: e.matmul(sbank.t[:, 128:256], lhsT=KTh[:, qs], rhs=QTh[:, qs],
                                                        start=True, stop=True), r=[bqk], w=[sbank])
                            lo = 0 if b > 0 else 128
                            sl = nxt("sil", sil)
                            slv = sl.t[:, :].bitcast(F32)
                            op("dve", lambda e: e.scalar_tensor_tensor(out=slv[:, lo:256], in0=distm.t[:, lo:256],
                                                                       scalar=float(cp), in1=sbank.t[:, lo:256],
                                                                       op0=ALU.mult, op1=ALU.add),
                               r=[distm, sbank], w=[sl])
                            pT = nxt("stg", stg)
                            op("act", lambda e: e.activation(out=pT.t[:, lo:256], in_=slv[:, lo:256], func=AF.Exp,
                                                             scale=float(QS)), r=[sl], w=[pT])
                            nbank = nxt("pf", pf)
                            if b > 0:
                                op("pe", lambda e: e.matmul(nbank.t[:, 0:128], lhsT=V4[:, r, b - 1, :], rhs=pT.t[:, 0:128],
                                                            start=True, stop=False), r=[bv, pT], w=[nbank])
                            op("pe", lambda e: e.matmul(nbank.t[:, 0:128], lhsT=V4[:, r, b, :], rhs=pT.t[:, 128:256],
                                                        start=(b == 0), stop=True), r=[bv, pT], w=[nbank])
                            if b > 0:
                                op("pe", lambda e: e.matmul(nbank.t[:, 128:256], lhsT=onesb.t[:], rhs=pT.t[:, 0:128],
                                                            start=True, stop=False), r=[onesb, pT], w=[nbank])
                            op("pe", lambda e: e.matmul(nbank.t[:, 128:256], lhsT=onesb.t[:], rhs=pT.t[:, 128:256],
                                                        start=(b == 0), stop=True), r=[onesb, pT], w=[nbank])
                            if bi == 0:
                                op("dve", lambda e: e.tensor_copy(out=numT[:, qs], in_=nbank.t[:, 0:128]), r=[nbank], w=[bnum])
                                op("act", lambda e: e.activation(out=denT[:, qs], in_=nbank.t[:, 128:256], func=AF.Copy),
                                   r=[nbank], w=[bnum])
                            else:
                                op("dve", lambda e: e.tensor_tensor(out=numT[:, qs], in0=numT[:, qs], in1=nbank.t[:, 0:128],
                                                                    op=ALU.add), r=[nbank, bnum], w=[bnum])
                                op("dve", lambda e: e.tensor_tensor(out=denT[:, qs], in0=denT[:, qs], in1=nbank.t[:, 128:256],
                                                                    op=ALU.add), r=[nbank, bnum], w=[bnum])
                op("dve", lambda e: e.reciprocal(out=denT, in_=denT), r=[bnum], w=[bnum])
                op("dve", lambda e: e.tensor_tensor(out=attT, in0=numT, in1=denT, op=ALU.mult), r=[bnum], w=[batt])
                dma("sp", MIXT[SSD_W + h * 128:SSD_W + (h + 1) * 128, :], attT, batt, bMIXT)
            join(sb_)

        def ssd():
            bg = big.t
            o = 0

            def carve(n, dt=BF16):
                nonlocal o
                ne = n if dt == BF16 else 2 * n
                v = bg[:, o:o + ne]
                o += ne
                return v if dt == BF16 else v.bitcast(F32)
            xs_c = carve(SSD_W)
            zs_c = carve(SSD_W)
            bm_c = carve(1024)
            bt_c = carve(1024)
            ct_c = carve(1024)
            xdt_b = carve(SSD_W)
            xdtd_b = carve(SSD_W)
            Hbf = carve(SSD_W)
            Hst = carve(SSD_W, F32)
            y_f = carve(SSD_W, F32)
            t_f = carve(SSD_W, F32)
            negm = carve(128, F32)
            snw = carve(SSD_W, F32)
            dskx = carve(SSD_W, F32)
            NR = 2
            Rjs = [carve(128, F32) for _ in range(NR)]
            LTs = [carve(128, F32) for _ in range(NR)]
            MTs = [carve(128) for _ in range(NR)]
            names = ["s_xs", "s_zs", "s_bm", "s_bt", "s_ct", "s_xdtb", "s_cst"]
            names += ["s_xdtd%d" % g for g in range(NG)] + ["s_H%d" % g for g in range(NG)] + ["s_Hbf%d" % g for g in range(NG)]
            names += ["s_y%d" % g for g in range(NG)] + ["s_t%d" % g for g in range(NG)]
            names += ["s_R%d" % i for i in range(NR)] + ["s_L%d" % i for i in range(NR)] + ["s_M%d" % i for i in range(NR)]
            sb_ = fork(names)
            b_xs, b_zs, b_bm, b_bt, b_ct, b_xdtb, b_cst = [sb_[n] for n in names[:7]]
            b_xdtd = [sb_["s_xdtd%d" % g] for g in range(NG)]
            b_H = [sb_["s_H%d" % g] for g in range(NG)]
            b_Hbf = [sb_["s_Hbf%d" % g] for g in range(NG)]
            b_y = [sb_["s_y%d" % g] for g in range(NG)]
            b_t = [sb_["s_t%d" % g] for g in range(NG)]
            b_R = [sb_["s_R%d" % i] for i in range(NR)]
            b_L = [sb_["s_L%d" % i] for i in range(NR)]
            b_M = [sb_["s_M%d" % i] for i in range(NR)]
            dma("sp", negm, negm_in[:, :], bcst, b_cst)
            dma("sp", snw, snw_in[:, :], bcst, b_cst)
            dma("sp", dskx, dskx_in[:, :], bcst, b_cst)
            smt = sm.t
            dtA = smt[:, 0:NH]
            acs = smt[:, NH:2 * NH]
            nacs = smt[:, 2 * NH:3 * NH]
            ea = smt[:, 3 * NH:4 * NH]
            cd = smt[:, 4 * NH:5 * NH]
            dsc = smt[:, 5 * NH:6 * NH]
            a_b = smt[:, 6 * NH:7 * NH]
            dtd = smt[:, 7 * NH:8 * NH]
            op("act", lambda e: e.activation(out=a_b, in_=headv.t[:, NH:2 * NH], func=AF.Exp), r=[headv], w=[sm])
            op("dve", lambda e: e.tensor_scalar(out=a_b, in0=a_b, scalar1=-1.0, scalar2=None, op0=ALU.mult), r=[sm], w=[sm])
            for g in range(NG):
                gs = slice(g * 320, (g + 1) * 320)
                op("pool", lambda e: e.memset(Hst[:, gs], 0.0), w=[b_H[g]])
                op("pool", lambda e: e.memset(Hbf[:, gs], 0.0), w=[b_Hbf[g]])
            hc = 0
            for c in range(NCH):
                rs = slice(c * 128, (c + 1) * 128)
                dma("sp", xs_c, XS[rs, :], bXS, b_xs)
                dma("sp", zs_c, ZS[rs, :], bZS, b_zs)
                dma("sp", bm_c, BM[rs, :], bBM, b_bm)
                dma("sp", bt_c.rearrange("p (g s) -> p g s", g=NG), BT.rearrange("(g n) s -> n g s", n=128)[:, :, rs], bBT, b_bt)
                dma("sp", ct_c.rearrange("p (g s) -> p g s", g=NG), CT.rearrange("(g n) s -> n g s", n=128)[:, :, rs], bCT, b_ct)
                dtc = dt_sb.t[:, c, :]
                op("dve", lambda e: e.tensor_tensor(out=dtA, in0=dtc, in1=a_b, op=ALU.mult), r=[dt_sb, sm], w=[sm])
                pa = pf[4]
                op("pe", lambda e: e.matmul(pa.t[:, 0:NH], lhsT=triu.t[:], rhs=dtA, start=True, stop=True), r=[triu, sm], w=[pa])
                op("pe", lambda e: e.matmul(pa.t[:, 64:64 + NH], lhsT=onesf.t[:], rhs=dtA, start=True, stop=True),
                   r=[onesf, sm], w=[pa])
                op("dve", lambda e: e.tensor_copy(out=acs, in_=pa.t[:, 0:NH]), r=[pa], w=[sm])
                op("dve", lambda e: e.tensor_scalar(out=nacs, in0=pa.t[:, 0:NH], scalar1=-1.0, scalar2=None, op0=ALU.mult),
                   r=[pa], w=[sm])
                op("act", lambda e: e.activation(out=ea, in_=pa.t[:, 0:NH], func=AF.Exp), r=[pa], w=[sm])
                op("act", lambda e: e.activation(out=cd, in_=pa.t[:, 64:64 + NH], func=AF.Exp), r=[pa], w=[sm])
                op("dve", lambda e: e.tensor_tensor(out=dsc, in0=pa.t[:, 64:64 + NH], in1=acs, op=ALU.subtract), r=[pa, sm], w=[sm])
                op("act", lambda e: e.activation(out=dsc, in_=dsc, func=AF.Exp), r=[sm], w=[sm])
                op("dve", lambda e: e.tensor_tensor(out=dtd, in0=dsc, in1=dtc, op=ALU.mult), r=[sm, dt_sb], w=[sm])
                for j in range(NH):
                    hs = slice(j * 64, (j + 1) * 64)
                    g = j // 5
                    op("dve", lambda e: e.tensor_scalar(out=xdt_b[:, hs], in0=xs_c[:, hs], scalar1=dtc[:, j:j + 1], scalar2=None,
                                                       op0=ALU.mult), r=[b_xs, dt_sb], w=[b_xdtb])
                    op("dve", lambda e: e.tensor_scalar(out=xdtd_b[:, hs], in0=xs_c[:, hs], scalar1=dtd[:, j:j + 1], scalar2=None,
                                                       op0=ALU.mult), r=[b_xs, sm], w=[b_xdtd[g]])
                for g in range(NG):
                    gs = slice(g * 320, (g + 1) * 320)
                    n_s = slice(g * 128, (g + 1) * 128)
                    pyo = pf[0]
                    op("pe", lambda e: e.matmul(pyo.t[:, 0:320], lhsT=ct_c[:, n_s], rhs=Hbf[:, gs], start=True, stop=True),
                       r=[b_ct, b_Hbf[g]], w=[pyo])
                    pst = pf[1]
                    op("pe", lambda e: e.matmul(pst.t[:, 0:320], lhsT=bm_c[:, n_s], rhs=xdtd_b[:, gs], start=True, stop=True),
                       r=[b_bm, b_xdtd[g]], w=[pst])
                    pcb = pf[2]
                    op("pe", lambda e: e.matmul(pcb.t[:, 0:128], lhsT=bt_c[:, n_s], rhs=ct_c[:, n_s], start=True, stop=True),
                       r=[b_bt, b_ct], w=[pcb])
                    pyd = pf[3]
                    for jj in range(5):
                        j = g * 5 + jj
                        hs = slice(j * 64, (j + 1) * 64)
                        ri = hc % NR
                        hc += 1
                        Rj, LT, MT = Rjs[ri], LTs[ri], MTs[ri]
                        op("dve", lambda e: e.tensor_scalar(out=Rj, in0=triu.t[:], scalar1=dtA[:, j:j + 1], scalar2=None,
                                                           op0=ALU.mult), r=[triu, sm], w=[b_R[ri]])
                        pbc = pf[4 + (j % 2)]
                        op("pe", lambda e: e.matmul(pbc.t[:, 0:128], lhsT=onesf.t[:], rhs=Rj, start=True, stop=False),
                           r=[onesf, b_R[ri]], w=[pbc])
                        op("pe", lambda e: e.matmul(pbc.t[:, 0:128], lhsT=identf.t[:], rhs=negm, start=False, stop=True),
                           r=[identf, b_cst], w=[pbc])
                        op("act", lambda e: e.activation(out=LT, in_=pbc.t[:, 0:128], func=AF.Exp, bias=nacs[:, j:j + 1], scale=1.0),
                           r=[pbc, sm], w=[b_L[ri]])
                        op("dve", lambda e: e.tensor_tensor(out=MT, in0=LT, in1=pcb.t[:, 0:128], op=ALU.mult), r=[b_L[ri], pcb], w=[b_M[ri]])
                        op("pe", lambda e: e.matmul(pyd.t[:, jj * 64:(jj + 1) * 64], lhsT=MT, rhs=xdt_b[:, hs], start=True, stop=True),
                           r=[b_M[ri], b_xdtb], w=[pyd])
                    op("pool", lambda e: e.tensor_tensor(out=t_f[:, gs], in0=xs_c[:, gs], in1=dskx[:, gs], op=ALU.mult),
                       r=[b_xs, b_cst], w=[b_t[g]])
                    op("dve", lambda e: e.tensor_tensor(out=y_f[:, gs], in0=t_f[:, gs], in1=pyd.t[:, 0:320], op=ALU.add),
                       r=[b_t[g], pyd], w=[b_y[g]])
                    for jj in range(5):
                        j = g * 5 + jj
                        hs = slice(j * 64, (j + 1) * 64)
                        op("dve", lambda e: e.scalar_tensor_tensor(out=y_f[:, hs], in0=pyo.t[:, jj * 64:(jj + 1) * 64],
                                                                   scalar=ea[:, j:j + 1], in1=y_f[:, hs],
                                                                   op0=ALU.mult, op1=ALU.add), r=[pyo, sm, b_y[g]], w=[b_y[g]])
                        op("dve", lambda e: e.scalar_tensor_tensor(out=Hst[:, hs], in0=Hst[:, hs], scalar=cd[:, j:j + 1],
                                                                   in1=pst.t[:, jj * 64:(jj + 1) * 64],
                                                                   op0=ALU.mult, op1=ALU.add), r=[pst, sm, b_H[g]], w=[b_H[g]])
                    op("act", lambda e: e.activation(out=Hbf[:, gs], in_=Hst[:, gs], func=AF.Copy), r=[b_H[g]], w=[b_Hbf[g]])
                    op("dve", lambda e: e.tensor_tensor(out=y_f[:, gs], in0=y_f[:, gs], in1=zs_c[:, gs], op=ALU.mult),
                       r=[b_y[g], b_zs], w=[b_y[g]])
                    op("pool", lambda e: e.memset(gss.t[:, g:g + 1], 0.0), w=[gss])
                    op("act", lambda e: e.activation(out=t_f[:, gs], in_=y_f[:, gs], func=AF.Square, accum_out=gss.t[:, g:g + 1]),
                       r=[b_y[g]], w=[b_t[g], gss])
                    op("dve", lambda e: e.tensor_scalar(out=gss.t[:, g:g + 1], in0=gss.t[:, g:g + 1], scalar1=1.0 / 320, scalar2=EPS,
                                                       op0=ALU.mult, op1=ALU.add), r=[gss], w=[gss])
                    op("act", lambda e: e.activation(out=gss.t[:, g:g + 1], in_=gss.t[:, g:g + 1], func=AF.Sqrt), r=[gss], w=[gss])
                    op("dve", lambda e: e.reciprocal(out=gss.t[:, g:g + 1], in_=gss.t[:, g:g + 1]), r=[gss], w=[gss])
                    op("dve", lambda e: e.scalar_tensor_tensor(out=xdtd_b[:, gs], in0=y_f[:, gs], scalar=gss.t[:, g:g + 1],
                                                               in1=snw[:, gs], op0=ALU.mult, op1=ALU.mult),
                       r=[b_y[g], gss, b_cst], w=[b_xdtd[g]])
                for q4 in range(5):
                    bank = nxt("pb", pb)
                    for kk in range(4):
                        k = q4 * 4 + kk
                        gk = (k * 128) // 320
                        gk2 = (k * 128 + 127) // 320
                        op("pe", lambda e: e.transpose(out=bank.t[:, kk * 128:(kk + 1) * 128], in_=xdtd_b[:, k * 128:(k + 1) * 128],
                                                       identity=ident.t[:]), r=[b_xdtd[gk], b_xdtd[gk2], ident], w=[bank])
                    sg = nxt("stg", stg)
                    op("act", lambda e: e.activation(out=sg.t[:], in_=bank.t[:, 0:512], func=AF.Copy), r=[bank], w=[sg])
                    dma("sp", MIXT.rearrange("(k p) s -> p k s", p=128)[:, q4 * 4:(q4 + 1) * 4, rs],
                        sg.t[:].rearrange("p (k s) -> p k s", s=128), sg, bMIXT)
            join(sb_)

        def tail_stage(ti):
            r0 = ti * 512
            hl = lambda c: Hloc[ti * 4 + c][:, :]
            tmpT = big.t[:, 0:KC * 512].rearrange("p (k n) -> p k n", n=512)
            mv = MIXT.rearrange("(k p) s -> p k s", p=128)
            dma("sp", xnT.t[:], mv[:, :, r0:r0 + 512], bMIXT, xnT)
            dma("sp", tmpT, mv[:, :, SL + r0:SL + r0 + 512], bMIXT, big)
            op("dve", lambda e: e.tensor_scalar(out=xnT.t[:], in0=xnT.t[:], scalar1=sel.t[:, 0:1], scalar2=None, op0=ALU.mult),
               r=[xnT, sel], w=[xnT])
            op("dve", lambda e: e.scalar_tensor_tensor(out=xnT.t[:], in0=tmpT, scalar=sel.t[:, 1:2], in1=xnT.t[:],
                                                       op0=ALU.mult, op1=ALU.add), r=[big, sel, xnT], w=[xnT])
            lin_tm(W_out, KC, 0, D, xnT, evac_to_Fd())
            post_stage(hl, bH, hl, bH, 1, 1.0)
            ffn_stage(1, hl, bH, hl, bH, 2, 2)
            rows_norm_T(hl, bH, 3)
            pT = big.t[:, 0:1024].rearrange("p (k n) -> p k n", n=512)
            for c in range(4):
                dma("sp", ra.t[:, 0:256], p_in[r0 + c * 128:r0 + (c + 1) * 128, :], bp, ra)
                op("dve", lambda e: e.tensor_copy(out=xs.t[:, 0:256], in_=ra.t[:, 0:256]), r=[ra], w=[xs])
                bank = nxt("pb", pb)
                for kk in range(2):
                    op("pe", lambda e: e.transpose(out=bank.t[:, kk * 128:(kk + 1) * 128], in_=xs.t[:, kk * 128:(kk + 1) * 128],
                                                   identity=ident.t[:]), r=[xs, ident], w=[bank])
                for kk in range(2):
                    op("dve", lambda e: e.tensor_copy(out=pT[:, kk, c * 128:(c + 1) * 128], in_=bank.t[:, kk * 128:(kk + 1) * 128]),
                       r=[bank], w=[big])
            ppst = ra.t[:, 0:2048].rearrange("p (t c) -> p t c", c=512)

            for j in range(D // 512):
                cc = j * 512
                banks = [nxt("pf", pf) for _ in range(4)]
                s_ = load_slab([(W_pp, 0, 2, cc, 512, 0)])
                for tc in range(4):
                    for k in range(2):
                        op("pe", lambda e: e.matmul(banks[tc].t[:, 0:512], lhsT=pT[:, k, tc * 128:(tc + 1) * 128],
                                                    rhs=s_.t[:, k, 0:512], start=(k == 0), stop=(k == 1)),
                           r=[s_, big], w=[banks[tc]])
                for tc in range(4):
                    op("act", lambda e: e.activation(out=ppst[:, tc, :], in_=banks[tc].t[:, 0:512], func=AF.Copy),
                       r=[banks[tc]], w=[ra])
                banks = [nxt("pf", pf) for _ in range(4)]
                for k0 in range(0, KC, KS):
                    s_ = load_slab([(W_pg, k0, KS, cc, 512, 0)])
                    for tc in range(4):
                        for kk in range(KS):
                            k = k0 + kk
                            op("pe", lambda e: e.matmul(banks[tc].t[:, 0:512], lhsT=xnT.t[:, k, tc * 128:(tc + 1) * 128],
                                                        rhs=s_.t[:, kk, 0:512], start=(k == 0), stop=(k == KC - 1)),
                               r=[s_, xnT], w=[banks[tc]])
                f = nxt("fst", fst)
                for tc in range(4):
                    op("act", lambda e: e.activation(out=f.t[:, tc, :], in_=banks[tc].t[:, 0:512], func=AF.Sigmoid),
                       r=[banks[tc]], w=[f])
                    op("dve", lambda e: e.tensor_tensor(out=f.t[:, tc, :], in0=f.t[:, tc, :], in1=ppst[:, tc, :], op=ALU.mult),
                       r=[f, ra], w=[f])
                dma("sp", Fd.rearrange("(t p) c -> p t c", p=128)[:, :, cc:cc + 512], f.t[:], f, bFd)
            post_stage(hl, bH, lambda c: out[r0 + c * 128:r0 + (c + 1) * 128, :], bout, 3, 1.0)

        groups = [[0, 1], [2, 3], [4, 5], [6, 7]]
        with nc.named_scope("A_ffn1"):
            for ti in range(NTL):
                ffn_stage(0, lambda c: x_in[ti * 512 + c * 128:ti * 512 + (c + 1) * 128, :], bx,
                          lambda c: Hloc[ti * 4 + c][:, :], bH, 0, 0)
        with nc.named_scope("X_cc"):
            for k in range(NCL):
                sch.cc(Hloc_t[k].ap().opt(), Hg_t[k].ap().opt(), bH, bHg, groups)
        with nc.named_scope("A_inproj"):
            for ti in range(NT):
                inproj_stage(ti)
        with nc.named_scope("B_att"):
            attention()
        with nc.named_scope("B_ssd"):
            ssd()
        with nc.named_scope("C_tail"):
            for ti in range(NTL):
                tail_stage(ti)
        sch.finish("sp", [bout])
        sch.finish("act", [bout])
    return nc


def prep_inputs(inp, b, par):
    f = np.float32

    def col(v):
        v = np.asarray(v, f).reshape(-1)
        return np.ascontiguousarray(v.reshape(-1, 128).T)

    def bc(v):
        v = np.asarray(v, f).reshape(1, -1)
        return np.ascontiguousarray(np.broadcast_to(v, (128, v.shape[1])))
    m = {}
    SL = inp["x"].shape[1] // 2
    m["x"] = np.ascontiguousarray(inp["x"][b, par * SL:(par + 1) * SL])
    m["p"] = np.ascontiguousarray(inp["p"][0, b, par * SL:(par + 1) * SL])
    m["sel"] = np.ascontiguousarray(np.broadcast_to(np.eye(2, dtype=f)[par][None, :], (128, 2)))
    for k in ("ffn1_w_gate", "ffn1_w_up", "ffn1_w_down", "ffn2_w_gate", "ffn2_w_up", "ffn2_w_down",
              "w_in", "w_out", "w_ple_gate", "w_ple_proj"):
        m[k] = np.asarray(inp[k][0], f)
    m["gcols"] = np.ascontiguousarray(np.concatenate(
        [col(inp[k][0]) for k in ("ffn1_pre_w", "mix_pre_w", "ffn2_pre_w", "ple_pre_w")], axis=1))
    m["postw"] = np.ascontiguousarray(np.stack(
        [bc(inp[k][0]) for k in ("ffn1_post_w", "mix_post_w", "ffn2_post_w", "ple_post_w")], axis=0))
    cw = np.asarray(inp["conv_w"][0], f)
    m["convw"] = np.ascontiguousarray(cw.T.reshape(36, 128, 4).transpose(1, 0, 2).reshape(128, 144))
    m["convb"] = col(inp["conv_b"][0])
    m["headv"] = np.ascontiguousarray(np.concatenate(
        [bc(inp["dt_bias"][0]), bc(inp["a_log"][0]), bc(inp["d_skip"][0])], axis=1))
    m["dskx"] = bc(np.repeat(np.asarray(inp["d_skip"][0], f), 64))
    m["snw"] = bc(inp["ssd_norm_w"][0])
    m.update(host_consts())
    return m


_NC_CACHE = {}


def kernel(**inputs):
    B, S, _ = inputs["x"].shape
    DFF = inputs["ffn1_w_gate"].shape[-1]
    key = (S, DFF)
    if key not in _NC_CACHE:
        _NC_CACHE[key] = build(S, DFF)
    nc = _NC_CACHE[key]
    n = 8
    in_maps = [prep_inputs(inputs, (c // 2) % B, c % 2) for c in range(n)]
    res = run_bass_kernel_spmd(nc, in_maps, core_ids=list(range(n)))
    outs = [np.concatenate([res.results[2 * b]["out"], res.results[2 * b + 1]["out"]], axis=0) for b in range(B)]
    return np.stack(outs, axis=0).astype(np.float32)
```

```python
import math
from contextlib import ExitStack
import numpy as np
import concourse.bass as bass
import concourse.mybir as mybir
from concourse.bass_utils import run_bass_kernel_spmd

F32 = mybir.dt.float32
BF16 = mybir.dt.bfloat16
AF = mybir.ActivationFunctionType
ALU = mybir.AluOpType

D = 4096
KC = 32
KS = 16
EPS = 1e-6
SSD_W = 2560
NH = 40
NG = 8
ATT_W = 1536
AH = 12
OFF_Z, OFF_X, OFF_DT, OFF_Q, OFF_K, OFF_V = 0, 2560, 7168, 7208, 8744, 10280
EPOCH = 20000
BIG = 1.0e6
import os
DEBUG = int(os.environ.get("KDEBUG", "0"))
QS = 128 ** -0.5


def alibi_slopes(n):
    def pow2(m):
        start = 2.0 ** (-8.0 / m)
        return [start ** (i + 1) for i in range(m)]
    if math.log2(n).is_integer():
        s = pow2(n)
    else:
        c = 2 ** int(math.floor(math.log2(n)))
        s = pow2(c) + pow2(2 * c)[0::2][: n - c]
    return [float(np.float32(v)) for v in s]


class Ev:
    __slots__ = ("sem", "val", "key", "order")

    def __init__(self, sem, val, key, order):
        self.sem, self.val, self.key, self.order = sem, val, key, order


class Buf:
    def __init__(self, name, t=None):
        self.name = name
        self.t = t
        self.writes = {}
        self.reads = {}


class Sched:
    def __init__(self, nc, es):
        self.nc, self.es = nc, es
        self.eng = {"pe": nc.tensor, "act": nc.scalar, "dve": nc.vector, "pool": nc.gpsimd, "sp": nc.sync}
        self.cnt = {k: 0 for k in self.eng}
        self.psems = {k: [] for k in self.eng}
        self.seen = {k: {} for k in self.eng}
        self.dsem = {}
        self.dcnt = {}
        self.nsem = 0

    def _sem(self, name):
        self.nsem += 1
        return self.es.enter_context(self.nc.semaphore(name))

    def _wait(self, e, ev):
        if self.seen[e].get(ev.key, -1) >= ev.order:
            return
        self.eng[e].wait_ge(ev.sem, ev.val)
        self.seen[e][ev.key] = ev.order

    def _deps(self, e, reads, writes):
        for b in reads:
            for ev in b.writes.values():
                if not (e == "pe" and ev.key == "pe"):
                    self._wait(e, ev)
        for b in writes:
            for ev in list(b.writes.values()) + list(b.reads.values()):
                if not (e == "pe" and ev.key == "pe"):
                    self._wait(e, ev)

    def _record(self, ev, reads, writes):
        for b in reads:
            b.reads[ev.key] = ev
        for b in writes:
            b.writes = {ev.key: ev}
            b.reads = {}

    def op(self, e, fn, r=(), w=()):
        self._deps(e, r, w)
        ins = fn(self.eng[e])
        n = self.cnt[e]
        ep = n // EPOCH
        while len(self.psems[e]) <= ep:
            self.psems[e].append(self._sem(f"p_{e}_{len(self.psems[e])}"))
        self.cnt[e] = n + 1
        ins.then_inc(self.psems[e][ep], 1)
        ev = Ev(self.psems[e][ep], n % EPOCH + 1, e, n)
        self._record(ev, r, w)
        return ev

    def dma(self, q, out_ap, in_ap, src, dst, **kw):
        self._deps(q, [src], [dst])
        key = "d:" + src.name + ">" + dst.name
        if key not in self.dsem:
            self.dsem[key] = self._sem("d%d" % len(self.dsem))
            self.dcnt[key] = 0
        self.dcnt[key] += 16
        self.eng[q].dma_start(out=out_ap, in_=in_ap, **kw).then_inc(self.dsem[key], 16)
        ev = Ev(self.dsem[key], self.dcnt[key], key, self.dcnt[key])
        self._record(ev, [src], [dst])
        return ev

    def cc(self, in_ap, out_ap, src, dst, groups):
        self._deps("pool", [src], [dst])
        if "cc" not in self.dsem:
            self.dsem["cc"] = self._sem("ccs")
            self.dcnt["cc"] = 0
        self.dcnt["cc"] += 1
        self.eng["pool"].collective_compute("AllGather", ALU.bypass, replica_groups=groups,
                                            ins=[in_ap], outs=[out_ap]).then_inc(self.dsem["cc"], 1)
        ev = Ev(self.dsem["cc"], self.dcnt["cc"], "cc", self.dcnt["cc"])
        self._record(ev, [src], [dst])
        return ev

    def finish(self, e, bufs):
        for b in bufs:
            for ev in list(b.writes.values()):
                self._wait(e, ev)


def host_consts():
    c = {}
    c["ident"] = np.eye(128, dtype=np.float32)
    kk = np.arange(128)[:, None]
    ll = np.arange(128)[None, :]
    c["triu"] = (kk <= ll).astype(np.float32)
    c["negm"] = np.where(ll < kk, -30000.0, 0.0).astype(np.float32)
    prev = np.where(kk >= ll, 128.0 + ll - kk, BIG)
    cur = np.where(kk <= ll, (ll - kk).astype(np.float64), BIG)
    c["distm"] = np.concatenate([prev, cur], axis=1).astype(np.float32)
    return c


def build(S, DFF):
    NT = S // 512
    NCH = S // 128
    SL = S // 2
    NTL = SL // 512
    NCL = SL // 128
    FC = DFF // 128
    SSD_W, NH, NG, ATT_W, AH = 1280, 20, 4, 768, 6
    NBC = 128 * NG
    NXC = (SSD_W + 2 * NBC) // 128
    MXC = SSD_W // 128
    OFF_Z, OFF_X, OFF_DT, OFF_Q, OFF_K, OFF_V = 0, 1280, 3584, 3604, 4372, 5140
    WIN_W = 5908
    MIXL_W = SSD_W + ATT_W
    NMX = MIXL_W // 256
    nc = bass.Bass("TRN2", target_bir_lowering=False)

    def din(name, shape):
        return nc.dram_tensor(name, list(shape), F32, kind="ExternalInput").ap()

    x_in = din("x", [SL, D])
    p_in = din("p", [SL, 256])
    sel_in = din("sel", [128, 2])
    Wg = [din("ffn1_w_gate", [D, DFF]), din("ffn2_w_gate", [D, DFF])]
    Wu = [din("ffn1_w_up", [D, DFF]), din("ffn2_w_up", [D, DFF])]
    Wd = [din("ffn1_w_down", [DFF, D]), din("ffn2_w_down", [DFF, D])]
    W_in = din("w_in", [D, WIN_W])
    W_out = din("w_out", [D, D])
    W_pg = din("w_ple_gate", [D, D])
    W_pp = din("w_ple_proj", [256, D])
    gcols_in = din("gcols", [128, 4 * KC])
    postw_in = din("postw", [4, 128, D])
    convw_in = din("convw", [128, NXC * 4])
    convb_in = din("convb", [128, NXC])
    hv_in = din("headv", [128, 3 * NH])
    dskx_in = din("dskx", [128, SSD_W])
    snw_in = din("snw", [128, SSD_W])
    ident_in = din("ident", [128, 128])
    triu_in = din("triu", [128, 128])
    negm_in = din("negm", [128, 128])
    distm_in = din("distm", [128, 256])
    slp_in = din("slp", [128, AH * 3])
    out = nc.dram_tensor("out", [SL, D], F32, kind="ExternalOutput").ap()
    def dscr(name, shape, dt):
        return nc.dram_tensor(name, list(shape), dt).ap()

    Hloc_t = [nc.dram_tensor("Hloc%d" % k, [128, D], F32) for k in range(NCL)]
    Hg_t = [nc.dram_tensor("Hg%d" % k, [256, D], F32) for k in range(NCL)]
    Hloc = [t.ap() for t in Hloc_t]
    Hg = [t.ap() for t in Hg_t]
    Fd = dscr("Fd", [512, D], F32)
    ZS = dscr("ZS", [S, SSD_W], BF16)
    XS = dscr("XS", [S, SSD_W], BF16)
    BM = dscr("BM", [S, NBC], BF16)
    BT = dscr("BT", [NBC, S], BF16)
    CT = dscr("CT", [NBC, S], BF16)
    QT = dscr("QT", [ATT_W, S], BF16)
    KT = dscr("KT", [ATT_W, S], BF16)
    Vd = dscr("Vd", [S, ATT_W], BF16)
    MIXL_t = [nc.dram_tensor("MIXL%d" % j, [256, S], BF16) for j in range(NMX)]
    MIXG_t = [nc.dram_tensor("MIXG%d" % j, [512, S], BF16) for j in range(NMX)]
    MIXL = [t.ap() for t in MIXL_t]
    MIXG = [t.ap() for t in MIXG_t]

    es = ExitStack()
    with es:
        sch = Sched(nc, es)
        op, dma = sch.op, sch.dma

        def sb(name, shape, dt):
            return Buf(name, es.enter_context(nc.sbuf_tensor("s_" + name, list(shape), dt)))

        def ps(name, shape, dt):
            return Buf(name, es.enter_context(nc.psum_tensor(name, list(shape), dt)))

        bx, bp, bW = Buf("x"), Buf("p"), Buf("W")
        bcst = Buf("cst")
        bH, bHg, bFd, bZS, bXS, bBM, bBT, bCT = Buf("H"), Buf("Hg"), Buf("Fd"), Buf("ZS"), Buf("XS"), Buf("BM"), Buf("BT"), Buf("CT")
        bQT, bKT, bV, bMIXT, bout = Buf("QT"), Buf("KT"), Buf("V"), Buf("MIXT"), Buf("out")
        bMIXG = Buf("MIXG")

        xnT = sb("xnT", [128, KC, 512], BF16)
        big = sb("big", [128, max(FC * 512, 43008)], BF16)
        slabs = [sb("slab%d" % i, [128, KS, 512], BF16) for i in range(2)]
        ra = sb("ra", [128, D], F32)
        xs = sb("xs", [128, D], BF16)
        fst = [sb("fst%d" % i, [128, 4, 512], F32) for i in range(1)]
        sil = [sb("sil%d" % i, [128, 512], BF16) for i in range(4)]
        stg = [sb("stg%d" % i, [128, 512], BF16) for i in range(2)]
        gcols = sb("gcols", [128, 4 * KC], F32)
        convw = sb("convw", [128, NXC * 4], F32)
        convb = sb("convb", [128, NXC], F32)
        halo = sb("halo", [128, NXC, 4], F32)
        headv = sb("headv", [128, 3 * NH], F32)
        ident = sb("ident", [128, 128], BF16)
        identf = sb("identf", [128, 128], F32)
        triu = sb("triu", [128, 128], F32)
        onesf = sb("onesf", [128, 128], F32)
        onesb = sb("onesb", [128, 128], BF16)
        distm = sb("distm", [128, 256], F32)
        dt_sb = sb("dt_sb", [128, NCH, NH], F32)
        ss = sb("ss", [128, 1], F32)
        rstd = sb("rstd", [128, 1], F32)
        cst = sb("cstage", [128, 516], F32)
        acc = sb("cacc", [128, 512], F32)
        sm = sb("small", [128, 8 * NH], F32)
        gss = sb("gss", [128, 8], F32)
        pf = [ps("pf%d" % i, [128, 512], F32) for i in range(6)]
        pb = [ps("pb%d" % i, [128, 1024], BF16) for i in range(2)]
        st = {"pf": 0, "pb": 0, "slab": 0, "fst": 0, "sil": 0, "stg": 0}

        def nxt(kind, lst):
            i = st[kind]
            st[kind] = i + 1
            return lst[i % len(lst)]

        def ld(buf, ap_in, tmp=None):
            dma("sp", buf.t[:], ap_in, bcst, buf)

        ld(gcols, gcols_in[:, :])
        ld(convw, convw_in[:, :])
        ld(convb, convb_in[:, :])
        ld(headv, hv_in[:, :])
        ld(identf, ident_in[:, :])
        ld(triu, triu_in[:, :])
        ld(distm, distm_in[:, :])
        sel = sb("sel", [128, 2], F32)
        ld(sel, sel_in[:, :])
        slp = sb("slp", [128, AH * 3], F32)
        ld(slp, slp_in[:, :])
        op("dve", lambda e: e.tensor_copy(out=ident.t[:], in_=identf.t[:]), r=[identf], w=[ident])
        op("pool", lambda e: e.memset(onesf.t[:], 1.0), w=[onesf])
        op("pool", lambda e: e.memset(onesb.t[:], 1.0), w=[onesb])
        op("pool", lambda e: e.memset(halo.t[:], 0.0), w=[halo])

        def load_slab(parts):
            s = nxt("slab", slabs)
            for (W, k0, nk, c0, ncols, d0) in parts:
                wv = W.rearrange("(kc p) n -> p kc n", p=128)
                dma("pool", s.t[:, 0:nk, d0:d0 + ncols], wv[:, k0:k0 + nk, c0:c0 + ncols], bW, s)
            return s

        def rmsnorm_stats(src):
            n = D
            op("pool", lambda e: e.memset(ss.t[:], 0.0), w=[ss])
            op("act", lambda e: e.activation(out=xs.t[:], in_=src.t[:], func=AF.Square, accum_out=ss.t[:, 0:1]),
               r=[src], w=[xs, ss])
            op("dve", lambda e: e.tensor_scalar(out=rstd.t[:], in0=ss.t[:], scalar1=1.0 / n, scalar2=EPS,
                                               op0=ALU.mult, op1=ALU.add), r=[ss], w=[rstd])
            op("act", lambda e: e.activation(out=rstd.t[:], in_=rstd.t[:], func=AF.Sqrt), r=[rstd], w=[rstd])
            op("dve", lambda e: e.reciprocal(out=rstd.t[:], in_=rstd.t[:]), r=[rstd], w=[rstd])

        def rows_norm_T(getsrc, bsrc, gi):
            for c in range(4):
                dma("sp", ra.t[:], getsrc(c), bsrc, ra)
                rmsnorm_stats(ra)
                op("act", lambda e: e.activation(out=xs.t[:], in_=ra.t[:], func=AF.Copy, scale=rstd.t[:, 0:1]),
                   r=[ra, rstd], w=[xs])
                for kg in range(4):
                    bank = nxt("pb", pb)
                    for kk in range(8):
                        k = kg * 8 + kk
                        op("pe", lambda e: e.transpose(out=bank.t[:, kk * 128:(kk + 1) * 128],
                                                       in_=xs.t[:, k * 128:(k + 1) * 128], identity=ident.t[:]),
                           r=[xs, ident], w=[bank])
                    for kk in range(8):
                        k = kg * 8 + kk
                        g = gcols.t[:, gi * KC + k:gi * KC + k + 1]
                        o = xnT.t[:, k, c * 128:(c + 1) * 128]
                        i_ = bank.t[:, kk * 128:(kk + 1) * 128]
                        if kk % 2:
                            op("act", lambda e: e.activation(out=o, in_=i_, func=AF.Copy, scale=g),
                               r=[bank, gcols], w=[xnT])
                        else:
                            op("dve", lambda e: e.tensor_scalar(out=o, in0=i_, scalar1=g, scalar2=None, op0=ALU.mult),
                               r=[bank, gcols], w=[xnT])

        def lin_fm(W, c0, nchunks, evac):
            m = 0
            while m < nchunks:
                nm = min(4, nchunks - m)
                banks = [nxt("pf", pf) for _ in range(nm)]
                for k0 in range(0, KC, KS):
                    s_ = load_slab([(W, k0, KS, c0 + m * 128, nm * 128, 0)])
                    for mm in range(nm):
                        for kk in range(KS):
                            k = k0 + kk
                            op("pe", lambda e: e.matmul(banks[mm].t[:], lhsT=s_.t[:, kk, mm * 128:(mm + 1) * 128],
                                                        rhs=xnT.t[:, k, :], start=(k == 0), stop=(k == KC - 1)),
                               r=[s_, xnT], w=[banks[mm]])
                for mm in range(nm):
                    evac(m + mm, banks[mm])
                m += nm

        def lin_tm(W, nk_tot, c0, ncols_tot, lhs, evac, blk=512):
            j = 0
            while j * blk < ncols_tot:
                cc = c0 + j * blk
                ncols = min(blk, ncols_tot - j * blk)
                banks = [nxt("pf", pf) for _ in range(4)]
                for k0 in range(0, nk_tot, KS):
                    nk = min(KS, nk_tot - k0)
                    s = load_slab([(W, k0, nk, cc, ncols, 0)])
                    for tc in range(4):
                        for kk in range(nk):
                            k = k0 + kk
                            op("pe", lambda e: e.matmul(banks[tc].t[:, 0:ncols],
                                                        lhsT=lhs.t[:, k, tc * 128:(tc + 1) * 128],
                                                        rhs=s.t[:, kk, 0:ncols],
                                                        start=(k == 0), stop=(k == nk_tot - 1)),
                               r=[s, lhs], w=[banks[tc]])
                evac(j, cc - c0, ncols, banks)
                j += 1

        def evac_to_Fd(func=None):
            def ev(j, coff, ncols, banks):
                f = nxt("fst", fst)
                for tc in range(4):
                    if tc % 2:
                        op("act", lambda e: e.activation(out=f.t[:, tc, 0:ncols], in_=banks[tc].t[:, 0:ncols],
                                                         func=AF.Copy), r=[banks[tc]], w=[f])
                    else:
                        op("dve", lambda e: e.tensor_copy(out=f.t[:, tc, 0:ncols], in_=banks[tc].t[:, 0:ncols]),
                           r=[banks[tc]], w=[f])
                dma("sp", Fd.rearrange("(t p) c -> p t c", p=128)[:, :, coff:coff + ncols], f.t[:, :, 0:ncols], f, bFd)
            return ev

        def post_stage(getsrc, bsrc, getdst, bdst, wi, coef):
            wbv = big.t[:, 2048:2048 + 2 * D].bitcast(F32)
            rbv = big.t[:, 2048 + 2 * D:2048 + 4 * D].bitcast(F32)
            dma("sp", wbv, postw_in[wi, :, :], bcst, big)
            for c in range(4):
                dma("sp", ra.t[:], Fd[c * 128:(c + 1) * 128, :], bFd, ra)
                dma("sp", rbv, getsrc(c), bsrc, big)
                rmsnorm_stats(ra)
                op("dve", lambda e: e.scalar_tensor_tensor(out=ra.t[:], in0=ra.t[:], scalar=rstd.t[:, 0:1], in1=wbv,
                                                           op0=ALU.mult, op1=ALU.mult), r=[ra, rstd, big], w=[ra])
                op("dve", lambda e: e.scalar_tensor_tensor(out=rbv, in0=ra.t[:], scalar=float(coef), in1=rbv,
                                                           op0=ALU.mult, op1=ALU.add), r=[ra, big], w=[big])
                dma("sp", getdst(c), rbv, big, bdst)

        def ffn_stage(li, getsrc, bsrc, getdst, bdst, gi, wi):
            rows_norm_T(getsrc, bsrc, gi)
            gT = big
            fc = 0
            while fc < FC:
                nm = min(4, FC - fc)
                sls = []
                for (Wm, is_up) in ((Wg[li], False), (Wu[li], True)):
                    banks = [nxt("pf", pf) for _ in range(nm)]
                    for k0 in range(0, KC, KS):
                        s_ = load_slab([(Wm, k0, KS, fc * 128, nm * 128, 0)])
                        for mm in range(nm):
                            for kk in range(KS):
                                k = k0 + kk
                                op("pe", lambda e: e.matmul(banks[mm].t[:], lhsT=s_.t[:, kk, mm * 128:(mm + 1) * 128],
                                                            rhs=xnT.t[:, k, :], start=(k == 0), stop=(k == KC - 1)),
                                   r=[s_, xnT], w=[banks[mm]])
                    for mm in range(nm):
                        if not is_up:
                            sl = nxt("sil", sil)
                            sls.append(sl)
                            op("act", lambda e: e.activation(out=sl.t[:], in_=banks[mm].t[:], func=AF.Silu), r=[banks[mm]], w=[sl])
                        else:
                            sl = sls[mm]
                            op("dve", lambda e: e.tensor_tensor(out=gT.t[:, (fc + mm) * 512:(fc + mm + 1) * 512], in0=sl.t[:],
                                                                in1=banks[mm].t[:], op=ALU.mult), r=[sl, banks[mm]], w=[gT])
                fc += nm
            gT3 = gT.t[:, 0:FC * 512].rearrange("p (k n) -> p k n", n=512)
            lin_tm_view(Wd[li], FC, 0, D, gT, gT3, evac_to_Fd())
            post_stage(getsrc, bsrc, getdst, bdst, wi, 0.5)

        def lin_tm_view(W, nk_tot, c0, ncols_tot, track, view, evac, blk=512):
            j = 0
            while j * blk < ncols_tot:
                cc = c0 + j * blk
                ncols = min(blk, ncols_tot - j * blk)
                banks = [nxt("pf", pf) for _ in range(4)]
                for k0 in range(0, nk_tot, KS):
                    nk = min(KS, nk_tot - k0)
                    s = load_slab([(W, k0, nk, cc, ncols, 0)])
                    for tc in range(4):
                        for kk in range(nk):
                            k = k0 + kk
                            op("pe", lambda e: e.matmul(banks[tc].t[:, 0:ncols],
                                                        lhsT=view[:, k, tc * 128:(tc + 1) * 128],
                                                        rhs=s.t[:, kk, 0:ncols],
                                                        start=(k == 0), stop=(k == nk_tot - 1)),
                               r=[s, track], w=[banks[tc]])
                evac(j, cc - c0, ncols, banks)
                j += 1

        def inproj_stage(ti):
            r0 = ti * 512
            rows_norm_T(lambda c: Hg[(ti * 4 + c) % NCL][((ti * 4 + c) // NCL) * 128:((ti * 4 + c) // NCL + 1) * 128, :], bHg, 1)

            def ev_z(j, coff, ncols, banks):
                for tc in range(4):
                    sg = nxt("stg", stg)
                    op("act", lambda e: e.activation(out=sg.t[:, 0:ncols], in_=banks[tc].t[:, 0:ncols], func=AF.Silu),
                       r=[banks[tc]], w=[sg])
                    dma("sp", ZS[r0 + tc * 128:r0 + (tc + 1) * 128, coff:coff + ncols], sg.t[:, 0:ncols], sg, bZS)
            lin_tm(W_in, KC, OFF_Z, SSD_W, xnT, ev_z)

            def ev_v(j, coff, ncols, banks):
                for tc in range(4):
                    sg = nxt("stg", stg)
                    op("dve", lambda e: e.tensor_copy(out=sg.t[:, 0:ncols], in_=banks[tc].t[:, 0:ncols]),
                       r=[banks[tc]], w=[sg])
                    dma("sp", Vd[r0 + tc * 128:r0 + (tc + 1) * 128, coff:coff + ncols], sg.t[:, 0:ncols], sg, bV)
            lin_tm(W_in, KC, OFF_V, ATT_W, xnT, ev_v)

            def ev_dt(j, coff, ncols, banks):
                for tc in range(4):
                    ch = ti * 4 + tc
                    o = dt_sb.t[:, ch, :]
                    op("dve", lambda e: e.tensor_tensor(out=o, in0=banks[tc].t[:, 0:NH], in1=headv.t[:, 0:NH], op=ALU.add),
                       r=[banks[tc], headv], w=[dt_sb])
                    op("act", lambda e: e.activation(out=o, in_=o, func=AF.Exp), r=[dt_sb], w=[dt_sb])
                    op("act", lambda e: e.activation(out=o, in_=o, func=AF.Ln, bias=1.0, scale=1.0), r=[dt_sb], w=[dt_sb])
            lin_tm(W_in, KC, OFF_DT, NH, xnT, ev_dt)

            def ev_xbc(m, bank):
                op("dve", lambda e: e.tensor_copy(out=cst.t[:, 0:3], in_=halo.t[:, m, 0:3]), r=[halo], w=[cst])
                op("act", lambda e: e.activation(out=cst.t[:, 3:515], in_=bank.t[:], func=AF.Copy), r=[bank], w=[cst])
                op("dve", lambda e: e.tensor_scalar(out=acc.t[:], in0=cst.t[:, 0:512], scalar1=convw.t[:, m * 4:m * 4 + 1],
                                                   scalar2=convb.t[:, m:m + 1], op0=ALU.mult, op1=ALU.add),
                   r=[cst, convw, convb], w=[acc])
                for kq in range(1, 4):
                    op("dve", lambda e: e.scalar_tensor_tensor(out=acc.t[:], in0=cst.t[:, kq:kq + 512],
                                                               scalar=convw.t[:, m * 4 + kq:m * 4 + kq + 1], in1=acc.t[:],
                                                               op0=ALU.mult, op1=ALU.add), r=[cst, convw, acc], w=[acc])
                op("dve", lambda e: e.tensor_copy(out=halo.t[:, m, 0:3], in_=cst.t[:, 512:515]), r=[cst], w=[halo])
                sg = nxt("stg", stg)
                op("act", lambda e: e.activation(out=sg.t[:], in_=acc.t[:], func=AF.Silu), r=[acc], w=[sg])
                if m >= MXC + NG:
                    dma("sp", CT[(m - MXC - NG) * 128:(m - MXC - NG + 1) * 128, r0:r0 + 512], sg.t[:], sg, bCT)
                    return
                if m >= MXC:
                    dma("sp", BT[(m - MXC) * 128:(m - MXC + 1) * 128, r0:r0 + 512], sg.t[:], sg, bBT)
                bank2 = nxt("pb", pb)
                for tc in range(4):
                    op("pe", lambda e: e.transpose(out=bank2.t[:, tc * 128:(tc + 1) * 128],
                                                   in_=sg.t[:, tc * 128:(tc + 1) * 128], identity=ident.t[:]),
                       r=[sg, ident], w=[bank2])
                sg2 = nxt("stg", stg)
                op("dve", lambda e: e.tensor_copy(out=sg2.t[:], in_=bank2.t[:, 0:512]), r=[bank2], w=[sg2])
                if m >= MXC:
                    dst, bd, cc = BM, bBM, (m - MXC) * 128
                else:
                    dst, bd, cc = XS, bXS, m * 128
                dma("sp", dst.rearrange("(t p) c -> p t c", p=128)[:, ti * 4:(ti + 1) * 4, cc:cc + 128],
                    sg2.t[:].rearrange("p (t c) -> p t c", c=128), sg2, bd)
            lin_fm(W_in, OFF_X, NXC, ev_xbc)

            def mk_ev(dst, bd):
                def ev(m, bank):
                    sg = nxt("stg", stg)
                    if m % 2:
                        op("act", lambda e: e.activation(out=sg.t[:], in_=bank.t[:], func=AF.Copy), r=[bank], w=[sg])
                    else:
                        op("dve", lambda e: e.tensor_copy(out=sg.t[:], in_=bank.t[:]), r=[bank], w=[sg])
                    dma("sp", dst[m * 128:(m + 1) * 128, r0:r0 + 512], sg.t[:], sg, bd)
                return ev
            lin_fm(W_in, OFF_Q, AH, mk_ev(QT, bQT))
            lin_fm(W_in, OFF_K, AH, mk_ev(KT, bKT))

        def fork(names):
            subs = {}
            for n in names:
                bb = Buf(n)
                bb.writes = dict(big.writes)
                bb.reads = dict(big.reads)
                subs[n] = bb
            return subs

        def join(subs):
            m = {}
            for bb in [big] + list(subs.values()):
                for ev in list(bb.writes.values()) + list(bb.reads.values()):
                    if ev.key not in m or m[ev.key].order < ev.order:
                        m[ev.key] = ev
            big.writes = m
            big.reads = {}

        def attention():
            bg = big.t
            QTh = bg[:, 0:S]
            KTh = bg[:, S:2 * S]
            Vhs = [bg[:, 2 * S:3 * S], bg[:, 8 * S:9 * S]]
            attT = bg[:, 3 * S:4 * S]
            numT = bg[:, 4 * S:6 * S].bitcast(F32)
            denT = bg[:, 6 * S:8 * S].bitcast(F32)
            sb_ = fork(["a_qk", "a_v0", "a_v1", "a_num", "a_att"])
            bqk, bnum, batt = sb_["a_qk"], sb_["a_num"], sb_["a_att"]
            bvs = [sb_["a_v0"], sb_["a_v1"]]
            vi = 0
            for h in range(AH):
                dma("sp", QTh, QT[h * 128:(h + 1) * 128, :], bQT, bqk)
                dma("sp", KTh, KT[h * 128:(h + 1) * 128, :], bKT, bqk)
                for bi, d in enumerate((1, 4, 16)):
                    nb = S // (128 * d)
                    cp = slp.t[:, h * 3 + bi:h * 3 + bi + 1]
                    Vh, bv = Vhs[vi % 2], bvs[vi % 2]
                    vi += 1
                    with nc.allow_non_contiguous_dma(reason="dilated V gather"):
                        for r in range(d):
                            dma("sp", Vh.rearrange("p (r b f) -> p r b f", r=d, b=nb)[:, r, :, :],
                                Vd.rearrange("(b i r) f -> i r b f", i=128, r=d)[:, r, :, h * 128:(h + 1) * 128], bV, bv)
                    V4 = Vh.rearrange("p (r b f) -> p r b f", r=d, b=nb)
                    for r in range(d):
                        for b in range(nb):
                            def tok(bb):
                                s0 = bb * 128 * d + r
                                return slice(s0, s0 + 127 * d + 1, d)
                            qs = tok(b)
                            sbank = nxt("pf", pf)
                            if b > 0:
                                op("pe", lambda e: e.matmul(sbank.t[:, 0:128], lhsT=KTh[:, tok(b - 1)], rhs=QTh[:, qs],
                                                            start=True, stop=True), r=[bqk], w=[sbank])
                            op("pe", lambda e: e.matmul(sbank.t[:, 128:256], lhsT=KTh[:, qs], rhs=QTh[:, qs],
                                                        start=True, stop=True), r=[bqk], w=[sbank])
                            lo = 0 if b > 0 else 128
                            sl = nxt("sil", sil)
                            slv = sl.t[:, :].bitcast(F32)
                            op("dve", lambda e: e.scalar_tensor_tensor(out=slv[:, lo:256], in0=distm.t[:, lo:256],
                                                                       scalar=cp, in1=sbank.t[:, lo:256],
                                                                       op0=ALU.mult, op1=ALU.add),
                               r=[distm, sbank, slp], w=[sl])
                            pT = nxt("stg", stg)
                            op("act", lambda e: e.activation(out=pT.t[:, lo:256], in_=slv[:, lo:256], func=AF.Exp,
                                                             scale=float(QS)), r=[sl], w=[pT])
                            nbank = nxt("pf", pf)
                            if b > 0:
                                op("pe", lambda e: e.matmul(nbank.t[:, 0:128], lhsT=V4[:, r, b - 1, :], rhs=pT.t[:, 0:128],
                                                            start=True, stop=False), r=[bv, pT], w=[nbank])
                            op("pe", lambda e: e.matmul(nbank.t[:, 0:128], lhsT=V4[:, r, b, :], rhs=pT.t[:, 128:256],
                                                        start=(b == 0), stop=True), r=[bv, pT], w=[nbank])
                            if b > 0:
                                op("pe", lambda e: e.matmul(nbank.t[:, 128:256], lhsT=onesb.t[:], rhs=pT.t[:, 0:128],
                                                            start=True, stop=False), r=[onesb, pT], w=[nbank])
                            op("pe", lambda e: e.matmul(nbank.t[:, 128:256], lhsT=onesb.t[:], rhs=pT.t[:, 128:256],
                                                        start=(b == 0), stop=True), r=[onesb, pT], w=[nbank])
                            if bi == 0:
                                op("dve", lambda e: e.tensor_copy(out=numT[:, qs], in_=nbank.t[:, 0:128]), r=[nbank], w=[bnum])
                                op("act", lambda e: e.activation(out=denT[:, qs], in_=nbank.t[:, 128:256], func=AF.Copy),
                                   r=[nbank], w=[bnum])
                            else:
                                op("dve", lambda e: e.tensor_tensor(out=numT[:, qs], in0=numT[:, qs], in1=nbank.t[:, 0:128],
                                                                    op=ALU.add), r=[nbank, bnum], w=[bnum])
                                op("dve", lambda e: e.tensor_tensor(out=denT[:, qs], in0=denT[:, qs], in1=nbank.t[:, 128:256],
                                                                    op=ALU.add), r=[nbank, bnum], w=[bnum])
                op("dve", lambda e: e.reciprocal(out=denT, in_=denT), r=[bnum], w=[bnum])
                op("dve", lambda e: e.tensor_tensor(out=attT, in0=numT, in1=denT, op=ALU.mult), r=[bnum], w=[batt])
                dma("sp", MIXL[MXC // 2 + h // 2][(h % 2) * 128:(h % 2 + 1) * 128, :], attT, batt, bMIXT)
            join(sb_)

        def ssd():
            bg = big.t
            o = 0

            def carve(n, dt=BF16):
                nonlocal o
                ne = n if dt == BF16 else 2 * n
                v = bg[:, o:o + ne]
                o += ne
                return v if dt == BF16 else v.bitcast(F32)
            xs_c = carve(SSD_W)
            zs_c = carve(SSD_W)
            bm_c = carve(NBC)
            bt_c = carve(NBC)
            ct_c = carve(NBC)
            xdt_b = carve(SSD_W)
            xdtd_b = carve(SSD_W)
            Hbf = carve(SSD_W)
            Hst = carve(SSD_W, F32)
            y_f = carve(SSD_W, F32)
            t_f = carve(SSD_W, F32)
            negm = carve(128, F32)
            snw = carve(SSD_W, F32)
            dskx = carve(SSD_W, F32)
            NR = 2
            Rjs = [carve(128, F32) for _ in range(NR)]
            LTs = [carve(128, F32) for _ in range(NR)]
            MTs = [carve(128) for _ in range(NR)]
            names = ["s_xs", "s_zs", "s_bm", "s_bt", "s_ct", "s_xdtb", "s_cst"]
            names += ["s_xdtd%d" % g for g in range(NG)] + ["s_H%d" % g for g in range(NG)] + ["s_Hbf%d" % g for g in range(NG)]
            names += ["s_y%d" % g for g in range(NG)] + ["s_t%d" % g for g in range(NG)]
            names += ["s_R%d" % i for i in range(NR)] + ["s_L%d" % i for i in range(NR)] + ["s_M%d" % i for i in range(NR)]
            sb_ = fork(names)
            b_xs, b_zs, b_bm, b_bt, b_ct, b_xdtb, b_cst = [sb_[n] for n in names[:7]]
            b_xdtd = [sb_["s_xdtd%d" % g] for g in range(NG)]
            b_H = [sb_["s_H%d" % g] for g in range(NG)]
            b_Hbf = [sb_["s_Hbf%d" % g] for g in range(NG)]
            b_y = [sb_["s_y%d" % g] for g in range(NG)]
            b_t = [sb_["s_t%d" % g] for g in range(NG)]
            b_R = [sb_["s_R%d" % i] for i in range(NR)]
            b_L = [sb_["s_L%d" % i] for i in range(NR)]
            b_M = [sb_["s_M%d" % i] for i in range(NR)]
            dma("sp", negm, negm_in[:, :], bcst, b_cst)
            dma("sp", snw, snw_in[:, :], bcst, b_cst)
            dma("sp", dskx, dskx_in[:, :], bcst, b_cst)
            smt = sm.t
            dtA = smt[:, 0:NH]
            acs = smt[:, NH:2 * NH]
            nacs = smt[:, 2 * NH:3 * NH]
            ea = smt[:, 3 * NH:4 * NH]
            cd = smt[:, 4 * NH:5 * NH]
            dsc = smt[:, 5 * NH:6 * NH]
            a_b = smt[:, 6 * NH:7 * NH]
            dtd = smt[:, 7 * NH:8 * NH]
            op("act", lambda e: e.activation(out=a_b, in_=headv.t[:, NH:2 * NH], func=AF.Exp), r=[headv], w=[sm])
            op("dve", lambda e: e.tensor_scalar(out=a_b, in0=a_b, scalar1=-1.0, scalar2=None, op0=ALU.mult), r=[sm], w=[sm])
            for g in range(NG):
                gs = slice(g * 320, (g + 1) * 320)
                op("pool", lambda e: e.memset(Hst[:, gs], 0.0), w=[b_H[g]])
                op("pool", lambda e: e.memset(Hbf[:, gs], 0.0), w=[b_Hbf[g]])
            hc = 0
            for c in range(NCH):
                rs = slice(c * 128, (c + 1) * 128)
                dma("sp", xs_c, XS[rs, :], bXS, b_xs)
                dma("sp", zs_c, ZS[rs, :], bZS, b_zs)
                dma("sp", bm_c, BM[rs, :], bBM, b_bm)
                dma("sp", bt_c.rearrange("p (g s) -> p g s", g=NG), BT.rearrange("(g n) s -> n g s", n=128)[:, :, rs], bBT, b_bt)
                dma("sp", ct_c.rearrange("p (g s) -> p g s", g=NG), CT.rearrange("(g n) s -> n g s", n=128)[:, :, rs], bCT, b_ct)
                dtc = dt_sb.t[:, c, :]
                op("dve", lambda e: e.tensor_tensor(out=dtA, in0=dtc, in1=a_b, op=ALU.mult), r=[dt_sb, sm], w=[sm])
                pa = pf[4]
                op("pe", lambda e: e.matmul(pa.t[:, 0:NH], lhsT=triu.t[:], rhs=dtA, start=True, stop=True), r=[triu, sm], w=[pa])
                op("pe", lambda e: e.matmul(pa.t[:, 64:64 + NH], lhsT=onesf.t[:], rhs=dtA, start=True, stop=True),
                   r=[onesf, sm], w=[pa])
                op("dve", lambda e: e.tensor_copy(out=acs, in_=pa.t[:, 0:NH]), r=[pa], w=[sm])
                op("dve", lambda e: e.tensor_scalar(out=nacs, in0=pa.t[:, 0:NH], scalar1=-1.0, scalar2=None, op0=ALU.mult),
                   r=[pa], w=[sm])
                op("act", lambda e: e.activation(out=ea, in_=pa.t[:, 0:NH], func=AF.Exp), r=[pa], w=[sm])
                op("act", lambda e: e.activation(out=cd, in_=pa.t[:, 64:64 + NH], func=AF.Exp), r=[pa], w=[sm])
                op("dve", lambda e: e.tensor_tensor(out=dsc, in0=pa.t[:, 64:64 + NH], in1=acs, op=ALU.subtract), r=[pa, sm], w=[sm])
                op("act", lambda e: e.activation(out=dsc, in_=dsc, func=AF.Exp), r=[sm], w=[sm])
                op("dve", lambda e: e.tensor_tensor(out=dtd, in0=dsc, in1=dtc, op=ALU.mult), r=[sm, dt_sb], w=[sm])
                for j in range(NH):
                    hs = slice(j * 64, (j + 1) * 64)
                    g = j // 5
                    op("dve", lambda e: e.tensor_scalar(out=xdt_b[:, hs], in0=xs_c[:, hs], scalar1=dtc[:, j:j + 1], scalar2=None,
                                                       op0=ALU.mult), r=[b_xs, dt_sb], w=[b_xdtb])
                    op("dve", lambda e: e.tensor_scalar(out=xdtd_b[:, hs], in0=xs_c[:, hs], scalar1=dtd[:, j:j + 1], scalar2=None,
                                                       op0=ALU.mult), r=[b_xs, sm], w=[b_xdtd[g]])
                for g in range(NG):
                    gs = slice(g * 320, (g + 1) * 320)
                    n_s = slice(g * 128, (g + 1) * 128)
                    pyo = pf[0]
                    op("pe", lambda e: e.matmul(pyo.t[:, 0:320], lhsT=ct_c[:, n_s], rhs=Hbf[:, gs], start=True, stop=True),
                       r=[b_ct, b_Hbf[g]], w=[pyo])
                    pst = pf[1]
                    op("pe", lambda e: e.matmul(pst.t[:, 0:320], lhsT=bm_c[:, n_s], rhs=xdtd_b[:, gs], start=True, stop=True),
                       r=[b_bm, b_xdtd[g]], w=[pst])
                    pcb = pf[2]
                    op("pe", lambda e: e.matmul(pcb.t[:, 0:128], lhsT=bt_c[:, n_s], rhs=ct_c[:, n_s], start=True, stop=True),
                       r=[b_bt, b_ct], w=[pcb])
                    pyd = pf[3]
                    for jj in range(5):
                        j = g * 5 + jj
                        hs = slice(j * 64, (j + 1) * 64)
                        ri = hc % NR
                        hc += 1
                        Rj, LT, MT = Rjs[ri], LTs[ri], MTs[ri]
                        op("dve", lambda e: e.tensor_scalar(out=Rj, in0=triu.t[:], scalar1=dtA[:, j:j + 1], scalar2=None,
                                                           op0=ALU.mult), r=[triu, sm], w=[b_R[ri]])
                        pbc = pf[4 + (j % 2)]
                        op("pe", lambda e: e.matmul(pbc.t[:, 0:128], lhsT=onesf.t[:], rhs=Rj, start=True, stop=False),
                           r=[onesf, b_R[ri]], w=[pbc])
                        op("pe", lambda e: e.matmul(pbc.t[:, 0:128], lhsT=identf.t[:], rhs=negm, start=False, stop=True),
                           r=[identf, b_cst], w=[pbc])
                        op("act", lambda e: e.activation(out=LT, in_=pbc.t[:, 0:128], func=AF.Exp, bias=nacs[:, j:j + 1], scale=1.0),
                           r=[pbc, sm], w=[b_L[ri]])
                        op("dve", lambda e: e.tensor_tensor(out=MT, in0=LT, in1=pcb.t[:, 0:128], op=ALU.mult), r=[b_L[ri], pcb], w=[b_M[ri]])
                        op("pe", lambda e: e.matmul(pyd.t[:, jj * 64:(jj + 1) * 64], lhsT=MT, rhs=xdt_b[:, hs], start=True, stop=True),
                           r=[b_M[ri], b_xdtb], w=[pyd])
                    op("pool", lambda e: e.tensor_tensor(out=t_f[:, gs], in0=xs_c[:, gs], in1=dskx[:, gs], op=ALU.mult),
                       r=[b_xs, b_cst], w=[b_t[g]])
                    op("dve", lambda e: e.tensor_tensor(out=y_f[:, gs], in0=t_f[:, gs], in1=pyd.t[:, 0:320], op=ALU.add),
                       r=[b_t[g], pyd], w=[b_y[g]])
                    for jj in range(5):
                        j = g * 5 + jj
                        hs = slice(j * 64, (j + 1) * 64)
                        op("dve", lambda e: e.scalar_tensor_tensor(out=y_f[:, hs], in0=pyo.t[:, jj * 64:(jj + 1) * 64],
                                                                   scalar=ea[:, j:j + 1], in1=y_f[:, hs],
                                                                   op0=ALU.mult, op1=ALU.add), r=[pyo, sm, b_y[g]], w=[b_y[g]])
                        op("dve", lambda e: e.scalar_tensor_tensor(out=Hst[:, hs], in0=Hst[:, hs], scalar=cd[:, j:j + 1],
                                                                   in1=pst.t[:, jj * 64:(jj + 1) * 64],
                                                                   op0=ALU.mult, op1=ALU.add), r=[pst, sm, b_H[g]], w=[b_H[g]])
                    op("act", lambda e: e.activation(out=Hbf[:, gs], in_=Hst[:, gs], func=AF.Copy), r=[b_H[g]], w=[b_Hbf[g]])
                    op("dve", lambda e: e.tensor_tensor(out=y_f[:, gs], in0=y_f[:, gs], in1=zs_c[:, gs], op=ALU.mult),
                       r=[b_y[g], b_zs], w=[b_y[g]])
                    op("pool", lambda e: e.memset(gss.t[:, g:g + 1], 0.0), w=[gss])
                    op("act", lambda e: e.activation(out=t_f[:, gs], in_=y_f[:, gs], func=AF.Square, accum_out=gss.t[:, g:g + 1]),
                       r=[b_y[g]], w=[b_t[g], gss])
                    op("dve", lambda e: e.tensor_scalar(out=gss.t[:, g:g + 1], in0=gss.t[:, g:g + 1], scalar1=1.0 / 320, scalar2=EPS,
                                                       op0=ALU.mult, op1=ALU.add), r=[gss], w=[gss])
                    op("act", lambda e: e.activation(out=gss.t[:, g:g + 1], in_=gss.t[:, g:g + 1], func=AF.Sqrt), r=[gss], w=[gss])
                    op("dve", lambda e: e.reciprocal(out=gss.t[:, g:g + 1], in_=gss.t[:, g:g + 1]), r=[gss], w=[gss])
                    op("dve", lambda e: e.scalar_tensor_tensor(out=xdtd_b[:, gs], in0=y_f[:, gs], scalar=gss.t[:, g:g + 1],
                                                               in1=snw[:, gs], op0=ALU.mult, op1=ALU.mult),
                       r=[b_y[g], gss, b_cst], w=[b_xdtd[g]])
                for q4 in range(MXC // 2):
                    bank = nxt("pb", pb)
                    for kk in range(2):
                        k = q4 * 2 + kk
                        gk = (k * 128) // 320
                        gk2 = (k * 128 + 127) // 320
                        op("pe", lambda e: e.transpose(out=bank.t[:, kk * 128:(kk + 1) * 128], in_=xdtd_b[:, k * 128:(k + 1) * 128],
                                                       identity=ident.t[:]), r=[b_xdtd[gk], b_xdtd[gk2], ident], w=[bank])
                    sg = nxt("stg", stg)
                    op("act", lambda e: e.activation(out=sg.t[:, 0:256], in_=bank.t[:, 0:256], func=AF.Copy), r=[bank], w=[sg])
                    dma("sp", MIXL[q4].rearrange("(k p) s -> p k s", p=128)[:, :, rs],
                        sg.t[:, 0:256].rearrange("p (k s) -> p k s", s=128), sg, bMIXT)
            join(sb_)

        def tail_stage(ti):
            r0 = ti * 512
            hl = lambda c: Hloc[ti * 4 + c][:, :]
            tmpT = big.t[:, 0:KC * 512].rearrange("p (k n) -> p k n", n=512)
            for j in range(NMX):
                mv = MIXG[j].rearrange("(q p) s -> p q s", p=128)
                dma("sp", xnT.t[:, j * 4:(j + 1) * 4, :], mv[:, :, r0:r0 + 512], bMIXG, xnT)
                dma("sp", tmpT[:, j * 4:(j + 1) * 4, :], mv[:, :, SL + r0:SL + r0 + 512], bMIXG, big)
            op("dve", lambda e: e.tensor_scalar(out=xnT.t[:], in0=xnT.t[:], scalar1=sel.t[:, 0:1], scalar2=None, op0=ALU.mult),
               r=[xnT, sel], w=[xnT])
            op("dve", lambda e: e.scalar_tensor_tensor(out=xnT.t[:], in0=tmpT, scalar=sel.t[:, 1:2], in1=xnT.t[:],
                                                       op0=ALU.mult, op1=ALU.add), r=[big, sel, xnT], w=[xnT])
            lin_tm(W_out, KC, 0, D, xnT, evac_to_Fd())
            post_stage(hl, bH, hl, bH, 1, 1.0)
            ffn_stage(1, hl, bH, hl, bH, 2, 2)
            rows_norm_T(hl, bH, 3)
            pT = big.t[:, 0:1024].rearrange("p (k n) -> p k n", n=512)
            for c in range(4):
                dma("sp", ra.t[:, 0:256], p_in[r0 + c * 128:r0 + (c + 1) * 128, :], bp, ra)
                op("dve", lambda e: e.tensor_copy(out=xs.t[:, 0:256], in_=ra.t[:, 0:256]), r=[ra], w=[xs])
                bank = nxt("pb", pb)
                for kk in range(2):
                    op("pe", lambda e: e.transpose(out=bank.t[:, kk * 128:(kk + 1) * 128], in_=xs.t[:, kk * 128:(kk + 1) * 128],
                                                   identity=ident.t[:]), r=[xs, ident], w=[bank])
                for kk in range(2):
                    op("dve", lambda e: e.tensor_copy(out=pT[:, kk, c * 128:(c + 1) * 128], in_=bank.t[:, kk * 128:(kk + 1) * 128]),
                       r=[bank], w=[big])
            ppst = ra.t[:, 0:2048].rearrange("p (t c) -> p t c", c=512)

            for j in range(D // 512):
                cc = j * 512
                banks = [nxt("pf", pf) for _ in range(4)]
                s_ = load_slab([(W_pp, 0, 2, cc, 512, 0)])
                for tc in range(4):
                    for k in range(2):
                        op("pe", lambda e: e.matmul(banks[tc].t[:, 0:512], lhsT=pT[:, k, tc * 128:(tc + 1) * 128],
                                                    rhs=s_.t[:, k, 0:512], start=(k == 0), stop=(k == 1)),
                           r=[s_, big], w=[banks[tc]])
                for tc in range(4):
                    op("act", lambda e: e.activation(out=ppst[:, tc, :], in_=banks[tc].t[:, 0:512], func=AF.Copy),
                       r=[banks[tc]], w=[ra])
                banks = [nxt("pf", pf) for _ in range(4)]
                for k0 in range(0, KC, KS):
                    s_ = load_slab([(W_pg, k0, KS, cc, 512, 0)])
                    for tc in range(4):
                        for kk in range(KS):
                            k = k0 + kk
                            op("pe", lambda e: e.matmul(banks[tc].t[:, 0:512], lhsT=xnT.t[:, k, tc * 128:(tc + 1) * 128],
                                                        rhs=s_.t[:, kk, 0:512], start=(k == 0), stop=(k == KC - 1)),
                               r=[s_, xnT], w=[banks[tc]])
                f = nxt("fst", fst)
                for tc in range(4):
                    op("act", lambda e: e.activation(out=f.t[:, tc, :], in_=banks[tc].t[:, 0:512], func=AF.Sigmoid),
                       r=[banks[tc]], w=[f])
                    op("dve", lambda e: e.tensor_tensor(out=f.t[:, tc, :], in0=f.t[:, tc, :], in1=ppst[:, tc, :], op=ALU.mult),
                       r=[f, ra], w=[f])
                dma("sp", Fd.rearrange("(t p) c -> p t c", p=128)[:, :, cc:cc + 512], f.t[:], f, bFd)
            post_stage(hl, bH, lambda c: out[r0 + c * 128:r0 + (c + 1) * 128, :], bout, 3, 1.0)

        groups = [[0, 1], [2, 3], [4, 5], [6, 7]]
        with nc.named_scope("A_ffn1"):
            for ti in range(NTL):
                ffn_stage(0, lambda c: x_in[ti * 512 + c * 128:ti * 512 + (c + 1) * 128, :], bx,
                          lambda c: Hloc[ti * 4 + c][:, :], bH, 0, 0)
        with nc.named_scope("X_cc"):
            for k in range(NCL):
                sch.cc(Hloc_t[k].ap().opt(), Hg_t[k].ap().opt(), bH, bHg, groups)
        with nc.named_scope("A_inproj"):
            for ti in range(NT):
                inproj_stage(ti)
        with nc.named_scope("B_att"):
            attention()
        with nc.named_scope("B_ssd"):
            ssd()
        with nc.named_scope("X_cc2"):
            for j in range(NMX):
                sch.cc(MIXL_t[j].ap().opt(), MIXG_t[j].ap().opt(), bMIXT, bMIXG, groups)
        with nc.named_scope("C_tail"):
            for ti in range(NTL):
                tail_stage(ti)
        sch.finish("sp", [bout])
        sch.finish("act", [bout])
    return nc


def prep_inputs(inp, b, par):
    f = np.float32

    def col(v):
        v = np.asarray(v, f).reshape(-1)
        return np.ascontiguousarray(v.reshape(-1, 128).T)

    def bc(v):
        v = np.asarray(v, f).reshape(1, -1)
        return np.ascontiguousarray(np.broadcast_to(v, (128, v.shape[1])))
    m = {}
    SL = inp["x"].shape[1] // 2
    m["x"] = np.ascontiguousarray(inp["x"][b, par * SL:(par + 1) * SL])
    m["p"] = np.ascontiguousarray(inp["p"][0, b, par * SL:(par + 1) * SL])
    m["sel"] = np.ascontiguousarray(np.broadcast_to(np.eye(2, dtype=f)[par][None, :], (128, 2)))
    for k in ("ffn1_w_gate", "ffn1_w_up", "ffn1_w_down", "ffn2_w_gate", "ffn2_w_up", "ffn2_w_down",
              "w_ple_gate", "w_ple_proj"):
        m[k] = np.asarray(inp[k][0], f)
    FO_Z, FO_X, FO_DT, FO_Q, FO_K, FO_V = 0, 2560, 7168, 7208, 8744, 10280
    cols = np.concatenate([
        np.arange(FO_Z + par * 1280, FO_Z + (par + 1) * 1280),
        np.arange(FO_X + par * 1280, FO_X + (par + 1) * 1280),
        np.arange(FO_X + 2560 + par * 512, FO_X + 2560 + (par + 1) * 512),
        np.arange(FO_X + 3584 + par * 512, FO_X + 3584 + (par + 1) * 512),
        np.arange(FO_DT + par * 20, FO_DT + (par + 1) * 20),
        np.arange(FO_Q + par * 768, FO_Q + (par + 1) * 768),
        np.arange(FO_K + par * 768, FO_K + (par + 1) * 768),
        np.arange(FO_V + par * 768, FO_V + (par + 1) * 768)])
    m["w_in"] = np.ascontiguousarray(np.asarray(inp["w_in"][0], f)[:, cols])
    cch = np.concatenate([np.arange(par * 1280, (par + 1) * 1280),
                          np.arange(2560 + par * 512, 2560 + (par + 1) * 512),
                          np.arange(3584 + par * 512, 3584 + (par + 1) * 512)])
    rows = []
    for j in range(8):
        for r in range(2):
            for hh in range(2):
                lf = j * 256 + hh * 128 + np.arange(128)
                rows.append(np.where(lf < 1280, r * 1280 + lf, 2560 + r * 768 + (lf - 1280)))
    rows = np.concatenate(rows)
    m["w_out"] = np.ascontiguousarray(np.asarray(inp["w_out"][0], f)[rows, :])
    m["gcols"] = np.ascontiguousarray(np.concatenate(
        [col(inp[k][0]) for k in ("ffn1_pre_w", "mix_pre_w", "ffn2_pre_w", "ple_pre_w")], axis=1))
    m["postw"] = np.ascontiguousarray(np.stack(
        [bc(inp[k][0]) for k in ("ffn1_post_w", "mix_post_w", "ffn2_post_w", "ple_post_w")], axis=0))
    cw = np.asarray(inp["conv_w"][0], f)[:, cch]
    m["convw"] = np.ascontiguousarray(cw.T.reshape(18, 128, 4).transpose(1, 0, 2).reshape(128, 72))
    m["convb"] = col(np.asarray(inp["conv_b"][0], f)[cch])
    hsl = slice(par * 20, (par + 1) * 20)
    m["headv"] = np.ascontiguousarray(np.concatenate(
        [bc(inp["dt_bias"][0][hsl]), bc(inp["a_log"][0][hsl]), bc(inp["d_skip"][0][hsl])], axis=1))
    m["dskx"] = bc(np.repeat(np.asarray(inp["d_skip"][0], f)[hsl], 64))
    m["snw"] = bc(np.asarray(inp["ssd_norm_w"][0], f)[par * 1280:(par + 1) * 1280])
    sl_all = alibi_slopes(12)
    m["slp"] = bc(np.asarray([-sl_all[par * 6 + a] * d / QS for a in range(6) for d in (1, 4, 16)], f))
    m.update(host_consts())
    return m


_NC_CACHE = {}


def kernel(**inputs):
    B, S, _ = inputs["x"].shape
    DFF = inputs["ffn1_w_gate"].shape[-1]
    key = (S, DFF)
    if key not in _NC_CACHE:
        _NC_CACHE[key] = build(S, DFF)
    nc = _NC_CACHE[key]
    n = 8
    in_maps = [prep_inputs(inputs, (c // 2) % B, c % 2) for c in range(n)]
    res = run_bass_kernel_spmd(nc, in_maps, core_ids=list(range(n)))
    outs = [np.concatenate([res.results[2 * b]["out"], res.results[2 * b + 1]["out"]], axis=0) for b in range(B)]
    return np.stack(outs, axis=0).astype(np.float32)
```

```python
import math
from contextlib import ExitStack
import numpy as np
import concourse.bass as bass
import concourse.mybir as mybir
from concourse.bass_utils import run_bass_kernel_spmd

F32 = mybir.dt.float32
BF16 = mybir.dt.bfloat16
AF = mybir.ActivationFunctionType
ALU = mybir.AluOpType

D = 4096
KC = 32
KS = 16
EPS = 1e-6
SSD_W = 2560
NH = 40
NG = 8
ATT_W = 1536
AH = 12
OFF_Z, OFF_X, OFF_DT, OFF_Q, OFF_K, OFF_V = 0, 2560, 7168, 7208, 8744, 10280
EPOCH = 20000
BIG = 1.0e6
import os
DEBUG = int(os.environ.get("KDEBUG", "0"))
QS = 128 ** -0.5


def alibi_slopes(n):
    def pow2(m):
        start = 2.0 ** (-8.0 / m)
        return [start ** (i + 1) for i in range(m)]
    if math.log2(n).is_integer():
        s = pow2(n)
    else:
        c = 2 ** int(math.floor(math.log2(n)))
        s = pow2(c) + pow2(2 * c)[0::2][: n - c]
    return [float(np.float32(v)) for v in s]


class Ev:
    __slots__ = ("sem", "val", "key", "order")

    def __init__(self, sem, val, key, order):
        self.sem, self.val, self.key, self.order = sem, val, key, order


class Buf:
    def __init__(self, name, t=None):
        self.name = name
        self.t = t
        self.writes = {}
        self.reads = {}


class Sched:
    def __init__(self, nc, es):
        self.nc, self.es = nc, es
        self.eng = {"pe": nc.tensor, "act": nc.scalar, "dve": nc.vector, "pool": nc.gpsimd, "sp": nc.sync}
        self.cnt = {k: 0 for k in self.eng}
        self.psems = {k: [] for k in self.eng}
        self.seen = {k: {} for k in self.eng}
        self.dsem = {}
        self.dcnt = {}
        self.nsem = 0

    def _sem(self, name):
        self.nsem += 1
        return self.es.enter_context(self.nc.semaphore(name))

    def _wait(self, e, ev):
        if self.seen[e].get(ev.key, -1) >= ev.order:
            return
        self.eng[e].wait_ge(ev.sem, ev.val)
        self.seen[e][ev.key] = ev.order

    def _deps(self, e, reads, writes):
        for b in reads:
            for ev in b.writes.values():
                if not (e == "pe" and ev.key == "pe"):
                    self._wait(e, ev)
        for b in writes:
            for ev in list(b.writes.values()) + list(b.reads.values()):
                if not (e == "pe" and ev.key == "pe"):
                    self._wait(e, ev)

    def _record(self, ev, reads, writes):
        for b in reads:
            b.reads[ev.key] = ev
        for b in writes:
            b.writes = {ev.key: ev}
            b.reads = {}

    def op(self, e, fn, r=(), w=(), inc=True):
        self._deps(e, r, w)
        ins = fn(self.eng[e])
        if not inc:
            return None
        n = self.cnt[e]
        ep = n // EPOCH
        while len(self.psems[e]) <= ep:
            self.psems[e].append(self._sem(f"p_{e}_{len(self.psems[e])}"))
        self.cnt[e] = n + 1
        ins.then_inc(self.psems[e][ep], 1)
        ev = Ev(self.psems[e][ep], n % EPOCH + 1, e, n)
        self._record(ev, r, w)
        return ev

    def dma(self, q, out_ap, in_ap, src, dst, **kw):
        self._deps(q, [src], [dst])
        key = "d:" + src.name + ">" + dst.name
        if key not in self.dsem:
            self.dsem[key] = self._sem("d%d" % len(self.dsem))
            self.dcnt[key] = 0
        self.dcnt[key] += 16
        self.eng[q].dma_start(out=out_ap, in_=in_ap, **kw).then_inc(self.dsem[key], 16)
        ev = Ev(self.dsem[key], self.dcnt[key], key, self.dcnt[key])
        self._record(ev, [src], [dst])
        return ev

    def cc(self, in_ap, out_ap, src, dst, groups):
        self._deps("pool", [src], [dst])
        if "cc" not in self.dsem:
            self.dsem["cc"] = self._sem("ccs")
            self.dcnt["cc"] = 0
        self.dcnt["cc"] += 1
        self.eng["pool"].collective_compute("AllGather", ALU.bypass, replica_groups=groups,
                                            ins=[in_ap], outs=[out_ap]).then_inc(self.dsem["cc"], 1)
        ev = Ev(self.dsem["cc"], self.dcnt["cc"], "cc", self.dcnt["cc"])
        self._record(ev, [src], [dst])
        return ev

    def finish(self, e, bufs):
        for b in bufs:
            for ev in list(b.writes.values()):
                self._wait(e, ev)


def host_consts():
    c = {}
    c["ident"] = np.eye(128, dtype=np.float32)
    kk = np.arange(128)[:, None]
    ll = np.arange(128)[None, :]
    c["triu"] = (kk <= ll).astype(np.float32)
    c["negm"] = np.where(ll < kk, -30000.0, 0.0).astype(np.float32)
    prev = np.where(kk >= ll, 128.0 + ll - kk, BIG)
    cur = np.where(kk <= ll, (ll - kk).astype(np.float64), BIG)
    c["distm"] = np.concatenate([prev, cur], axis=1).astype(np.float32)
    return c


def build(S, DFF):
    NT = S // 512
    NCH = S // 128
    SL = S // 2
    NTL = SL // 512
    NCL = SL // 128
    FC = DFF // 128
    SSD_W, NH, NG, ATT_W, AH = 1280, 20, 4, 768, 6
    NBC = 128 * NG
    NXC = (SSD_W + 2 * NBC) // 128
    MXC = SSD_W // 128
    OFF_Z, OFF_X, OFF_DT, OFF_Q, OFF_K, OFF_V = 0, 1280, 3584, 3604, 4372, 5140
    WIN_W = 5908
    MIXL_W = SSD_W + ATT_W
    NMX = MIXL_W // 256
    nc = bass.Bass("TRN2", target_bir_lowering=False)

    def din(name, shape):
        return nc.dram_tensor(name, list(shape), F32, kind="ExternalInput").ap()

    x_in = din("x", [SL, D])
    p_in = din("p", [SL, 256])
    sel_in = din("sel", [128, 2])
    Wg = [din("ffn1_w_gate", [D, DFF]), din("ffn2_w_gate", [D, DFF])]
    Wu = [din("ffn1_w_up", [D, DFF]), din("ffn2_w_up", [D, DFF])]
    Wd = [din("ffn1_w_down", [DFF, D]), din("ffn2_w_down", [DFF, D])]
    W_in = din("w_in", [D, WIN_W])
    W_out = din("w_out", [D, D])
    W_pg = din("w_ple_gate", [D, D])
    W_pp = din("w_ple_proj", [256, D])
    gcols_in = din("gcols", [128, 4 * KC])
    postw_in = din("postw", [4, 128, D])
    convw_in = din("convw", [128, NXC * 4])
    convb_in = din("convb", [128, NXC])
    hv_in = din("headv", [128, 3 * NH])
    dskx_in = din("dskx", [128, SSD_W])
    snw_in = din("snw", [128, SSD_W])
    ident_in = din("ident", [128, 128])
    triu_in = din("triu", [128, 128])
    negm_in = din("negm", [128, 128])
    distm_in = din("distm", [128, 256])
    slp_in = din("slp", [128, AH * 3])
    out = nc.dram_tensor("out", [SL, D], F32, kind="ExternalOutput").ap()
    def dscr(name, shape, dt):
        return nc.dram_tensor(name, list(shape), dt).ap()

    Hloc_t = [nc.dram_tensor("Hloc%d" % k, [128, D], F32) for k in range(NCL)]
    Hg_t = [nc.dram_tensor("Hg%d" % k, [256, D], F32) for k in range(NCL)]
    Hloc = [t.ap() for t in Hloc_t]
    Hg = [t.ap() for t in Hg_t]
    Fd = dscr("Fd", [512, D], F32)
    ZS = dscr("ZS", [S, SSD_W], BF16)
    XS = dscr("XS", [S, SSD_W], BF16)
    BM = dscr("BM", [S, NBC], BF16)
    BT = dscr("BT", [NBC, S], BF16)
    CT = dscr("CT", [NBC, S], BF16)
    QT = dscr("QT", [ATT_W, S], BF16)
    KT = dscr("KT", [ATT_W, S], BF16)
    Vd = dscr("Vd", [S, ATT_W], BF16)
    MIXL_t = [nc.dram_tensor("MIXL%d" % j, [256, S], BF16) for j in range(NMX)]
    MIXG_t = [nc.dram_tensor("MIXG%d" % j, [512, S], BF16) for j in range(NMX)]
    MIXL = [t.ap() for t in MIXL_t]
    MIXG = [t.ap() for t in MIXG_t]

    es = ExitStack()
    with es:
        sch = Sched(nc, es)
        op, dma = sch.op, sch.dma

        def sb(name, shape, dt):
            return Buf(name, es.enter_context(nc.sbuf_tensor("s_" + name, list(shape), dt)))

        def ps(name, shape, dt):
            return Buf(name, es.enter_context(nc.psum_tensor(name, list(shape), dt)))

        bx, bp, bW = Buf("x"), Buf("p"), Buf("W")
        bcst = Buf("cst")
        bH, bHg, bFd, bZS, bXS, bBM, bBT, bCT = Buf("H"), Buf("Hg"), Buf("Fd"), Buf("ZS"), Buf("XS"), Buf("BM"), Buf("BT"), Buf("CT")
        bQT, bKT, bV, bMIXT, bout = Buf("QT"), Buf("KT"), Buf("V"), Buf("MIXT"), Buf("out")
        bMIXG = Buf("MIXG")

        xnT = sb("xnT", [128, KC, 512], BF16)
        big = sb("big", [128, max(FC * 512, 43008)], BF16)
        slabs = [sb("slab%d" % i, [128, KS, 512], BF16) for i in range(2)]
        ra = sb("ra", [128, D], F32)
        xs = sb("xs", [128, D], BF16)
        fst = [sb("fst%d" % i, [128, 4, 512], F32) for i in range(1)]
        sil = [sb("sil%d" % i, [128, 512], BF16) for i in range(4)]
        stg = [sb("stg%d" % i, [128, 512], BF16) for i in range(2)]
        gcols = sb("gcols", [128, 4 * KC], F32)
        convw = sb("convw", [128, NXC * 4], F32)
        convb = sb("convb", [128, NXC], F32)
        halo = sb("halo", [128, NXC, 4], F32)
        headv = sb("headv", [128, 3 * NH], F32)
        ident = sb("ident", [128, 128], BF16)
        identf = sb("identf", [128, 128], F32)
        triu = sb("triu", [128, 128], F32)
        onesf = sb("onesf", [128, 128], F32)
        onesb = sb("onesb", [128, 128], BF16)
        distm = sb("distm", [128, 256], F32)
        dt_sb = sb("dt_sb", [128, NCH, NH], F32)
        ss = sb("ss", [128, 1], F32)
        rstd = sb("rstd", [128, 1], F32)
        cst = sb("cstage", [128, 516], F32)
        acc = sb("cacc", [128, 512], F32)
        sm = sb("small", [128, 8 * NH], F32)
        gss = sb("gss", [128, 8], F32)
        pf = [ps("pf%d" % i, [128, 512], F32) for i in range(6)]
        pb = [ps("pb%d" % i, [128, 1024], BF16) for i in range(2)]
        st = {"pf": 0, "pb": 0, "slab": 0, "fst": 0, "sil": 0, "stg": 0}

        def nxt(kind, lst):
            i = st[kind]
            st[kind] = i + 1
            return lst[i % len(lst)]

        def ld(buf, ap_in, tmp=None):
            dma("sp", buf.t[:], ap_in, bcst, buf)

        ld(gcols, gcols_in[:, :])
        ld(convw, convw_in[:, :])
        ld(convb, convb_in[:, :])
        ld(headv, hv_in[:, :])
        ld(identf, ident_in[:, :])
        ld(triu, triu_in[:, :])
        ld(distm, distm_in[:, :])
        sel = sb("sel", [128, 2], F32)
        ld(sel, sel_in[:, :])
        slp = sb("slp", [128, AH * 3], F32)
        ld(slp, slp_in[:, :])
        op("dve", lambda e: e.tensor_copy(out=ident.t[:], in_=identf.t[:]), r=[identf], w=[ident])
        op("pool", lambda e: e.memset(onesf.t[:], 1.0), w=[onesf])
        op("pool", lambda e: e.memset(onesb.t[:], 1.0), w=[onesb])
        op("pool", lambda e: e.memset(halo.t[:], 0.0), w=[halo])

        def load_slab(parts):
            s = nxt("slab", slabs)
            for (W, k0, nk, c0, ncols, d0) in parts:
                wv = W.rearrange("(kc p) n -> p kc n", p=128)
                dma("pool", s.t[:, 0:nk, d0:d0 + ncols], wv[:, k0:k0 + nk, c0:c0 + ncols], bW, s)
            return s

        def rmsnorm_stats(src):
            n = D
            op("pool", lambda e: e.memset(ss.t[:], 0.0), w=[ss])
            op("act", lambda e: e.activation(out=xs.t[:], in_=src.t[:], func=AF.Square, accum_out=ss.t[:, 0:1]),
               r=[src], w=[xs, ss])
            op("dve", lambda e: e.tensor_scalar(out=rstd.t[:], in0=ss.t[:], scalar1=1.0 / n, scalar2=EPS,
                                               op0=ALU.mult, op1=ALU.add), r=[ss], w=[rstd])
            op("act", lambda e: e.activation(out=rstd.t[:], in_=rstd.t[:], func=AF.Sqrt), r=[rstd], w=[rstd])
            op("dve", lambda e: e.reciprocal(out=rstd.t[:], in_=rstd.t[:]), r=[rstd], w=[rstd])

        def rows_norm_T(getsrc, bsrc, gi):
            for c in range(4):
                dma("sp", ra.t[:], getsrc(c), bsrc, ra)
                rmsnorm_stats(ra)
                op("act", lambda e: e.activation(out=xs.t[:], in_=ra.t[:], func=AF.Copy, scale=rstd.t[:, 0:1]),
                   r=[ra, rstd], w=[xs])
                for kg in range(4):
                    bank = nxt("pb", pb)
                    for kk in range(8):
                        k = kg * 8 + kk
                        op("pe", lambda e: e.transpose(out=bank.t[:, kk * 128:(kk + 1) * 128],
                                                       in_=xs.t[:, k * 128:(k + 1) * 128], identity=ident.t[:]),
                           r=[xs, ident], w=[bank])
                    for kk in range(8):
                        k = kg * 8 + kk
                        g = gcols.t[:, gi * KC + k:gi * KC + k + 1]
                        o = xnT.t[:, k, c * 128:(c + 1) * 128]
                        i_ = bank.t[:, kk * 128:(kk + 1) * 128]
                        if kk % 2:
                            op("act", lambda e: e.activation(out=o, in_=i_, func=AF.Copy, scale=g),
                               r=[bank, gcols], w=[xnT])
                        else:
                            op("dve", lambda e: e.tensor_scalar(out=o, in0=i_, scalar1=g, scalar2=None, op0=ALU.mult),
                               r=[bank, gcols], w=[xnT])

        def lin_fm(W, c0, nchunks, evac):
            m = 0
            while m < nchunks:
                nm = min(4, nchunks - m)
                banks = [nxt("pf", pf) for _ in range(nm)]
                for k0 in range(0, KC, KS):
                    s_ = load_slab([(W, k0, KS, c0 + m * 128, nm * 128, 0)])
                    for mm in range(nm):
                        for kk in range(KS):
                            k = k0 + kk
                            op("pe", lambda e: e.matmul(banks[mm].t[:], lhsT=s_.t[:, kk, mm * 128:(mm + 1) * 128],
                                                        rhs=xnT.t[:, k, :], start=(k == 0), stop=(k == KC - 1)),
                               r=[s_, xnT], w=[banks[mm]], inc=(kk == KS - 1))
                for mm in range(nm):
                    evac(m + mm, banks[mm])
                m += nm

        def lin_tm(W, nk_tot, c0, ncols_tot, lhs, evac, blk=512):
            j = 0
            while j * blk < ncols_tot:
                cc = c0 + j * blk
                ncols = min(blk, ncols_tot - j * blk)
                banks = [nxt("pf", pf) for _ in range(4)]
                for k0 in range(0, nk_tot, KS):
                    nk = min(KS, nk_tot - k0)
                    s = load_slab([(W, k0, nk, cc, ncols, 0)])
                    for tc in range(4):
                        for kk in range(nk):
                            k = k0 + kk
                            op("pe", lambda e: e.matmul(banks[tc].t[:, 0:ncols],
                                                        lhsT=lhs.t[:, k, tc * 128:(tc + 1) * 128],
                                                        rhs=s.t[:, kk, 0:ncols],
                                                        start=(k == 0), stop=(k == nk_tot - 1)),
                               r=[s, lhs], w=[banks[tc]], inc=(kk == nk - 1))
                evac(j, cc - c0, ncols, banks)
                j += 1

        def evac_to_Fd(func=None):
            def ev(j, coff, ncols, banks):
                f = nxt("fst", fst)
                for tc in range(4):
                    if tc % 2:
                        op("act", lambda e: e.activation(out=f.t[:, tc, 0:ncols], in_=banks[tc].t[:, 0:ncols],
                                                         func=AF.Copy), r=[banks[tc]], w=[f])
                    else:
                        op("dve", lambda e: e.tensor_copy(out=f.t[:, tc, 0:ncols], in_=banks[tc].t[:, 0:ncols]),
                           r=[banks[tc]], w=[f])
                dma("sp", Fd.rearrange("(t p) c -> p t c", p=128)[:, :, coff:coff + ncols], f.t[:, :, 0:ncols], f, bFd)
            return ev

        def post_stage(getsrc, bsrc, getdst, bdst, wi, coef):
            wbv = big.t[:, 2048:2048 + 2 * D].bitcast(F32)
            rbv = big.t[:, 2048 + 2 * D:2048 + 4 * D].bitcast(F32)
            dma("sp", wbv, postw_in[wi, :, :], bcst, big)
            for c in range(4):
                dma("sp", ra.t[:], Fd[c * 128:(c + 1) * 128, :], bFd, ra)
                dma("sp", rbv, getsrc(c), bsrc, big)
                rmsnorm_stats(ra)
                op("dve", lambda e: e.scalar_tensor_tensor(out=ra.t[:], in0=ra.t[:], scalar=rstd.t[:, 0:1], in1=wbv,
                                                           op0=ALU.mult, op1=ALU.mult), r=[ra, rstd, big], w=[ra])
                op("dve", lambda e: e.scalar_tensor_tensor(out=rbv, in0=ra.t[:], scalar=float(coef), in1=rbv,
                                                           op0=ALU.mult, op1=ALU.add), r=[ra, big], w=[big])
                dma("sp", getdst(c), rbv, big, bdst)

        def ffn_stage(li, getsrc, bsrc, getdst, bdst, gi, wi):
            rows_norm_T(getsrc, bsrc, gi)
            gT = big
            fc = 0
            while fc < FC:
                nm = min(4, FC - fc)
                sls = []
                for (Wm, is_up) in ((Wg[li], False), (Wu[li], True)):
                    banks = [nxt("pf", pf) for _ in range(nm)]
                    for k0 in range(0, KC, KS):
                        s_ = load_slab([(Wm, k0, KS, fc * 128, nm * 128, 0)])
                        for mm in range(nm):
                            for kk in range(KS):
                                k = k0 + kk
                                op("pe", lambda e: e.matmul(banks[mm].t[:], lhsT=s_.t[:, kk, mm * 128:(mm + 1) * 128],
                                                            rhs=xnT.t[:, k, :], start=(k == 0), stop=(k == KC - 1)),
                                   r=[s_, xnT], w=[banks[mm]], inc=(kk == KS - 1))
                    for mm in range(nm):
                        if not is_up:
                            sl = nxt("sil", sil)
                            sls.append(sl)
                            op("act", lambda e: e.activation(out=sl.t[:], in_=banks[mm].t[:], func=AF.Silu), r=[banks[mm]], w=[sl])
                        else:
                            sl = sls[mm]
                            op("dve", lambda e: e.tensor_tensor(out=gT.t[:, (fc + mm) * 512:(fc + mm + 1) * 512], in0=sl.t[:],
                                                                in1=banks[mm].t[:], op=ALU.mult), r=[sl, banks[mm]], w=[gT])
                fc += nm
            gT3 = gT.t[:, 0:FC * 512].rearrange("p (k n) -> p k n", n=512)
            lin_tm_view(Wd[li], FC, 0, D, gT, gT3, evac_to_Fd())
            post_stage(getsrc, bsrc, getdst, bdst, wi, 0.5)

        def lin_tm_view(W, nk_tot, c0, ncols_tot, track, view, evac, blk=512):
            j = 0
            while j * blk < ncols_tot:
                cc = c0 + j * blk
                ncols = min(blk, ncols_tot - j * blk)
                banks = [nxt("pf", pf) for _ in range(4)]
                for k0 in range(0, nk_tot, KS):
                    nk = min(KS, nk_tot - k0)
                    s = load_slab([(W, k0, nk, cc, ncols, 0)])
                    for tc in range(4):
                        for kk in range(nk):
                            k = k0 + kk
                            op("pe", lambda e: e.matmul(banks[tc].t[:, 0:ncols],
                                                        lhsT=view[:, k, tc * 128:(tc + 1) * 128],
                                                        rhs=s.t[:, kk, 0:ncols],
                                                        start=(k == 0), stop=(k == nk_tot - 1)),
                               r=[s, track], w=[banks[tc]], inc=(kk == nk - 1))
                evac(j, cc - c0, ncols, banks)
                j += 1

        def inproj_stage(ti):
            r0 = ti * 512
            rows_norm_T(lambda c: Hg[(ti * 4 + c) % NCL][((ti * 4 + c) // NCL) * 128:((ti * 4 + c) // NCL + 1) * 128, :], bHg, 1)

            def ev_z(j, coff, ncols, banks):
                for tc in range(4):
                    sg = nxt("stg", stg)
                    op("act", lambda e: e.activation(out=sg.t[:, 0:ncols], in_=banks[tc].t[:, 0:ncols], func=AF.Silu),
                       r=[banks[tc]], w=[sg])
                    dma("sp", ZS[r0 + tc * 128:r0 + (tc + 1) * 128, coff:coff + ncols], sg.t[:, 0:ncols], sg, bZS)
            lin_tm(W_in, KC, OFF_Z, SSD_W, xnT, ev_z)

            def ev_v(j, coff, ncols, banks):
                for tc in range(4):
                    sg = nxt("stg", stg)
                    op("dve", lambda e: e.tensor_copy(out=sg.t[:, 0:ncols], in_=banks[tc].t[:, 0:ncols]),
                       r=[banks[tc]], w=[sg])
                    dma("sp", Vd[r0 + tc * 128:r0 + (tc + 1) * 128, coff:coff + ncols], sg.t[:, 0:ncols], sg, bV)
            lin_tm(W_in, KC, OFF_V, ATT_W, xnT, ev_v)

            def ev_dt(j, coff, ncols, banks):
                for tc in range(4):
                    ch = ti * 4 + tc
                    o = dt_sb.t[:, ch, :]
                    op("dve", lambda e: e.tensor_tensor(out=o, in0=banks[tc].t[:, 0:NH], in1=headv.t[:, 0:NH], op=ALU.add),
                       r=[banks[tc], headv], w=[dt_sb])
                    op("act", lambda e: e.activation(out=o, in_=o, func=AF.Exp), r=[dt_sb], w=[dt_sb])
                    op("act", lambda e: e.activation(out=o, in_=o, func=AF.Ln, bias=1.0, scale=1.0), r=[dt_sb], w=[dt_sb])
            lin_tm(W_in, KC, OFF_DT, NH, xnT, ev_dt)

            def ev_xbc(m, bank):
                op("dve", lambda e: e.tensor_copy(out=cst.t[:, 0:3], in_=halo.t[:, m, 0:3]), r=[halo], w=[cst])
                op("act", lambda e: e.activation(out=cst.t[:, 3:515], in_=bank.t[:], func=AF.Copy), r=[bank], w=[cst])
                op("dve", lambda e: e.tensor_scalar(out=acc.t[:], in0=cst.t[:, 0:512], scalar1=convw.t[:, m * 4:m * 4 + 1],
                                                   scalar2=convb.t[:, m:m + 1], op0=ALU.mult, op1=ALU.add),
                   r=[cst, convw, convb], w=[acc])
                for kq in range(1, 4):
                    op("dve", lambda e: e.scalar_tensor_tensor(out=acc.t[:], in0=cst.t[:, kq:kq + 512],
                                                               scalar=convw.t[:, m * 4 + kq:m * 4 + kq + 1], in1=acc.t[:],
                                                               op0=ALU.mult, op1=ALU.add), r=[cst, convw, acc], w=[acc])
                op("dve", lambda e: e.tensor_copy(out=halo.t[:, m, 0:3], in_=cst.t[:, 512:515]), r=[cst], w=[halo])
                sg = nxt("stg", stg)
                op("act", lambda e: e.activation(out=sg.t[:], in_=acc.t[:], func=AF.Silu), r=[acc], w=[sg])
                if m >= MXC + NG:
                    dma("sp", CT[(m - MXC - NG) * 128:(m - MXC - NG + 1) * 128, r0:r0 + 512], sg.t[:], sg, bCT)
                    return
                if m >= MXC:
                    dma("sp", BT[(m - MXC) * 128:(m - MXC + 1) * 128, r0:r0 + 512], sg.t[:], sg, bBT)
                bank2 = nxt("pb", pb)
                for tc in range(4):
                    op("pe", lambda e: e.transpose(out=bank2.t[:, tc * 128:(tc + 1) * 128],
                                                   in_=sg.t[:, tc * 128:(tc + 1) * 128], identity=ident.t[:]),
                       r=[sg, ident], w=[bank2])
                sg2 = nxt("stg", stg)
                op("dve", lambda e: e.tensor_copy(out=sg2.t[:], in_=bank2.t[:, 0:512]), r=[bank2], w=[sg2])
                if m >= MXC:
                    dst, bd, cc = BM, bBM, (m - MXC) * 128
                else:
                    dst, bd, cc = XS, bXS, m * 128
                dma("sp", dst.rearrange("(t p) c -> p t c", p=128)[:, ti * 4:(ti + 1) * 4, cc:cc + 128],
                    sg2.t[:].rearrange("p (t c) -> p t c", c=128), sg2, bd)
            lin_fm(W_in, OFF_X, NXC, ev_xbc)

            def mk_ev(dst, bd):
                def ev(m, bank):
                    sg = nxt("stg", stg)
                    if m % 2:
                        op("act", lambda e: e.activation(out=sg.t[:], in_=bank.t[:], func=AF.Copy), r=[bank], w=[sg])
                    else:
                        op("dve", lambda e: e.tensor_copy(out=sg.t[:], in_=bank.t[:]), r=[bank], w=[sg])
                    dma("sp", dst[m * 128:(m + 1) * 128, r0:r0 + 512], sg.t[:], sg, bd)
                return ev
            lin_fm(W_in, OFF_Q, AH, mk_ev(QT, bQT))
            lin_fm(W_in, OFF_K, AH, mk_ev(KT, bKT))

        def fork(names):
            subs = {}
            for n in names:
                bb = Buf(n)
                bb.writes = dict(big.writes)
                bb.reads = dict(big.reads)
                subs[n] = bb
            return subs

        def join(subs):
            m = {}
            for bb in [big] + list(subs.values()):
                for ev in list(bb.writes.values()) + list(bb.reads.values()):
                    if ev.key not in m or m[ev.key].order < ev.order:
                        m[ev.key] = ev
            big.writes = m
            big.reads = {}

        def attention():
            bg = big.t
            QTh = bg[:, 0:S]
            KTh = bg[:, S:2 * S]
            Vhs = [bg[:, 2 * S:3 * S], bg[:, 8 * S:9 * S]]
            attT = bg[:, 3 * S:4 * S]
            numT = bg[:, 4 * S:6 * S].bitcast(F32)
            denT = bg[:, 6 * S:8 * S].bitcast(F32)
            sb_ = fork(["a_qk", "a_v0", "a_v1", "a_num", "a_att"])
            bqk, bnum, batt = sb_["a_qk"], sb_["a_num"], sb_["a_att"]
            bvs = [sb_["a_v0"], sb_["a_v1"]]
            vi = 0
            for h in range(AH):
                dma("sp", QTh, QT[h * 128:(h + 1) * 128, :], bQT, bqk)
                dma("sp", KTh, KT[h * 128:(h + 1) * 128, :], bKT, bqk)
                for bi, d in enumerate((1, 4, 16)):
                    nb = S // (128 * d)
                    cp = slp.t[:, h * 3 + bi:h * 3 + bi + 1]
                    Vh, bv = Vhs[vi % 2], bvs[vi % 2]
                    vi += 1
                    with nc.allow_non_contiguous_dma(reason="dilated V gather"):
                        for r in range(d):
                            dma("sp", Vh.rearrange("p (r b f) -> p r b f", r=d, b=nb)[:, r, :, :],
                                Vd.rearrange("(b i r) f -> i r b f", i=128, r=d)[:, r, :, h * 128:(h + 1) * 128], bV, bv)
                    V4 = Vh.rearrange("p (r b f) -> p r b f", r=d, b=nb)
                    for r in range(d):
                        for b in range(nb):
                            def tok(bb):
                                s0 = bb * 128 * d + r
                                return slice(s0, s0 + 127 * d + 1, d)
                            qs = tok(b)
                            sbank = nxt("pf", pf)
                            if b > 0:
                                op("pe", lambda e: e.matmul(sbank.t[:, 0:128], lhsT=KTh[:, tok(b - 1)], rhs=QTh[:, qs],
                                                            start=True, stop=True), r=[bqk], w=[sbank])
                            op("pe", lambda e: e.matmul(sbank.t[:, 128:256], lhsT=KTh[:, qs], rhs=QTh[:, qs],
                                                        start=True, stop=True), r=[bqk], w=[sbank])
                            lo = 0 if b > 0 else 128
                            sl = nxt("sil", sil)
                            slv = sl.t[:, :].bitcast(F32)
                            op("dve", lambda e: e.scalar_tensor_tensor(out=slv[:, lo:256], in0=distm.t[:, lo:256],
                                                                       scalar=cp, in1=sbank.t[:, lo:256],
                                                                       op0=ALU.mult, op1=ALU.add),
                               r=[distm, sbank, slp], w=[sl])
                            pT = nxt("stg", stg)
                            op("act", lambda e: e.activation(out=pT.t[:, lo:256], in_=slv[:, lo:256], func=AF.Exp,
                                                             scale=float(QS)), r=[sl], w=[pT])
                            nbank = nxt("pf", pf)
                            if b > 0:
                                op("pe", lambda e: e.matmul(nbank.t[:, 0:128], lhsT=V4[:, r, b - 1, :], rhs=pT.t[:, 0:128],
                                                            start=True, stop=False), r=[bv, pT], w=[nbank])
                            op("pe", lambda e: e.matmul(nbank.t[:, 0:128], lhsT=V4[:, r, b, :], rhs=pT.t[:, 128:256],
                                                        start=(b == 0), stop=True), r=[bv, pT], w=[nbank])
                            if b > 0:
                                op("pe", lambda e: e.matmul(nbank.t[:, 128:256], lhsT=onesb.t[:], rhs=pT.t[:, 0:128],
                                                            start=True, stop=False), r=[onesb, pT], w=[nbank])
                            op("pe", lambda e: e.matmul(nbank.t[:, 128:256], lhsT=onesb.t[:], rhs=pT.t[:, 128:256],
                                                        start=(b == 0), stop=True), r=[onesb, pT], w=[nbank])
                            if bi == 0:
                                op("dve", lambda e: e.tensor_copy(out=numT[:, qs], in_=nbank.t[:, 0:128]), r=[nbank], w=[bnum])
                                op("act", lambda e: e.activation(out=denT[:, qs], in_=nbank.t[:, 128:256], func=AF.Copy),
                                   r=[nbank], w=[bnum])
                            else:
                                op("dve", lambda e: e.tensor_tensor(out=numT[:, qs], in0=numT[:, qs], in1=nbank.t[:, 0:128],
                                                                    op=ALU.add), r=[nbank, bnum], w=[bnum])
                                op("dve", lambda e: e.tensor_tensor(out=denT[:, qs], in0=denT[:, qs], in1=nbank.t[:, 128:256],
                                                                    op=ALU.add), r=[nbank, bnum], w=[bnum])
                op("dve", lambda e: e.reciprocal(out=denT, in_=denT), r=[bnum], w=[bnum])
                op("dve", lambda e: e.tensor_tensor(out=attT, in0=numT, in1=denT, op=ALU.mult), r=[bnum], w=[batt])
                dma("sp", MIXL[MXC // 2 + h // 2][(h % 2) * 128:(h % 2 + 1) * 128, :], attT, batt, bMIXT)
            join(sb_)

        def ssd():
            bg = big.t
            o = 0

            def carve(n, dt=BF16):
                nonlocal o
                ne = n if dt == BF16 else 2 * n
                v = bg[:, o:o + ne]
                o += ne
                return v if dt == BF16 else v.bitcast(F32)
            xs_c = carve(SSD_W)
            zs_c = carve(SSD_W)
            bm_c = carve(NBC)
            bt_c = carve(NBC)
            ct_c = carve(NBC)
            xdt_b = carve(SSD_W)
            xdtd_b = carve(SSD_W)
            Hbf = carve(SSD_W)
            Hst = carve(SSD_W, F32)
            y_f = carve(SSD_W, F32)
            t_f = carve(SSD_W, F32)
            negm = carve(128, F32)
            snw = carve(SSD_W, F32)
            dskx = carve(SSD_W, F32)
            NR = 2
            Rjs = [carve(128, F32) for _ in range(NR)]
            LTs = [carve(128, F32) for _ in range(NR)]
            MTs = [carve(128) for _ in range(NR)]
            names = ["s_xs", "s_zs", "s_bm", "s_bt", "s_ct", "s_xdtb", "s_cst"]
            names += ["s_xdtd%d" % g for g in range(NG)] + ["s_H%d" % g for g in range(NG)] + ["s_Hbf%d" % g for g in range(NG)]
            names += ["s_y%d" % g for g in range(NG)] + ["s_t%d" % g for g in range(NG)]
            names += ["s_R%d" % i for i in range(NR)] + ["s_L%d" % i for i in range(NR)] + ["s_M%d" % i for i in range(NR)]
            sb_ = fork(names)
            b_xs, b_zs, b_bm, b_bt, b_ct, b_xdtb, b_cst = [sb_[n] for n in names[:7]]
            b_xdtd = [sb_["s_xdtd%d" % g] for g in range(NG)]
            b_H = [sb_["s_H%d" % g] for g in range(NG)]
            b_Hbf = [sb_["s_Hbf%d" % g] for g in range(NG)]
            b_y = [sb_["s_y%d" % g] for g in range(NG)]
            b_t = [sb_["s_t%d" % g] for g in range(NG)]
            b_R = [sb_["s_R%d" % i] for i in range(NR)]
            b_L = [sb_["s_L%d" % i] for i in range(NR)]
            b_M = [sb_["s_M%d" % i] for i in range(NR)]
            dma("sp", negm, negm_in[:, :], bcst, b_cst)
            dma("sp", snw, snw_in[:, :], bcst, b_cst)
            dma("sp", dskx, dskx_in[:, :], bcst, b_cst)
            smt = sm.t
            dtA = smt[:, 0:NH]
            acs = smt[:, NH:2 * NH]
            nacs = smt[:, 2 * NH:3 * NH]
            ea = smt[:, 3 * NH:4 * NH]
            cd = smt[:, 4 * NH:5 * NH]
            dsc = smt[:, 5 * NH:6 * NH]
            a_b = smt[:, 6 * NH:7 * NH]
            dtd = smt[:, 7 * NH:8 * NH]
            op("act", lambda e: e.activation(out=a_b, in_=headv.t[:, NH:2 * NH], func=AF.Exp), r=[headv], w=[sm])
            op("dve", lambda e: e.tensor_scalar(out=a_b, in0=a_b, scalar1=-1.0, scalar2=None, op0=ALU.mult), r=[sm], w=[sm])
            for g in range(NG):
                gs = slice(g * 320, (g + 1) * 320)
                op("pool", lambda e: e.memset(Hst[:, gs], 0.0), w=[b_H[g]])
                op("pool", lambda e: e.memset(Hbf[:, gs], 0.0), w=[b_Hbf[g]])
            hc = 0
            for c in range(NCH):
                rs = slice(c * 128, (c + 1) * 128)
                dma("sp", xs_c, XS[rs, :], bXS, b_xs)
                dma("sp", zs_c, ZS[rs, :], bZS, b_zs)
                dma("sp", bm_c, BM[rs, :], bBM, b_bm)
                dma("sp", bt_c.rearrange("p (g s) -> p g s", g=NG), BT.rearrange("(g n) s -> n g s", n=128)[:, :, rs], bBT, b_bt)
                dma("sp", ct_c.rearrange("p (g s) -> p g s", g=NG), CT.rearrange("(g n) s -> n g s", n=128)[:, :, rs], bCT, b_ct)
                dtc = dt_sb.t[:, c, :]
                op("dve", lambda e: e.tensor_tensor(out=dtA, in0=dtc, in1=a_b, op=ALU.mult), r=[dt_sb, sm], w=[sm])
                pa = pf[4]
                op("pe", lambda e: e.matmul(pa.t[:, 0:NH], lhsT=triu.t[:], rhs=dtA, start=True, stop=True), r=[triu, sm], w=[pa])
                op("pe", lambda e: e.matmul(pa.t[:, 64:64 + NH], lhsT=onesf.t[:], rhs=dtA, start=True, stop=True),
                   r=[onesf, sm], w=[pa])
                op("dve", lambda e: e.tensor_copy(out=acs, in_=pa.t[:, 0:NH]), r=[pa], w=[sm])
                op("dve", lambda e: e.tensor_scalar(out=nacs, in0=pa.t[:, 0:NH], scalar1=-1.0, scalar2=None, op0=ALU.mult),
                   r=[pa], w=[sm])
                op("act", lambda e: e.activation(out=ea, in_=pa.t[:, 0:NH], func=AF.Exp), r=[pa], w=[sm])
                op("act", lambda e: e.activation(out=cd, in_=pa.t[:, 64:64 + NH], func=AF.Exp), r=[pa], w=[sm])
                op("dve", lambda e: e.tensor_tensor(out=dsc, in0=pa.t[:, 64:64 + NH], in1=acs, op=ALU.subtract), r=[pa, sm], w=[sm])
                op("act", lambda e: e.activation(out=dsc, in_=dsc, func=AF.Exp), r=[sm], w=[sm])
                op("dve", lambda e: e.tensor_tensor(out=dtd, in0=dsc, in1=dtc, op=ALU.mult), r=[sm, dt_sb], w=[sm])
                for j in range(NH):
                    hs = slice(j * 64, (j + 1) * 64)
                    g = j // 5
                    op("dve", lambda e: e.tensor_scalar(out=xdt_b[:, hs], in0=xs_c[:, hs], scalar1=dtc[:, j:j + 1], scalar2=None,
                                                       op0=ALU.mult), r=[b_xs, dt_sb], w=[b_xdtb])
                    op("dve", lambda e: e.tensor_scalar(out=xdtd_b[:, hs], in0=xs_c[:, hs], scalar1=dtd[:, j:j + 1], scalar2=None,
                                                       op0=ALU.mult), r=[b_xs, sm], w=[b_xdtd[g]])
                for g in range(NG):
                    gs = slice(g * 320, (g + 1) * 320)
                    n_s = slice(g * 128, (g + 1) * 128)
                    pyo = pf[0]
                    op("pe", lambda e: e.matmul(pyo.t[:, 0:320], lhsT=ct_c[:, n_s], rhs=Hbf[:, gs], start=True, stop=True),
                       r=[b_ct, b_Hbf[g]], w=[pyo])
                    pst = pf[1]
                    op("pe", lambda e: e.matmul(pst.t[:, 0:320], lhsT=bm_c[:, n_s], rhs=xdtd_b[:, gs], start=True, stop=True),
                       r=[b_bm, b_xdtd[g]], w=[pst])
                    pcb = pf[2]
                    op("pe", lambda e: e.matmul(pcb.t[:, 0:128], lhsT=bt_c[:, n_s], rhs=ct_c[:, n_s], start=True, stop=True),
                       r=[b_bt, b_ct], w=[pcb])
                    pyd = pf[3]
                    for jj in range(5):
                        j = g * 5 + jj
                        hs = slice(j * 64, (j + 1) * 64)
                        ri = hc % NR
                        hc += 1
                        Rj, LT, MT = Rjs[ri], LTs[ri], MTs[ri]
                        op("dve", lambda e: e.tensor_scalar(out=Rj, in0=triu.t[:], scalar1=dtA[:, j:j + 1], scalar2=None,
                                                           op0=ALU.mult), r=[triu, sm], w=[b_R[ri]])
                        pbc = pf[4 + (j % 2)]
                        op("pe", lambda e: e.matmul(pbc.t[:, 0:128], lhsT=onesf.t[:], rhs=Rj, start=True, stop=False),
                           r=[onesf, b_R[ri]], w=[pbc])
                        op("pe", lambda e: e.matmul(pbc.t[:, 0:128], lhsT=identf.t[:], rhs=negm, start=False, stop=True),
                           r=[identf, b_cst], w=[pbc])
                        op("act", lambda e: e.activation(out=LT, in_=pbc.t[:, 0:128], func=AF.Exp, bias=nacs[:, j:j + 1], scale=1.0),
                           r=[pbc, sm], w=[b_L[ri]])
                        op("dve", lambda e: e.tensor_tensor(out=MT, in0=LT, in1=pcb.t[:, 0:128], op=ALU.mult), r=[b_L[ri], pcb], w=[b_M[ri]])
                        op("pe", lambda e: e.matmul(pyd.t[:, jj * 64:(jj + 1) * 64], lhsT=MT, rhs=xdt_b[:, hs], start=True, stop=True),
                           r=[b_M[ri], b_xdtb], w=[pyd])
                    op("pool", lambda e: e.tensor_tensor(out=t_f[:, gs], in0=xs_c[:, gs], in1=dskx[:, gs], op=ALU.mult),
                       r=[b_xs, b_cst], w=[b_t[g]])
                    op("dve", lambda e: e.tensor_tensor(out=y_f[:, gs], in0=t_f[:, gs], in1=pyd.t[:, 0:320], op=ALU.add),
                       r=[b_t[g], pyd], w=[b_y[g]])
                    for jj in range(5):
                        j = g * 5 + jj
                        hs = slice(j * 64, (j + 1) * 64)
                        op("dve", lambda e: e.scalar_tensor_tensor(out=y_f[:, hs], in0=pyo.t[:, jj * 64:(jj + 1) * 64],
                                                                   scalar=ea[:, j:j + 1], in1=y_f[:, hs],
                                                                   op0=ALU.mult, op1=ALU.add), r=[pyo, sm, b_y[g]], w=[b_y[g]])
                        op("dve", lambda e: e.scalar_tensor_tensor(out=Hst[:, hs], in0=Hst[:, hs], scalar=cd[:, j:j + 1],
                                                                   in1=pst.t[:, jj * 64:(jj + 1) * 64],
                                                                   op0=ALU.mult, op1=ALU.add), r=[pst, sm, b_H[g]], w=[b_H[g]])
                    op("act", lambda e: e.activation(out=Hbf[:, gs], in_=Hst[:, gs], func=AF.Copy), r=[b_H[g]], w=[b_Hbf[g]])
                    op("dve", lambda e: e.tensor_tensor(out=y_f[:, gs], in0=y_f[:, gs], in1=zs_c[:, gs], op=ALU.mult),
                       r=[b_y[g], b_zs], w=[b_y[g]])
                    op("pool", lambda e: e.memset(gss.t[:, g:g + 1], 0.0), w=[gss])
                    op("act", lambda e: e.activation(out=t_f[:, gs], in_=y_f[:, gs], func=AF.Square, accum_out=gss.t[:, g:g + 1]),
                       r=[b_y[g]], w=[b_t[g], gss])
                    op("dve", lambda e: e.tensor_scalar(out=gss.t[:, g:g + 1], in0=gss.t[:, g:g + 1], scalar1=1.0 / 320, scalar2=EPS,
                                                       op0=ALU.mult, op1=ALU.add), r=[gss], w=[gss])
                    op("act", lambda e: e.activation(out=gss.t[:, g:g + 1], in_=gss.t[:, g:g + 1], func=AF.Sqrt), r=[gss], w=[gss])
                    op("dve", lambda e: e.reciprocal(out=gss.t[:, g:g + 1], in_=gss.t[:, g:g + 1]), r=[gss], w=[gss])
                    op("dve", lambda e: e.scalar_tensor_tensor(out=xdtd_b[:, gs], in0=y_f[:, gs], scalar=gss.t[:, g:g + 1],
                                                               in1=snw[:, gs], op0=ALU.mult, op1=ALU.mult),
                       r=[b_y[g], gss, b_cst], w=[b_xdtd[g]])
                for q4 in range(MXC // 2):
                    bank = nxt("pb", pb)
                    for kk in range(2):
                        k = q4 * 2 + kk
                        gk = (k * 128) // 320
                        gk2 = (k * 128 + 127) // 320
                        op("pe", lambda e: e.transpose(out=bank.t[:, kk * 128:(kk + 1) * 128], in_=xdtd_b[:, k * 128:(k + 1) * 128],
                                                       identity=ident.t[:]), r=[b_xdtd[gk], b_xdtd[gk2], ident], w=[bank])
                    sg = nxt("stg", stg)
                    op("act", lambda e: e.activation(out=sg.t[:, 0:256], in_=bank.t[:, 0:256], func=AF.Copy), r=[bank], w=[sg])
                    dma("sp", MIXL[q4].rearrange("(k p) s -> p k s", p=128)[:, :, rs],
                        sg.t[:, 0:256].rearrange("p (k s) -> p k s", s=128), sg, bMIXT)
            join(sb_)

        def tail_stage(ti):
            r0 = ti * 512
            hl = lambda c: Hloc[ti * 4 + c][:, :]
            tmpT = big.t[:, 0:KC * 512].rearrange("p (k n) -> p k n", n=512)
            for j in range(NMX):
                mv = MIXG[j].rearrange("(q p) s -> p q s", p=128)
                dma("sp", xnT.t[:, j * 4:(j + 1) * 4, :], mv[:, :, r0:r0 + 512], bMIXG, xnT)
                dma("sp", tmpT[:, j * 4:(j + 1) * 4, :], mv[:, :, SL + r0:SL + r0 + 512], bMIXG, big)
            op("dve", lambda e: e.tensor_scalar(out=xnT.t[:], in0=xnT.t[:], scalar1=sel.t[:, 0:1], scalar2=None, op0=ALU.mult),
               r=[xnT, sel], w=[xnT])
            op("dve", lambda e: e.scalar_tensor_tensor(out=xnT.t[:], in0=tmpT, scalar=sel.t[:, 1:2], in1=xnT.t[:],
                                                       op0=ALU.mult, op1=ALU.add), r=[big, sel, xnT], w=[xnT])
            lin_tm(W_out, KC, 0, D, xnT, evac_to_Fd())
            post_stage(hl, bH, hl, bH, 1, 1.0)
            ffn_stage(1, hl, bH, hl, bH, 2, 2)
            rows_norm_T(hl, bH, 3)
            pT = big.t[:, 0:1024].rearrange("p (k n) -> p k n", n=512)
            for c in range(4):
                dma("sp", ra.t[:, 0:256], p_in[r0 + c * 128:r0 + (c + 1) * 128, :], bp, ra)
                op("dve", lambda e: e.tensor_copy(out=xs.t[:, 0:256], in_=ra.t[:, 0:256]), r=[ra], w=[xs])
                bank = nxt("pb", pb)
                for kk in range(2):
                    op("pe", lambda e: e.transpose(out=bank.t[:, kk * 128:(kk + 1) * 128], in_=xs.t[:, kk * 128:(kk + 1) * 128],
                                                   identity=ident.t[:]), r=[xs, ident], w=[bank])
                for kk in range(2):
                    op("dve", lambda e: e.tensor_copy(out=pT[:, kk, c * 128:(c + 1) * 128], in_=bank.t[:, kk * 128:(kk + 1) * 128]),
                       r=[bank], w=[big])
            ppst = ra.t[:, 0:2048].rearrange("p (t c) -> p t c", c=512)

            for j in range(D // 512):
                cc = j * 512
                banks = [nxt("pf", pf) for _ in range(4)]
                s_ = load_slab([(W_pp, 0, 2, cc, 512, 0)])
                for tc in range(4):
                    for k in range(2):
                        op("pe", lambda e: e.matmul(banks[tc].t[:, 0:512], lhsT=pT[:, k, tc * 128:(tc + 1) * 128],
                                                    rhs=s_.t[:, k, 0:512], start=(k == 0), stop=(k == 1)),
                           r=[s_, big], w=[banks[tc]])
                for tc in range(4):
                    op("act", lambda e: e.activation(out=ppst[:, tc, :], in_=banks[tc].t[:, 0:512], func=AF.Copy),
                       r=[banks[tc]], w=[ra])
                banks = [nxt("pf", pf) for _ in range(4)]
                for k0 in range(0, KC, KS):
                    s_ = load_slab([(W_pg, k0, KS, cc, 512, 0)])
                    for tc in range(4):
                        for kk in range(KS):
                            k = k0 + kk
                            op("pe", lambda e: e.matmul(banks[tc].t[:, 0:512], lhsT=xnT.t[:, k, tc * 128:(tc + 1) * 128],
                                                        rhs=s_.t[:, kk, 0:512], start=(k == 0), stop=(k == KC - 1)),
                               r=[s_, xnT], w=[banks[tc]])
                f = nxt("fst", fst)
                for tc in range(4):
                    op("act", lambda e: e.activation(out=f.t[:, tc, :], in_=banks[tc].t[:, 0:512], func=AF.Sigmoid),
                       r=[banks[tc]], w=[f])
                    op("dve", lambda e: e.tensor_tensor(out=f.t[:, tc, :], in0=f.t[:, tc, :], in1=ppst[:, tc, :], op=ALU.mult),
                       r=[f, ra], w=[f])
                dma("sp", Fd.rearrange("(t p) c -> p t c", p=128)[:, :, cc:cc + 512], f.t[:], f, bFd)
            post_stage(hl, bH, lambda c: out[r0 + c * 128:r0 + (c + 1) * 128, :], bout, 3, 1.0)

        groups = [[0, 1], [2, 3], [4, 5], [6, 7]]
        with nc.named_scope("A_ffn1"):
            for ti in range(NTL):
                ffn_stage(0, lambda c: x_in[ti * 512 + c * 128:ti * 512 + (c + 1) * 128, :], bx,
                          lambda c: Hloc[ti * 4 + c][:, :], bH, 0, 0)
        with nc.named_scope("X_cc"):
            for k in range(NCL):
                sch.cc(Hloc_t[k].ap().opt(), Hg_t[k].ap().opt(), bH, bHg, groups)
        with nc.named_scope("A_inproj"):
            for ti in range(NT):
                inproj_stage(ti)
        with nc.named_scope("B_att"):
            attention()
        with nc.named_scope("B_ssd"):
            ssd()
        with nc.named_scope("X_cc2"):
            for j in range(NMX):
                sch.cc(MIXL_t[j].ap().opt(), MIXG_t[j].ap().opt(), bMIXT, bMIXG, groups)
        with nc.named_scope("C_tail"):
            for ti in range(NTL):
                tail_stage(ti)
        sch.finish("sp", [bout])
        sch.finish("act", [bout])
    return nc


def prep_inputs(inp, b, par):
    f = np.float32

    def col(v):
        v = np.asarray(v, f).reshape(-1)
        return np.ascontiguousarray(v.reshape(-1, 128).T)

    def bc(v):
        v = np.asarray(v, f).reshape(1, -1)
        return np.ascontiguousarray(np.broadcast_to(v, (128, v.shape[1])))
    m = {}
    SL = inp["x"].shape[1] // 2
    m["x"] = np.ascontiguousarray(inp["x"][b, par * SL:(par + 1) * SL])
    m["p"] = np.ascontiguousarray(inp["p"][0, b, par * SL:(par + 1) * SL])
    m["sel"] = np.ascontiguousarray(np.broadcast_to(np.eye(2, dtype=f)[par][None, :], (128, 2)))
    for k in ("ffn1_w_gate", "ffn1_w_up", "ffn1_w_down", "ffn2_w_gate", "ffn2_w_up", "ffn2_w_down",
              "w_ple_gate", "w_ple_proj"):
        m[k] = np.asarray(inp[k][0], f)
    FO_Z, FO_X, FO_DT, FO_Q, FO_K, FO_V = 0, 2560, 7168, 7208, 8744, 10280
    cols = np.concatenate([
        np.arange(FO_Z + par * 1280, FO_Z + (par + 1) * 1280),
        np.arange(FO_X + par * 1280, FO_X + (par + 1) * 1280),
        np.arange(FO_X + 2560 + par * 512, FO_X + 2560 + (par + 1) * 512),
        np.arange(FO_X + 3584 + par * 512, FO_X + 3584 + (par + 1) * 512),
        np.arange(FO_DT + par * 20, FO_DT + (par + 1) * 20),
        np.arange(FO_Q + par * 768, FO_Q + (par + 1) * 768),
        np.arange(FO_K + par * 768, FO_K + (par + 1) * 768),
        np.arange(FO_V + par * 768, FO_V + (par + 1) * 768)])
    m["w_in"] = np.ascontiguousarray(np.asarray(inp["w_in"][0], f)[:, cols])
    cch = np.concatenate([np.arange(par * 1280, (par + 1) * 1280),
                          np.arange(2560 + par * 512, 2560 + (par + 1) * 512),
                          np.arange(3584 + par * 512, 3584 + (par + 1) * 512)])
    rows = []
    for j in range(8):
        for r in range(2):
            for hh in range(2):
                lf = j * 256 + hh * 128 + np.arange(128)
                rows.append(np.where(lf < 1280, r * 1280 + lf, 2560 + r * 768 + (lf - 1280)))
    rows = np.concatenate(rows)
    m["w_out"] = np.ascontiguousarray(np.asarray(inp["w_out"][0], f)[rows, :])
    m["gcols"] = np.ascontiguousarray(np.concatenate(
        [col(inp[k][0]) for k in ("ffn1_pre_w", "mix_pre_w", "ffn2_pre_w", "ple_pre_w")], axis=1))
    m["postw"] = np.ascontiguousarray(np.stack(
        [bc(inp[k][0]) for k in ("ffn1_post_w", "mix_post_w", "ffn2_post_w", "ple_post_w")], axis=0))
    cw = np.asarray(inp["conv_w"][0], f)[:, cch]
    m["convw"] = np.ascontiguousarray(cw.T.reshape(18, 128, 4).transpose(1, 0, 2).reshape(128, 72))
    m["convb"] = col(np.asarray(inp["conv_b"][0], f)[cch])
    hsl = slice(par * 20, (par + 1) * 20)
    m["headv"] = np.ascontiguousarray(np.concatenate(
        [bc(inp["dt_bias"][0][hsl]), bc(inp["a_log"][0][hsl]), bc(inp["d_skip"][0][hsl])], axis=1))
    m["dskx"] = bc(np.repeat(np.asarray(inp["d_skip"][0], f)[hsl], 64))
    m["snw"] = bc(np.asarray(inp["ssd_norm_w"][0], f)[par * 1280:(par + 1) * 1280])
    sl_all = alibi_slopes(12)
    m["slp"] = bc(np.asarray([-sl_all[par * 6 + a] * d / QS for a in range(6) for d in (1, 4, 16)], f))
    m.update(host_consts())
    return m


_NC_CACHE = {}


def kernel(**inputs):
    B, S, _ = inputs["x"].shape
    DFF = inputs["ffn1_w_gate"].shape[-1]
    key = (S, DFF)
    if key not in _NC_CACHE:
        _NC_CACHE[key] = build(S, DFF)
    nc = _NC_CACHE[key]
    n = 8
    in_maps = [prep_inputs(inputs, (c // 2) % B, c % 2) for c in range(n)]
    res = run_bass_kernel_spmd(nc, in_maps, core_ids=list(range(n)))
    outs = [np.concatenate([res.results[2 * b]["out"], res.results[2 * b + 1]["out"]], axis=0) for b in range(B)]
    return np.stack(outs, axis=0).astype(np.float32)
```
